# Optimizing a Trainium2 kernel written in Bass

```python
import jax, jax.numpy as jnp
from jax import lax
import numpy as np

D_MODEL = 1024
BATCH = 2
SEQ = 8192
DEPTH = 1

EPS = 1e-6
D_FF = 4 * D_MODEL
M_HEADS = 4
M_DV = D_MODEL // 8
M_DQK = M_DV // 2
M_CONV = 5
M_CHUNK = 128
A_HEADS = 4
A_NOPE = D_MODEL // 8
A_ROPE = D_MODEL // 16
A_DV = D_MODEL // 8
A_DH = A_NOPE + A_ROPE
Q_LORA = D_MODEL // 4
KV_LORA = D_MODEL // 8
ROPE_THETA = 10000.0
Q_BLOCK = 128
M_QK_W = M_HEADS * M_DQK
M_V_W = M_HEADS * M_DV
N_GATES = 4 * M_HEADS
D_MIX = M_HEADS * M_DV + A_HEADS * A_DV
IN_WIDTHS = (M_QK_W, M_QK_W, M_V_W, M_V_W, N_GATES, Q_LORA, KV_LORA, A_ROPE)
D_IN = M_QK_W * 2 + M_V_W * 2 + N_GATES + Q_LORA + KV_LORA + A_ROPE

kernel_name = "hybrid_mlstm_mla_encoder_layer"


def rms_norm(x, g):
    xf = x.astype(jnp.float32)
    y = xf * lax.rsqrt(jnp.mean(xf * xf, axis=-1, keepdims=True) + EPS)
    return (y * g.astype(jnp.float32)).astype(x.dtype)


def rope(x, cos, sin):
    extra = x.ndim - 3
    c = cos.reshape(cos.shape[:2] + (1,) * extra + cos.shape[2:]).astype(jnp.float32)
    s = sin.reshape(sin.shape[:2] + (1,) * extra + sin.shape[2:]).astype(jnp.float32)
    xf = x.astype(jnp.float32)
    x1, x2 = xf[..., : A_ROPE // 2], xf[..., A_ROPE // 2:]
    return jnp.concatenate([x1 * c - x2 * s, x2 * c + x1 * s], axis=-1).astype(x.dtype)


def conv_centred(u, w, b):
    out = lax.conv_general_dilated(
        u, w[:, None, :].astype(u.dtype), window_strides=(1,),
        padding=[(M_CONV // 2, M_CONV // 2)],
        dimension_numbers=("NWC", "WIO", "NWC"), feature_group_count=u.shape[-1])
    return out + b.astype(u.dtype)


def mlstm_chunkwise(q, k, v, log_i, log_f):
    B, H, S, _ = q.shape
    L = M_CHUNK
    NC = S // L
    qc = q.reshape(B, H, NC, L, M_DQK)
    kc = k.reshape(B, H, NC, L, M_DQK)
    vc = v.reshape(B, H, NC, L, M_DV)
    li = log_i.reshape(B, H, NC, L)
    b = jnp.cumsum(log_f.reshape(B, H, NC, L), axis=-1)
    a = b[..., -1]
    w_log = a[..., None] - b + li
    g = jnp.max(w_log, axis=-1)
    w = jnp.exp(w_log - g[..., None])
    S_c = jnp.einsum('bhclv,bhcld->bhcvd', w[..., None] * vc, kc)
    n_c = jnp.einsum('bhcl,bhcld->bhcd', w, kc)

    def step(carry, inp):
        C, n, m = carry
        a_k, g_k, S_k, n_k = inp
        m_new = jnp.maximum(a_k + m, g_k)
        dec = jnp.exp(a_k + m - m_new)
        add = jnp.exp(g_k - m_new)
        C_new = dec[..., None, None] * C + add[..., None, None] * S_k
        n_new = dec[..., None] * n + add[..., None] * n_k
        return (C_new, n_new, m_new), (C, n, m)

    init = (jnp.zeros((B, H, M_DV, M_DQK), jnp.float32),
            jnp.zeros((B, H, M_DQK), jnp.float32),
            jnp.zeros((B, H), jnp.float32))
    xs = (jnp.moveaxis(a, 2, 0), jnp.moveaxis(g, 2, 0),
          jnp.moveaxis(S_c, 2, 0), jnp.moveaxis(n_c, 2, 0))
    _, (C_prev, n_prev, m_prev) = lax.scan(step, init, xs)
    C_prev = jnp.moveaxis(C_prev, 0, 2)
    n_prev = jnp.moveaxis(n_prev, 0, 2)
    m_prev = jnp.moveaxis(m_prev, 0, 2)

    D = b[..., :, None] - b[..., None, :] + li[..., None, :]
    causal_in_chunk = jnp.tril(jnp.ones((L, L), dtype=bool))
    D = jnp.where(causal_in_chunk, D, -jnp.inf)
    m_t = jnp.maximum(b + m_prev[..., None], jnp.max(D, axis=-1))
    P = jnp.exp(D - m_t[..., None]) * jnp.einsum('bhctd,bhcsd->bhcts', qc, kc)
    inter = jnp.exp(b + m_prev[..., None] - m_t)
    num = (jnp.einsum('bhcts,bhcsv->bhctv', P, vc)
           + inter[..., None] * jnp.einsum('bhctd,bhcvd->bhctv', qc, C_prev))
    den = jnp.sum(P, axis=-1) + inter * jnp.einsum('bhctd,bhcd->bhct', qc, n_prev)
    den = jnp.maximum(jnp.abs(den), jnp.exp(-m_t))
    return (num / den[..., None]).reshape(B, H, S, M_DV)


def mlstm_group(q, k, v, o, gates, b_gates, g_out):
    B, S, _ = q.shape
    f32 = jnp.float32

    def heads(t, d):
        return t.reshape(B, S, M_HEADS, d).transpose(0, 2, 1, 3).astype(f32)

    qh = heads(q, M_DQK) * (M_DQK ** -0.5)
    kh = heads(k, M_DQK)
    vh = heads(v, M_DV)
    gp = (gates.astype(f32) + b_gates.astype(f32)).reshape(B, S, 4, M_HEADS).transpose(2, 0, 3, 1)
    i_fw, f_fw, i_bw, f_bw = gp[0], gp[1], gp[2], gp[3]
    h_fw = mlstm_chunkwise(qh, kh, vh, i_fw, jax.nn.log_sigmoid(f_fw))

    def flip(t):
        return jnp.flip(t, axis=2)

    h_bw = flip(mlstm_chunkwise(flip(qh), flip(kh), flip(vh), flip(i_bw),
                                flip(jax.nn.log_sigmoid(f_bw))))
    h = (h_fw + h_bw).transpose(0, 2, 1, 3)
    h = rms_norm(h, g_out.reshape(M_HEADS, M_DV)).reshape(B, S, M_V_W)
    return (jax.nn.sigmoid(o.astype(f32)) * h).astype(q.dtype)


def mla_group(c_q, c_kv, k_pe, cos, sin, g_cq, g_ckv, w_uq, w_ukv, g_q, g_k):
    B, S, _ = c_q.shape
    q = (rms_norm(c_q, g_cq) @ w_uq).reshape(B, S, A_HEADS, A_DH)
    kv = (rms_norm(c_kv, g_ckv) @ w_ukv).reshape(B, S, A_HEADS, A_NOPE + A_DV)
    k_nope, v = kv[..., :A_NOPE], kv[..., A_NOPE:]
    k = jnp.concatenate(
        [k_nope, jnp.broadcast_to(k_pe[:, :, None, :], (B, S, A_HEADS, A_ROPE))], axis=-1)
    q = rms_norm(q, g_q)
    k = rms_norm(k, g_k)
    q = jnp.concatenate([q[..., :A_NOPE], rope(q[..., A_NOPE:], cos, sin)], axis=-1)
    k = jnp.concatenate([k[..., :A_NOPE], rope(k[..., A_NOPE:], cos, sin)], axis=-1)
    nb = S // Q_BLOCK
    qb = q.reshape(B, nb, Q_BLOCK, A_HEADS, A_DH).transpose(1, 0, 3, 2, 4)
    kh = k.transpose(0, 2, 1, 3)
    vh = v.transpose(0, 2, 1, 3)
    scale = A_DH ** -0.5

    def attend(q_blk):
        s = jnp.einsum('bhqd,bhkd->bhqk', q_blk, kh,
                       preferred_element_type=jnp.float32) * scale
        p = jax.nn.softmax(s, axis=-1)
        return jnp.einsum('bhqk,bhkv->bhqv', p.astype(vh.dtype), vh)

    out = lax.map(attend, qb)
    return out.transpose(1, 0, 3, 2, 4).reshape(B, S, A_HEADS * A_DV)


def setup_inputs(seed: int = 0) -> dict:
    key = jax.random.key(seed)
    ks = jax.random.split(key, 24)
    f32 = jnp.float32

    def nrm(k, shape, scale):
        return jax.random.normal(k, shape, f32) * scale

    def gain(k, shape):
        return 1.0 + 0.02 * jax.random.normal(k, shape, f32)

    x = jax.random.normal(ks[0], (BATCH, SEQ, D_MODEL), f32)
    offset = jax.random.randint(ks[1], (BATCH, 1), 0, 1024, dtype=jnp.int32)
    positions = (jnp.arange(SEQ, dtype=jnp.int32)[None, :] + offset).astype(jnp.int32)
    gb = jax.random.normal(ks[2], (DEPTH, 4, M_HEADS), f32)
    forget_bias = 3.0 + jnp.linspace(0.0, 3.0, M_HEADS, dtype=f32)
    b_gates = jnp.stack([0.1 * gb[:, 0], forget_bias + 0.3 * gb[:, 1],
                         0.1 * gb[:, 2], forget_bias + 0.3 * gb[:, 3]], axis=1).reshape(DEPTH, N_GATES)
    return {
        "x": x,
        "positions": positions,
        "g_mix_norm": gain(ks[3], (DEPTH, D_MODEL)),
        "w_in": nrm(ks[4], (DEPTH, D_MODEL, D_IN), D_MODEL ** -0.5),
        "conv_w": nrm(ks[5], (DEPTH, M_CONV, 2 * M_QK_W), M_CONV ** -0.5),
        "conv_b": nrm(ks[6], (DEPTH, 2 * M_QK_W), 0.01),
        "b_gates": b_gates,
        "g_mlstm_out": gain(ks[7], (DEPTH, M_V_W)),
        "g_cq": gain(ks[8], (DEPTH, Q_LORA)),
        "g_ckv": gain(ks[9], (DEPTH, KV_LORA)),
        "w_uq": nrm(ks[10], (DEPTH, Q_LORA, A_HEADS * A_DH), Q_LORA ** -0.5),
        "w_ukv": nrm(ks[11], (DEPTH, KV_LORA, A_HEADS * (A_NOPE + A_DV)), KV_LORA ** -0.5),
        "g_q": gain(ks[12], (DEPTH, A_DH)),
        "g_k": gain(ks[13], (DEPTH, A_DH)),
        "w_out": nrm(ks[14], (DEPTH, D_MIX, D_MODEL), D_MIX ** -0.5),
        "g_ffn_norm": gain(ks[15], (DEPTH, D_MODEL)),
        "w_ff1": nrm(ks[16], (DEPTH, D_MODEL, D_FF), D_MODEL ** -0.5),
        "w_ff2": nrm(ks[17], (DEPTH, D_FF, D_MODEL), D_FF ** -0.5),
    }


def reference(x, positions, g_mix_norm, w_in, conv_w, conv_b, b_gates, g_mlstm_out,
              g_cq, g_ckv, w_uq, w_ukv, g_q, g_k, w_out, g_ffn_norm, w_ff1, w_ff2):
    inv_freq = ROPE_THETA ** (-jnp.arange(0, A_ROPE, 2, dtype=jnp.float32) / A_ROPE)
    ang = positions.astype(jnp.float32)[..., None] * inv_freq
    cos, sin = jnp.cos(ang), jnp.sin(ang)
    splits = [int(s) for s in np.cumsum(IN_WIDTHS)[:-1]]
    for l in range(DEPTH):
        h = rms_norm(x, g_mix_norm[l])
        z = h @ w_in[l]
        q_m, k_m, v_m, o_m, gates, c_q, c_kv, k_pe = jnp.split(z, splits, axis=-1)
        qk = jax.nn.silu(conv_centred(jnp.concatenate([q_m, k_m], axis=-1), conv_w[l], conv_b[l]))
        q_m, k_m = qk[..., :M_QK_W], qk[..., M_QK_W:]
        y_a = mlstm_group(q_m, k_m, v_m, o_m, gates, b_gates[l], g_mlstm_out[l])
        y_b = mla_group(c_q, c_kv, k_pe, cos, sin, g_cq[l], g_ckv[l], w_uq[l], w_ukv[l],
                        g_q[l], g_k[l])
        x = x + jnp.concatenate([y_a, y_b], axis=-1) @ w_out[l]
        h = rms_norm(x, g_ffn_norm[l])
        x = x + jnp.square(jax.nn.relu(h @ w_ff1[l])) @ w_ff2[l]
    return x
```

```python
import numpy as np
import ml_dtypes
import concourse.bass as bass
import concourse.mybir as mybir
from concourse.bass_utils import run_bass_kernel_spmd

F32 = mybir.dt.float32
BF16 = mybir.dt.bfloat16
I32 = mybir.dt.int32
ALU = mybir.AluOpType
AF = mybir.ActivationFunctionType
AX = mybir.AxisListType

EPS = 1e-6
NEG = -30000.0
TWO_PI = float(2.0 * np.pi)


class Prog:
    ENGS = ("pe", "act", "dve", "pool", "sp")

    def __init__(self, nc):
        self.nc = nc
        self.ops = []
        self.last_w = {}
        self.readers = {}
        self.dma_count = {}
        self.last_on_eng = {}
        self.dma_last = {}
        self.barrier_deps = None

    def _add(self, eng, fn, reads, writes, dma_key=None):
        writes = tuple(writes) + tuple(r for r in reads if r.startswith("ps") and r not in writes)
        oid = len(self.ops)
        deps = set()
        for r in reads:
            if r in self.last_w:
                deps.add(self.last_w[r])
        for w in writes:
            if w in self.last_w:
                deps.add(self.last_w[w])
            for rd in self.readers.get(w, ()):
                deps.add(rd)
        if self.barrier_deps is not None and eng not in self.barrier_deps[1]:
            deps |= self.barrier_deps[0]
            self.barrier_deps[1].add(eng)
        deps.discard(oid)
        op = dict(id=oid, eng=eng, fn=fn, deps=deps, dma_key=dma_key, signal=False)
        if dma_key is not None:
            self.dma_count[dma_key] = self.dma_count.get(dma_key, 0) + 16
            op["dma_val"] = self.dma_count[dma_key]
            self.dma_last[dma_key] = oid
        self.ops.append(op)
        for w in writes:
            self.last_w[w] = oid
            self.readers[w] = []
        for r in reads:
            self.readers.setdefault(r, []).append(oid)
        self.last_on_eng[eng if dma_key is None else ("dma", dma_key)] = oid
        return oid

    def op(self, eng, fn, reads=(), writes=()):
        return self._add(eng, fn, tuple(reads), tuple(writes))

    def dma(self, fn, reads=(), writes=(), key=None, queue="sp"):
        return self._add(queue, fn, tuple(reads), tuple(writes), dma_key=key)

    def barrier(self):
        deps = set()
        for k, oid in self.last_on_eng.items():
            deps.add(oid)
        self.barrier_deps = (deps, set())

    def emit(self, block, sems, dma_sems):
        ops = self.ops
        for o in ops:
            for d in o["deps"]:
                ops[d]["signal"] = True
        cnt = {e: 0 for e in self.ENGS}
        for o in ops:
            if o["dma_key"] is None and o["signal"]:
                cnt[o["eng"]] += 1
                o["sig_val"] = cnt[o["eng"]]
        by_eng = {e: [o for o in ops if o["eng"] == e] for e in self.ENGS}

        def run(eng_name, e):
            waited = {}
            for o in by_eng[eng_name]:
                need = {}
                for d in o["deps"]:
                    od = ops[d]
                    if od["dma_key"] is not None:
                        k = ("dma", od["dma_key"])
                        need[k] = max(need.get(k, 0), od["dma_val"])
                    else:
                        if od["eng"] == "pe" and eng_name == "pe":
                            continue
                        k = ("eng", od["eng"])
                        need[k] = max(need.get(k, 0), od["sig_val"])
                for k, v in need.items():
                    if waited.get(k, 0) >= v:
                        continue
                    waited[k] = v
                    if k[0] == "dma":
                        e.wait_ge(dma_sems[k[1]], v)
                    else:
                        e.wait_ge(sems[k[1]], v)
                ins = o["fn"](e)
                if o["dma_key"] is not None:
                    ins.then_inc(dma_sems[o["dma_key"]], 16)
                elif o["signal"]:
                    ins.then_inc(sems[o["eng"]], 1)
            if eng_name == "sp":
                for k, v in self.dma_count.items():
                    e.wait_ge(dma_sems[k], v)

        @block.tensor
        def _(e):
            run("pe", e)

        @block.scalar
        def _(e):
            run("act", e)

        @block.vector
        def _(e):
            run("dve", e)

        @block.gpsimd
        def _(e):
            run("pool", e)

        @block.sync
        def _(e):
            run("sp", e)


def _prod(s):
    r = 1
    for v in s:
        r *= int(v)
    return r


class Arena:
    def __init__(self, ap, words):
        self.ap = ap
        self.words = words
        self.off = 0

    def mark(self):
        return self.off

    def reset(self, off):
        self.off = off

    def alloc(self, shape, dtype=F32):
        n = _prod(shape)
        size = 4 if dtype in (F32, I32) else 2
        w = (n * size + 3) // 4
        w = (w + 7) // 8 * 8
        assert self.off + w <= self.words, ("arena overflow", self.off, w, self.words)
        v = self.ap[:, self.off:self.off + w]
        self.off += w
        if dtype != F32:
            v = v.bitcast(dtype)
        v = v[:, 0:n]
        return view(v, shape)


def view(ap, shape):
    if len(shape) == 1:
        return ap
    names = "abcdefg"[:len(shape)]
    kw = {names[i]: int(shape[i]) for i in range(len(shape) - 1)}
    return ap.rearrange("p (" + " ".join(names) + ") -> p " + " ".join(names), **kw)


def _consts():
    c = {}
    idx = np.arange(128)
    c["ident"] = np.eye(128, dtype=np.float32)
    k = idx[:, None]
    p = idx[None, :]
    c["tri_le"] = (k <= p).astype(np.float32)
    c["tri_lt"] = (k < p).astype(np.float32)
    c["tri_ge"] = (k >= p).astype(np.float32)
    inv = (10000.0 ** (-np.arange(0, 64, 2, dtype=np.float32) / 64.0)).astype(np.float32)
    col = np.zeros((128, 4), np.float32)
    for q in range(128):
        i = q % 64
        col[q, 0] = inv[i % 32]
        col[q, 1] = (np.pi / 2) if q < 64 else 0.0
        col[q, 2] = 1.0 if q < 64 else (-1.0 if i < 32 else 1.0)
    c["ropec"] = col
    fold = np.zeros((128, 64), np.float32)
    fold[np.arange(64), np.arange(64)] = 1.0
    fold[np.arange(64) + 64, np.arange(64)] = 1.0
    c["fold"] = fold
    return c


def _perm64():
    i = np.arange(64)
    return np.where(i < 32, i + 32, i - 32)


VC = {}
_o = 0
for _n, _w in (("gmix", 8), ("gffn", 8), ("cwq", 10), ("cwk", 10), ("cbq", 2), ("cbk", 2), ("gcq", 2),
               ("gckv", 1), ("gqn", 1), ("gkn", 1), ("gqr", 1), ("gkr", 1)):
    VC[_n] = (_o, _w)
    _o += _w
NVC = _o
VR = {}
_o = 0
for _n, _w in (("bg", 16), ("gout", 512), ("cm", 128)):
    VR[_n] = (_o, _w)
    _o += _w
NVR = _o


def make_in_maps(inp):
    f32 = np.float32
    x = np.asarray(inp["x"], f32)
    pos = np.asarray(inp["positions"]).astype(np.int32)
    perm = _perm64()
    vc = np.zeros((128, NVC), f32)

    def put(name, arr):
        o, w = VC[name]
        vc[:, o:o + w] = np.asarray(arr, f32).reshape(128, w)

    put("gmix", np.asarray(inp["g_mix_norm"], f32)[0].reshape(8, 128).T)
    put("gffn", np.asarray(inp["g_ffn_norm"], f32)[0].reshape(8, 128).T)
    cw = np.asarray(inp["conv_w"], f32)[0]
    cb = np.asarray(inp["conv_b"], f32)[0]
    put("cwq", cw[:, 0:256].reshape(5, 2, 128).transpose(2, 1, 0))
    put("cwk", cw[:, 256:512].reshape(5, 2, 128).transpose(2, 1, 0))
    put("cbq", cb[0:256].reshape(2, 128).T)
    put("cbk", cb[256:512].reshape(2, 128).T)
    put("gcq", np.asarray(inp["g_cq"], f32)[0].reshape(2, 128).T)
    put("gckv", np.asarray(inp["g_ckv"], f32)[0].reshape(128, 1))
    gq = np.asarray(inp["g_q"], f32)[0]
    gk = np.asarray(inp["g_k"], f32)[0]
    put("gqn", gq[0:128].reshape(128, 1))
    put("gkn", gk[0:128].reshape(128, 1))
    put("gqr", np.concatenate([gq[128:192], gq[128:192][perm]]).reshape(128, 1))
    put("gkr", np.concatenate([gk[128:192], gk[128:192][perm]]).reshape(128, 1))
    cst = _consts()
    maps = []
    for c in range(8):
        b, j = c // 4, c % 4
        xb = x[b]
        xT = xb.T
        xa = np.ascontiguousarray(xT.reshape(8, 128, 16, 512).transpose(2, 1, 0, 3))
        s, e = 2048 * j, 2048 * (j + 1)
        xo = np.ascontiguousarray(xT[:, s:e].reshape(8, 128, 4, 512).transpose(2, 1, 0, 3))
        halo = np.zeros((1024, 4), f32)
        if j > 0:
            halo[:, 0:2] = xT[:, s - 2:s]
        if j < 3:
            halo[:, 2:4] = xT[:, e:e + 2]
        xh = np.ascontiguousarray(halo.reshape(8, 128, 4).transpose(1, 0, 2))
        xr = np.ascontiguousarray(xb[s:e].reshape(16, 128, 1024))
        vr = np.zeros((128, NVR), f32)
        o, w = VR["bg"]
        vr[:, o:o + w] = np.asarray(inp["b_gates"], f32)[0][None, :]
        o, w = VR["gout"]
        vr[:, o:o + w] = np.asarray(inp["g_mlstm_out"], f32)[0][None, :]
        cm = np.zeros((2, 64), f32)
        cm[0, 0:16 * j] = 1.0
        cm[1, 16 * (j + 1):] = 1.0
        o, w = VR["cm"]
        vr[:, o:o + w] = cm.reshape(1, 128)
        m = {
            "xa": xa, "xo": xo, "xh": xh, "xr": xr,
            "posa": np.ascontiguousarray(pos[b].reshape(1, 8192)),
            "poso": np.ascontiguousarray(pos[b, s:e].reshape(1, 2048)),
            "vc": vc, "vr": vr,
            "w_in": np.ascontiguousarray(np.asarray(inp["w_in"], f32)[0]),
            "w_uq": np.ascontiguousarray(np.asarray(inp["w_uq"], f32)[0]),
            "w_ukv": np.ascontiguousarray(np.asarray(inp["w_ukv"], f32)[0]),
            "w_out": np.ascontiguousarray(np.asarray(inp["w_out"], f32)[0]),
            "w_ff1": np.ascontiguousarray(np.asarray(inp["w_ff1"], f32)[0]),
            "w_ff2": np.ascontiguousarray(np.asarray(inp["w_ff2"], f32)[0]),
            "c_ident": cst["ident"], "c_le": cst["tri_le"], "c_lt": cst["tri_lt"], "c_ge": cst["tri_ge"],
            "c_rope": cst["ropec"], "c_fold": cst["fold"],
        }
        maps.append(m)
    return maps


ARENA_WORDS = 52800
R1 = 2600
R2 = 10900
R5 = 23600


class _Stop(Exception):
    pass


def build(dbg=False, upto=None):
    nc = bass.Bass("TRN2", target_bir_lowering=False)
    P = Prog(nc)

    def din(name, shape, dt=F32):
        return nc.dram_tensor(name, list(shape), dt, kind="ExternalInput").ap()

    xa = din("xa", [16, 128, 8, 512])
    xo = din("xo", [4, 128, 8, 512])
    xh = din("xh", [128, 8, 4])
    xr = din("xr", [16, 128, 1024])
    posa = din("posa", [1, 8192], I32)
    poso = din("poso", [1, 2048], I32)
    vc_d = din("vc", [128, NVC])
    vr_d = din("vr", [128, NVR])
    w_in = din("w_in", [1024, 2000])
    w_uq = din("w_uq", [256, 768])
    w_ukv = din("w_ukv", [128, 1024])
    w_out = din("w_out", [1024, 1024])
    w_ff1 = din("w_ff1", [1024, 4096])
    w_ff2 = din("w_ff2", [4096, 1024])
    c_ident = din("c_ident", [128, 128])
    c_le = din("c_le", [128, 128])
    c_lt = din("c_lt", [128, 128])
    c_ge = din("c_ge", [128, 128])
    c_rope = din("c_rope", [128, 4])
    c_fold = din("c_fold", [128, 64])
    out_d = nc.dram_tensor("out", [16, 128, 1024], F32, kind="ExternalOutput").ap()
    w1s = nc.dram_tensor("w1s", [8, 128, 8, 512], BF16, kind="Internal").ap()
    dbg_d = {}
    if dbg:
        dbg_d["yT"] = nc.dram_tensor("d_yT", [128, 8, 2048], BF16, kind="ExternalOutput").ap()
        dbg_d["misc"] = nc.dram_tensor("d_misc", [128, 8192], F32, kind="ExternalOutput").ap()

    import contextlib
    es = contextlib.ExitStack()
    with es:
        arena_t = es.enter_context(nc.sbuf_tensor("arena", [128, ARENA_WORDS], F32))
        ipos_t = es.enter_context(nc.sbuf_tensor("ipos", [128, 256], I32))
        ps = [es.enter_context(nc.psum_tensor(f"ps{i}", [128, 512], F32)) for i in range(8)]
        sems = {e: es.enter_context(nc.semaphore(f"s_{e}")) for e in ("pe", "act", "dve", "pool")}
        dma_sems = {}
        A = Arena(arena_t[:, :], ARENA_WORDS)

        def MM(out, lhsT, rhs, R, W, start=True, stop=True, skip=False):
            P.op("pe", lambda e: e.matmul(out, lhsT=lhsT, rhs=rhs, start=start, stop=stop,
                                          skip_group_check=skip), R, W)

        def ACT(out, in_, func, R, W, bias=0.0, scale=1.0, accum=None):
            if accum is None:
                P.op("act", lambda e: e.activation(out=out, in_=in_, func=func, bias=bias, scale=scale), R, W)
            else:
                P.op("act", lambda e: e.activation(out=out, in_=in_, func=func, bias=bias, scale=scale,
                                                   accum_out=accum), R, W)

        def _eng(e, name):
            return e

        def TT(eng, out, in0, in1, op, R, W):
            P.op(eng, lambda e: e.tensor_tensor(out=out, in0=in0, in1=in1, op=op), R, W)

        def TS(eng, out, in0, s1, op0, R, W, s2=None, op1=None):
            if op1 is None:
                P.op(eng, lambda e: e.tensor_scalar(out=out, in0=in0, scalar1=s1, scalar2=None, op0=op0), R, W)
            else:
                P.op(eng, lambda e: e.tensor_scalar(out=out, in0=in0, scalar1=s1, scalar2=s2, op0=op0, op1=op1),
                     R, W)

        def STT(eng, out, in0, scalar, in1, op0, op1, R, W):
            P.op(eng, lambda e: e.scalar_tensor_tensor(out=out, in0=in0, scalar=scalar, in1=in1, op0=op0, op1=op1),
                 R, W)

        def CP(eng, out, in_, R, W):
            if eng == "act":
                P.op(eng, lambda e: e.activation(out=out, in_=in_, func=AF.Copy), R, W)
            else:
                P.op(eng, lambda e: e.tensor_copy(out=out, in_=in_), R, W)

        def RED(eng, out, in_, op, R, W):
            P.op(eng, lambda e: e.tensor_reduce(out=out, in_=in_, axis=AX.X, op=op), R, W)

        def RECIP(out, in_, R, W):
            P.op("dve", lambda e: e.reciprocal(out=out, in_=in_), R, W)

        def MSET(eng, ap, val, W):
            P.op(eng, lambda e: e.memset(ap, val), (), W)

        def DMA(out, in_, R, W, key, queue="sp"):
            if key not in dma_sems:
                dma_sems[key] = es.enter_context(nc.semaphore("d_" + key))
            P.dma(lambda e: e.dma_start(out=out, in_=in_), R, W, key=key, queue=queue)

        PS = [p[:, :] for p in ps]
        PK = [f"ps{i}" for i in range(8)]

        try:
            ident_f = A.alloc([128])
            tri_le = A.alloc([128])
            tri_lt = A.alloc([128])
            tri_ge = A.alloc([128])
            ones_f = A.alloc([128])
            ropec = A.alloc([8])
            fold_f = A.alloc([64])
            vcs = A.alloc([NVC])
            vrs = A.alloc([NVR])
            ident_b = A.alloc([128], BF16)
            ones10 = A.alloc([128], BF16)
            ones8 = A.alloc([128], BF16)
            ones7 = A.alloc([128], BF16)
            fold_b = A.alloc([64], BF16)
            tri_le_b = A.alloc([128], BF16)
            tri_ge_b = A.alloc([128], BF16)
            sel_lo = A.alloc([128], BF16)
            sel_hi = A.alloc([128], BF16)
            cmb = A.alloc([128])
            ropef = A.alloc([4])
            assert A.off <= R1, A.off

            def vcol(name, i=0, n=1):
                o, w = VC[name]
                return vcs[:, o + i:o + i + n]

            def vrow(name):
                o, w = VR[name]
                return vrs[:, o:o + w]

            for dst, src, nm in ((ident_f, c_ident, "id"), (tri_le, c_le, "le"), (tri_lt, c_lt, "lt"),
                                 (tri_ge, c_ge, "ge"), (ropec[:, 0:4], c_rope, "rp"), (fold_f, c_fold, "fo"),
                                 (vcs, vc_d, "vc"), (vrs, vr_d, "vr")):
                DMA(dst, src[:, :], (), ("const",), "const")
            MSET("pool", ones_f, 1.0, ("const2",))
            MSET("pool", ones10, 2.0 ** -10, ("const2",))
            MSET("pool", ones8, 2.0 ** -8, ("const2",))
            MSET("pool", ones7, 2.0 ** -7, ("const2",))
            MSET("pool", sel_lo, 0.0, ("const2",))
            MSET("pool", sel_hi, 0.0, ("const2",))
            MSET("pool", sel_lo[0:64, :], 2.0 ** -8, ("const2",))
            MSET("pool", sel_hi[64:128, :], 2.0 ** -8, ("const2",))
            CP("dve", ident_b, ident_f, ("const",), ("const3",))
            CP("dve", fold_b, fold_f, ("const",), ("const3",))
            CP("dve", tri_le_b, tri_le, ("const",), ("const3",))
            CP("dve", tri_ge_b, tri_ge, ("const",), ("const3",))
            TS("dve", cmb, vrow("cm"), -1.0, ALU.add, ("const",), ("const3",), s2=30000.0, op1=ALU.mult)
            TT("dve", ropef[:, 0:1], vcol("gkr"), ropec[:, 2:3], ALU.mult, ("const",), ("const3",))
            TT("dve", ropef[:, 1:2], vcol("gqr"), ropec[:, 2:3], ALU.mult, ("const",), ("const3",))
            CONST = ("const", "const2", "const3")
            if upto == "consts":
                raise _Stop()

            A.reset(R1)
            Wb = A.alloc([8, 2064], BF16)
            A.reset(R2)
            ckvT = A.alloc([8192], BF16)
            RT = A.alloc([8192], BF16)
            stB = A.alloc([4, 129])
            mB = A.alloc([8])
            cqnT = A.alloc([2, 2048], BF16)
            Wuqb = A.alloc([2, 768], BF16)
            Wqr = A.alloc([2, 4, 128], BF16)
            Wukvb = A.alloc([1024], BF16)
            assert A.off <= R5, A.off

            w_in_v = w_in.rearrange("(c p) n -> p c n", p=128)
            for dc in range(8):
                DMA(Wb[:, dc, 0:2000], w_in_v[:, dc, :], (), ("Wb",), "wb", queue="pool")
            DMA(Wuqb, w_uq.rearrange("(c p) n -> p c n", p=128), (), ("Wuq",), "wu", queue="pool")
            DMA(Wukvb, w_ukv[:, :], (), ("Wukv",), "wukv", queue="pool")
            CP("pool", Wb[:, :, 2000:2032], Wb[:, :, 1968:2000], ("Wb",), ("Wb2",))
            CP("pool", Wb[:, :, 2032:2064], Wb[:, :, 1936:1968], ("Wb",), ("Wb2",))
            for h in range(4):
                CP("pool", Wqr[:, :, h, 0:64], Wuqb[:, :, h * 192 + 128:h * 192 + 192], ("Wuq",), ("Wqr",))
                CP("pool", Wqr[:, :, h, 64:96], Wuqb[:, :, h * 192 + 160:h * 192 + 192], ("Wuq",), ("Wqr",))
                CP("pool", Wqr[:, :, h, 96:128], Wuqb[:, :, h * 192 + 128:h * 192 + 160], ("Wuq",), ("Wqr",))
            w1_v = w_ff1.rearrange("(c p) (g n) -> g p c n", p=128, n=512)
            for g in range(8):
                DMA(w1s[g], w1_v[g], (), ("w1s",), "w1s", queue="pool")

            def rstd(out_ap, ps_ap, okey, pkey, eps=EPS):
                TS("dve", out_ap, ps_ap, float(eps), ALU.add, (pkey,), (okey,))
                ACT(out_ap, out_ap, AF.Ln, (okey,), (okey,))
                ACT(out_ap, out_ap, AF.Exp, (okey,), (okey,), scale=-0.5)

            def tile_p1(xt, xkey, N, xb, xbk, xsq):
                o, w = VC["gmix"]
                TT("pool", xb[:, :, 0:N], xt[:, :, 0:N], vcs[:, o:o + 8].unsqueeze(2).to_broadcast([128, 8, N]),
                   ALU.mult, (xkey,) + CONST, (xbk,))
                ACT(xsq[:, :, 0:N], xt[:, :, 0:N], AF.Square, (xkey,), ("xsq",))

            def tile_p2a(N, xsq, rxbc, rk):
                for dc in range(8):
                    MM(PS[0][:, 0:N], ones10, xsq[:, dc, 0:N], ("xsq",) + CONST, (PK[0],), start=(dc == 0), stop=(dc == 7))
                rstd(rxbc[:, 0:N], PS[0][:, 0:N], rk, PK[0])

            def tile_p2b(N, rxbc, rk, rxcol, ck):
                nsub = max(N // 128, 1)
                for sub in range(nsub):
                    MM(PS[0][:, 508 + sub:509 + sub], rxbc[0:1, sub * 128:(sub + 1) * 128], ones_f[0:1, 0:1],
                       (rk,) + CONST, (PK[0],))
                CP("dve", rxcol[:, 0:nsub], PS[0][:, 508:508 + nsub], (PK[0],), (ck,))

            def tile_body(N, xb, xbk, fm, tm, psrot):
                nsub = max(N // 128, 1)
                for (c0, M, handler) in fm:
                    b = psrot[0] % len(psrot[1])
                    bank = psrot[1][b]
                    psrot[0] += 1
                    for dc in range(8):
                        MM(PS[bank][0:M, 0:N], Wb[:, dc, c0:c0 + M], xb[:, dc, 0:N], (xbk, "Wb", "Wb2"), (PK[bank],),
                           start=(dc == 0), stop=(dc == 7))
                    handler(PS[bank], PK[bank])
                for (c0, ncol, handler) in tm:
                    for sub in range(nsub):
                        b = psrot[0] % len(psrot[1])
                        bank = psrot[1][b]
                        psrot[0] += 1
                        for dc in range(8):
                            MM(PS[bank][:, 0:ncol], xb[:, dc, sub * 128:(sub + 1) * 128], Wb[:, dc, c0:c0 + ncol],
                               (xbk, "Wb", "Wb2"), (PK[bank],), start=(dc == 0), stop=(dc == 7))
                        handler(sub, PS[bank], PK[bank])

            def sweep(ntiles, Ns, xt, xbs, xsq_, rxbcs, rxcols, dma_fn, handlers_fn, lag_fn, psrot):
                def p1(t):
                    dma_fn(t)
                    tile_p1(xt, "xt0", Ns[t], xbs[t % 2], f"xb{t % 2}", xsq_)

                p1(0)
                tile_p2a(Ns[0], xsq_, rxbcs[0], "rxbc0")
                tile_p2b(Ns[0], rxbcs[0], "rxbc0", rxcols[0], "rxcol0")
                for t in range(ntiles):
                    if t + 1 < ntiles:
                        p1(t + 1)
                    fm, tm = handlers_fn(t)
                    tile_body(Ns[t], xbs[t % 2], f"xb{t % 2}", fm, tm, psrot)
                    s1 = (t + 1) % 2
                    if t + 1 < ntiles:
                        tile_p2a(Ns[t + 1], xsq_, rxbcs[s1], f"rxbc{s1}")
                    if lag_fn is not None and t >= 1:
                        lag_fn(t - 1)
                    if t + 1 < ntiles:
                        tile_p2b(Ns[t + 1], rxbcs[s1], f"rxbc{s1}", rxcols[s1], f"rxcol{s1}")
                if lag_fn is not None:
                    lag_fn(ntiles - 1)

            A.reset(R5)
            xt_a = [A.alloc([8, 512])] * 2
            xb_a = [A.alloc([8, 512], BF16) for _ in range(2)]
            xsq = A.alloc([8, 512], BF16)
            rxbcs = [A.alloc([512]) for _ in range(2)]
            rxcols = [A.alloc([4]) for _ in range(2)]
            W = [A.alloc([512]) for _ in range(4)]
            G_all = A.alloc([64, 16])
            w_all = A.alloc([512])
            sm = A.alloc([64])
            rxc_all = A.alloc([64])
            markA = A.mark()
            GB = {n: A.alloc([512]) for n in ("li", "fp", "lim", "pfa", "pfb", "t1", "ell")}
            Wkp = A.alloc([512])
            Wsq = A.alloc([512])
            print("pass A1 arena end", A.off)

            posf = W[3]

            def rope_table(tab, tkey, pos_src, N):
                ip = ipos_t[:, :]
                kf = A_tmp["kf"]
                for hf in range(N // 256):
                    cs_ = slice(hf * 256, (hf + 1) * 256)
                    DMA(ip, pos_src[:, hf * 256:(hf + 1) * 256].partition_broadcast(128), (), ("ipos",), "posi")
                    CP("dve", tab[:, cs_], ip, ("ipos",), (tkey,))
                    TS("dve", tab[:, cs_], tab[:, cs_], ropec[:, 0:1], ALU.mult, (tkey,) + CONST, (tkey,), s2=ropec[:, 1:2],
                       op1=ALU.add)
                    TS("dve", kf[:, cs_], tab[:, cs_], 1.0 / TWO_PI, ALU.mult, (tkey,), ("W1",))
                    CP("dve", ip, kf[:, cs_], ("W1",), ("ipos",))
                    CP("dve", kf[:, cs_], ip, ("ipos",), ("W1",))
                STT("dve", tab[:, 0:N], kf[:, 0:N], -6.28125, tab[:, 0:N], ALU.mult, ALU.add, ("W1", tkey), (tkey,))
                STT("dve", tab[:, 0:N], kf[:, 0:N], -(TWO_PI - 6.28125), tab[:, 0:N], ALU.mult, ALU.add, ("W1", tkey),
                    (tkey,))
                TS("dve", tab[:, 0:N], tab[:, 0:N], 3.1415925, ALU.min, (tkey,), (tkey,), s2=-3.1415925, op1=ALU.max)
                ACT(tab[:, 0:N], tab[:, 0:N], AF.Sin, (tkey,), (tkey,))

            A_tmp = {"pi": W[2].bitcast(I32), "kf": W[1]}
            psrotA = [0, [1, 2, 3]]

            def passA1_tile(t):
                slot = t % 2
                rxbc, rxcol, rk, ck = rxbcs[slot], rxcols[slot], f"rxbc{slot}", f"rxcol{slot}"
                tok = slice(t * 512, (t + 1) * 512)
                rope_table(posf, "posf", posa[0:1, t * 512:(t + 1) * 512], 512)

                def h_ckv(psb, pk):
                    TT("dve", W[0], psb[:, 0:512], rxbc, ALU.mult, (pk, rk), ("W0",))
                    ACT(W[1].bitcast(BF16)[:, 0:512], W[0], AF.Square, ("W0",), ("W1",))
                    MM(PS[4][:, 0:512], ones7, W[1].bitcast(BF16)[:, 0:512], ("W1",) + CONST, (PK[4],))
                    rstd(W[2], PS[4][:, 0:512], "W2", PK[4])
                    STT("dve", ckvT[:, tok], W[0], vcol("gckv"), W[2], ALU.mult, ALU.mult, ("W0", "W2") + CONST,
                        (f"ckvT{t}",))

                def h_kp(psb, pk):
                    STT("dve", Wkp.bitcast(BF16)[:, 0:512], psb[:, 0:512], ropef[:, 0:1], posf, ALU.mult, ALU.mult,
                        (pk, "posf") + CONST, ("Wkp",))
                    MM(PS[5][0:64, 0:512], fold_b, Wkp.bitcast(BF16)[:, 0:512], ("Wkp",) + CONST, (PK[5],))
                    TT("dve", RT[0:64, tok], PS[5][0:64, 0:512], rxbc[0:64, :], ALU.mult, (PK[5], rk), (f"RTa{t}",))

                def h_kpsq(psb, pk):
                    TT("dve", Wsq[64:128, :], psb[64:128, 0:512], rxbc[64:128, :], ALU.mult, (pk, rk), ("Wsq",))
                    ACT(RT[64:128, tok], Wsq[64:128, :], AF.Square, ("Wsq",), (f"RTb{t}",))

                def h_g(sub, psb, pk):
                    TS("dve", G_all[:, 4 * t + sub, :], psb[:, 0:16], rxcol[:, sub:sub + 1], ALU.mult, (pk, ck),
                       ("G_all",))

                CP("pool", rxc_all[:, 4 * t:4 * t + 4], rxcol[:, 0:4], (ck,), ("rxc_all",))
                fm = [(1808, 128, h_ckv), (1936, 128, h_kp), (1872, 128, h_kpsq)]
                tm = [(1536, 16, h_g)]
                return fm, tm

            if upto == "W":
                raise _Stop()
            sweep(16, [512] * 16, xt_a[0], xb_a, xsq, rxbcs, rxcols,
                  lambda t: DMA(xt_a[0], xa[t], (), ("xt0",), "xt0"), passA1_tile, None, psrotA)

            if upto == "A1":
                raise _Stop()
            def d3(ap):
                return view(ap, [2, 64, 4])

            Gv = view(G_all.rearrange("p a b -> p (a b)"), [64, 2, 2, 4])
            bgv = view(vrow("bg"), [2, 2, 4])
            for two, nm in ((0, "li"), (1, "fp")):
                TT("dve", GB[nm].rearrange("p (d c h) -> p c d h", d=2, c=64), Gv[:, :, :, two, :],
                   bgv[:, :, two, :].unsqueeze(1).to_broadcast([128, 64, 2, 4]), ALU.add, ("G_all",) + CONST, ("b_" + nm,))
            ACT(GB["fp"], GB["fp"], AF.Exp, ("b_fp",), ("b_fp",), scale=-1.0)
            ACT(GB["fp"], GB["fp"], AF.Ln, ("b_fp",), ("b_fp",), bias=1.0)
            cmv = view(vrow("cm"), [2, 64]).unsqueeze(3).to_broadcast([128, 2, 64, 4])
            cmbv = view(cmb, [2, 64]).unsqueeze(3).to_broadcast([128, 2, 64, 4])
            TT("dve", d3(GB["fp"]), d3(GB["fp"]), cmv, ALU.mult, ("b_fp",) + CONST, ("b_fp",))
            TT("dve", d3(GB["lim"]), d3(GB["li"]), cmv, ALU.mult, ("b_li",) + CONST, ("b_lim",))
            TT("dve", d3(GB["lim"]), d3(GB["lim"]), cmbv, ALU.add, ("b_lim",) + CONST, ("b_lim",))
            for q4 in range(4):
                cs_ = slice(q4 * 128, (q4 + 1) * 128)
                MM(PS[5][:, cs_], tri_le if q4 < 2 else tri_lt, GB["fp"][:, cs_], ("b_fp",) + CONST, (PK[5],))
                MM(PS[4][:, cs_], ones_f, GB["fp"][:, cs_], ("b_fp",) + CONST, (PK[4],))
            CP("dve", GB["pfa"], PS[4][:, 0:512], (PK[4],), ("b_pfa",))
            CP("act", GB["li"], PS[4][:, 0:512], (PK[4], "b_li", "b_lim"), ("b_tot", "b_li"))
            src, dst = "pfa", "pfb"
            k_ = 1
            while k_ < 64:
                TT("dve", d3(GB[dst])[:, :, k_:64, :], d3(GB[src])[:, :, k_:64, :], d3(GB[src])[:, :, 0:64 - k_, :],
                   ALU.add, ("b_" + src,), ("b_" + dst,))
                CP("pool", d3(GB[dst])[:, :, 0:k_, :], d3(GB[src])[:, :, 0:k_, :], ("b_" + src,), ("b_" + dst,))
                src, dst = dst, src
                k_ *= 2
            incl = src
            carry = sm[:, 0:8]
            CP("dve", view(carry, [2, 4]), d3(GB[incl])[:, :, 63, :], ("b_" + incl,), ("sm",))
            TT("dve", GB["t1"], GB[incl], GB["li"], ALU.subtract, ("b_" + incl, "b_tot"), ("b_t1",))
            TT("dve", GB["t1"], GB["t1"], PS[5][:, 0:512], ALU.add, ("b_t1", PK[5]), ("b_t1",))
            TT("dve", GB["ell"][:, 0:256], GB["lim"][:, 0:256], GB["t1"][:, 0:256], ALU.add, ("b_lim", "b_t1"), ("b_ell",))
            TT("dve", GB["ell"][:, 256:512], GB["lim"][:, 256:512], GB["t1"][:, 256:512], ALU.subtract,
               ("b_lim", "b_t1"), ("b_ell",))
            gm1_ = sm[:, 8:16]
            mx_ = sm[:, 16:17]
            dg_ = sm[:, 24:32]
            mref = sm[:, 32:40]
            RED("dve", view(gm1_, [2, 4]), GB["ell"].rearrange("p (d c h) -> p d h c", d=2, c=64), ALU.max, ("b_ell",),
                ("sm",))
            MM(PS[5][0:8, 0:128], gm1_, ident_f, ("sm",) + CONST, (PK[5],))
            RED("dve", mx_[0:8, :], PS[5][0:8, 0:128], ALU.max, (PK[5], "sm"), ("sm",))
            TS("dve", dg_[0:8, :], ident_f[0:8, 0:8], mx_[0:8, 0:1], ALU.mult, ("sm",) + CONST, ("sm",))
            MM(PS[5][:, 256:264], ones_f[0:8, :], dg_[0:8, :], ("sm",) + CONST, (PK[5],))
            TS("dve", mref[:, 0:4], PS[5][:, 256:260], 0.0, ALU.max, (PK[5], "sm"), ("sm",))
            STT("dve", mref[:, 4:8], carry[:, 4:8], -1.0, PS[5][:, 260:264], ALU.mult, ALU.max, (PK[5], "sm"), ("sm",))
            TT("dve", mB[:, 0:4], mref[:, 0:4], carry[:, 0:4], ALU.subtract, ("sm",), ("mB",))
            CP("dve", mB[:, 4:8], mref[:, 4:8], ("sm",), ("mB",))
            TT("dve", d3(w_all), d3(GB["ell"]), view(mref, [2, 4]).unsqueeze(2).to_broadcast([128, 2, 64, 4]),
               ALU.subtract, ("b_ell", "sm"), ("w_all",))
            ACT(w_all, w_all, AF.Exp, ("w_all",), ("w_all",))
            P.barrier()
            if upto == "A1g":
                raise _Stop()

            A.reset(markA)
            kpre = A.alloc([2, 8196], BF16)
            vext = [A.alloc([4, 4, 129], BF16) for _ in range(2)]
            ktok = A.alloc([4, 256], BF16)
            kc = A.alloc([2, 512], BF16)
            wk = A.alloc([2, 4, 256], BF16)
            dgk = A.alloc([2, 5, 128], BF16)
            print("pass A2 arena end", A.off)
            o_cw, _ = VC["cwk"]
            for m_ in range(2):
                for j_ in range(5):
                    TS("dve", dgk[:, m_, j_, :], ident_f, vcs[:, o_cw + m_ * 5 + j_:o_cw + m_ * 5 + j_ + 1], ALU.mult, CONST,
                       ("dgk",))
            MSET("dve", kpre[:, :, 0:2], 0.0, ("kpre_l",))
            MSET("dve", kpre[:, :, 8194:8196], 0.0, ("kpre_r",))
            for s_ in range(2):
                MSET("pool", vext[s_][:, :, :, 128:129], 1.0, (f"vext{s_}",))
            SQ = [(6, 0), (6, 129), (6, 258), (7, 0)]

            def conv_silu(dst, dkey, src, skeys, cwname, cbname, m, n0, N, scale=None, eng="dve"):
                o, _ = VC[cwname]
                ob, _ = VC[cbname]
                acc = W[0]
                TS(eng, acc[:, 0:N], src[:, m, n0:n0 + N], vcs[:, o + m * 5:o + m * 5 + 1], ALU.mult, tuple(skeys) + CONST,
                   ("W0",))
                for j in range(1, 5):
                    STT(eng, acc[:, 0:N], src[:, m, n0 + j:n0 + j + N], vcs[:, o + m * 5 + j:o + m * 5 + j + 1],
                        acc[:, 0:N], ALU.mult, ALU.add, tuple(skeys) + ("W0",) + CONST, ("W0",))
                ACT(dst, acc[:, 0:N], AF.Silu, ("W0",) + CONST, (dkey,), bias=vcs[:, ob + m:ob + m + 1])
                if scale is not None:
                    TS("pool", dst, dst, scale, ALU.mult, (dkey,), (dkey,))

            def passA2_tile(t):
                slot = t % 2
                rxbc, rxcol, rk, ck = rxbcs[slot], rxcols[slot], f"rxbc{slot}", f"rxcol{slot}"

                def h_km(m):
                    def f(psb, pk):
                        TT("dve", kpre[:, m, 2 + t * 512:2 + (t + 1) * 512], psb[:, 0:512], rxbc, ALU.mult,
                           (pk, rk), (f"kpre{t}",))
                    return f

                def h_v(sub, psb, pk):
                    TS("dve", vext[slot][:, sub, :, 0:128], view(psb[:, 0:512], [4, 128]), rxcol[:, sub:sub + 1], ALU.mult,
                       (pk, ck), (f"vext{slot}",))

                fm = [(256, 128, h_km(0)), (384, 128, h_km(1))]
                tm = [(512, 512, h_v)]
                return fm, tm

            started = set()

            def passA2_lag(t):
                slot = t % 2
                ob_k, _ = VC["cbk"]
                kkeys = (f"kpre{t}", f"kpre{max(t - 1, 0)}", f"kpre{min(t + 1, 15)}", "kpre_l", "kpre_r", "dgk")
                for m in range(2):
                    for j in range(5):
                        MM(PS[5][:, 0:512], dgk[:, m, j, :], kpre[:, m, t * 512 + j:t * 512 + j + 512], kkeys, (PK[5],),
                           start=(j == 0), stop=(j == 4))
                    ACT(kc[:, m, :], PS[5][:, 0:512], AF.Silu, (PK[5],) + CONST, ("kc",), bias=vcs[:, ob_k + m:ob_k + m + 1])
                for m in range(2):
                    for sub in range(4):
                        MM(PS[4][:, sub * 128:(sub + 1) * 128], kc[:, m, sub * 128:(sub + 1) * 128], ident_b,
                           ("kc",) + CONST, (PK[4],))
                    CP("act", ktok[:, :, m * 128:(m + 1) * 128], view(PS[4][:, 0:512], [4, 128]), (PK[4],), ("ktok",))
                for d in range(2):
                    TT("pool" if d else "dve", view(wk[:, d].rearrange("p a b -> p (a b)"), [16, 64]),
                       view(ktok.rearrange("p a b -> p (a b)"), [16, 64]),
                       w_all[:, d * 256 + 16 * t:d * 256 + 16 * t + 16].unsqueeze(2).to_broadcast([128, 16, 64]), ALU.mult,
                       ("ktok", "w_all"), (f"wk{d}",))
                for q in (0, 3, 1, 2):
                    d, hp = q // 2, q % 2
                    bank, col = SQ[q]
                    for half in range(2):
                        h = 2 * hp + half
                        for c in range(4):
                            first = (bank, half) not in started
                            started.add((bank, half))
                            MM(PS[bank][half * 64:half * 64 + 64, col:col + 129], wk[:, d, c, h * 64:(h + 1) * 64],
                               vext[slot][:, c, h, :], (f"wk{d}", f"vext{slot}"), (PK[bank],), start=first,
                               stop=(t == 15 and c == 3), skip=True)

            dgt = A.alloc([512])
            o_gm, _ = VC["gmix"]

            def a2_p1(t):
                DMA(xt_a[0], xa[t], (), ("xt0",), "xt0")
                TT("pool", xb_a[t % 2], xt_a[0], vcs[:, o_gm:o_gm + 8].unsqueeze(2).to_broadcast([128, 8, 512]), ALU.mult,
                   ("xt0",) + CONST, (f"xb{t % 2}",))

            def a2_rx(t):
                sl = t % 2
                for sub in range(4):
                    TS("dve", dgt[:, sub * 128:(sub + 1) * 128], ident_f, rxc_all[:, 4 * t + sub:4 * t + sub + 1], ALU.mult,
                       ("rxc_all",) + CONST, ("dgt",))
                for sub in range(4):
                    MM(PS[0][:, sub * 128:(sub + 1) * 128], ones_f, dgt[:, sub * 128:(sub + 1) * 128], ("dgt",) + CONST,
                       (PK[0],))
                CP("act", rxbcs[sl], PS[0][:, 0:512], (PK[0],), (f"rxbc{sl}",))
                CP("pool", rxcols[sl][:, 0:4], rxc_all[:, 4 * t:4 * t + 4], ("rxc_all",), (f"rxcol{sl}",))

            a2_p1(0)
            a2_rx(0)
            for t in range(16):
                if t + 1 < 16:
                    a2_p1(t + 1)
                fm, tm = passA2_tile(t)
                tile_body(512, xb_a[t % 2], f"xb{t % 2}", fm, tm, psrotA)
                if t + 1 < 16:
                    a2_rx(t + 1)
                if t >= 1:
                    passA2_lag(t - 1)
            passA2_lag(15)
            for q in range(4):
                bank, col = SQ[q]
                CP("dve" if q % 2 else "act", stB[:, q, :], PS[bank][:, col:col + 129], (PK[bank],), ("stB",))
            P.barrier()
            if upto == "A":
                raise _Stop()

            A.reset(R5)
            qkpre = A.alloc([4, 2052], BF16)
            vext_o = A.alloc([16, 4, 129], BF16)
            so = A.alloc([16, 512], BF16)
            G_o = A.alloc([16, 16])
            markB = A.mark()
            xt_b = [A.alloc([8, 512])] * 2
            xb_b = [A.alloc([8, 512], BF16) for _ in range(2)]
            xsq = A.alloc([8, 512], BF16)
            rxbcs = [A.alloc([512]) for _ in range(2)]
            rxcols = [A.alloc([4]) for _ in range(2)]
            W[:] = [A.alloc([512]) for _ in range(4)]
            print("pass B arena end", A.off)
            MSET("pool", vext_o[:, :, :, 128:129], 1.0, ("vext_o",))
            psrotB = [0, [1, 2, 3]]

            def passB_tile(T):
                N = 512 if T < 4 else 4
                slot = T % 2
                rxbc, rxcol, rk, ck = rxbcs[slot], rxcols[slot], f"rxbc{slot}", f"rxcol{slot}"
                if N == 512:
                    lo = 2 + T * 512
                    dsts = [(lo, 0, 512)]
                else:
                    dsts = [(0, 0, 2), (2050, 2, 4)]

                def h_qk(idx):
                    def f(psb, pk):
                        for (d0, s0, s1) in dsts:
                            TT("dve", qkpre[:, idx, d0:d0 + (s1 - s0)], psb[:, s0:s1], rxbc[:, s0:s1], ALU.mult,
                               (pk, rk), ("qkpre",))
                    return f

                def h_cq(m):
                    def f(psb, pk):
                        TT("dve", W[m], psb[:, 0:512], rxbc, ALU.mult, (pk, rk), (f"W{m}",))
                        ACT(W[3].bitcast(BF16)[:, m * 512:(m + 1) * 512], W[m], AF.Square, (f"W{m}",), ("W3",))
                        if m == 1:
                            for mm_ in range(2):
                                MM(PS[4][:, 0:512], ones8, W[3].bitcast(BF16)[:, mm_ * 512:(mm_ + 1) * 512],
                                   ("W3",) + CONST, (PK[4],), start=(mm_ == 0), stop=(mm_ == 1))
                            rstd(W[2], PS[4][:, 0:512], "W2", PK[4])
                            for mm_ in range(2):
                                STT("dve", cqnT[:, mm_, T * 512:(T + 1) * 512], W[mm_], vcol("gcq", mm_), W[2], ALU.mult,
                                    ALU.mult, (f"W{mm_}", "W2") + CONST, ("cqnT",))
                    return f

                def h_v(sub, psb, pk):
                    TS("dve", vext_o[:, 4 * T + sub, :, 0:128], view(psb[:, 0:512], [4, 128]), rxcol[:, sub:sub + 1],
                       ALU.mult, (pk, ck), ("vext_o",))

                def h_o(sub, psb, pk):
                    ACT(so[:, 4 * T + sub, :], psb[:, 0:512], AF.Sigmoid, (pk, ck), ("so",), scale=rxcol[:, sub:sub + 1])

                def h_g(sub, psb, pk):
                    TS("dve", G_o[:, 4 * T + sub, :], psb[:, 0:16], rxcol[:, sub:sub + 1], ALU.mult, (pk, ck),
                       ("G_o",))

                fm = [(0, 128, h_qk(0)), (128, 128, h_qk(1)), (256, 128, h_qk(2)), (384, 128, h_qk(3))]
                tm = []
                if N == 512:
                    fm += [(1552, 128, h_cq(0)), (1680, 128, h_cq(1))]
                    tm = [(512, 512, h_v), (1024, 512, h_o), (1536, 16, h_g)]
                return fm, tm

            def dmaB(T):
                if T < 4:
                    DMA(xt_b[0], xo[T], (), ("xt0",), "xt0")
                else:
                    DMA(xt_b[0][:, :, 0:4], xh[:, :, :], (), ("xt0",), "xt0")

            sweep(5, [512] * 4 + [4], xt_b[0], xb_b, xsq, rxbcs, rxcols, dmaB, passB_tile, None, psrotB)
            P.barrier()
            if upto == "B":
                raise _Stop()

            A.reset(markB)
            qc = A.alloc([2, 2048], BF16)
            kcO = A.alloc([2, 2048], BF16)
            ktok_o = A.alloc([16, 256], BF16)
            hsum = A.alloc([16, 2, 128])
            W[:] = [None, None, None, None]
            dgc = A.alloc([4, 5, 128], BF16)
            tq = A.alloc([8])
            GA = {n: A.alloc([128]) for n in ("li", "fp", "bn", "an", "u", "cmT", "umb", "g", "M17", "M17b", "Mt", "al",
                                              "be", "flo", "wi", "ws", "dec", "t", "mp", "mn")}
            umaxc = A.alloc([1])
            Ab = [A.alloc([128], BF16) for _ in range(4)]
            wsk = [A.alloc([64], BF16) for _ in range(4)]
            stf = A.alloc([2, 129])
            stb = A.alloc([2, 129], BF16)
            cmbt_all = A.alloc([4, 129])
            cmbt = [cmbt_all[:, u_, :] for u_ in range(4)]
            dn_all = A.alloc([3, 4])
            sqw = A.alloc([1024])
            ssq8 = A.alloc([8])
            print("phase M arena end", A.off)
            A.reset(R1)
            yT = A.alloc([8, 2048], BF16)

            for idx in range(4):
                o_c, _ = VC["cwq" if idx < 2 else "cwk"]
                for j_ in range(5):
                    TS("dve", dgc[:, idx, j_, :], ident_f, vcs[:, o_c + (idx % 2) * 5 + j_:o_c + (idx % 2) * 5 + j_ + 1],
                       ALU.mult, CONST, ("dgc",))
            ob_q, _ = VC["cbq"]
            ob_k2, _ = VC["cbk"]
            for piece in range(4):
                n0 = piece * 512
                for idx in range(4):
                    bk = (0, 1, 2, 3)[idx]
                    m = idx % 2
                    for j_ in range(5):
                        MM(PS[bk][:, 0:512], dgc[:, idx, j_, :], qkpre[:, idx, n0 + j_:n0 + j_ + 512], ("qkpre", "dgc"),
                           (PK[bk],), start=(j_ == 0), stop=(j_ == 4))
                    if idx < 2:
                        ACT(qc[:, m, n0:n0 + 512], PS[bk][:, 0:512], AF.Silu, (PK[bk],) + CONST, ("qc",),
                            bias=vcs[:, ob_q + m:ob_q + m + 1])
                        TS("pool", qc[:, m, n0:n0 + 512], qc[:, m, n0:n0 + 512], 0.125, ALU.mult, ("qc",), ("qc",))
                    else:
                        ACT(kcO[:, m, n0:n0 + 512], PS[bk][:, 0:512], AF.Silu, (PK[bk],) + CONST, ("kcO",),
                            bias=vcs[:, ob_k2 + m:ob_k2 + m + 1])
            for cc in range(4):
                for m in range(2):
                    for k4 in range(4):
                        c = cc * 4 + k4
                        MM(PS[4][:, k4 * 128:(k4 + 1) * 128], kcO[:, m, c * 128:(c + 1) * 128], ident_b, ("kcO",) + CONST,
                           (PK[4],))
                    CP("act", ktok_o[:, cc * 4:cc * 4 + 4, m * 128:(m + 1) * 128], view(PS[4][:, 0:512], [4, 128]),
                       (PK[4],), ("ktok_o",))

            def dch16(ap):
                return ap.rearrange("p (d c h) -> p c d h", d=2, c=16)

            def v3(ap):
                return view(ap, [2, 16, 4])

            Gv = view(G_o.rearrange("p a b -> p (a b)"), [16, 2, 2, 4])
            bgv = view(vrow("bg"), [2, 2, 4])
            for two, nm in ((0, "li"), (1, "fp")):
                TT("dve", dch16(GA[nm]), Gv[:, :, :, two, :], bgv[:, :, two, :].unsqueeze(1).to_broadcast([128, 16, 2, 4]),
                   ALU.add, ("G_o",) + CONST, ("g_" + nm,))
            ACT(GA["fp"], GA["fp"], AF.Exp, ("g_fp",), ("g_fp",), scale=-1.0)
            ACT(GA["fp"], GA["fp"], AF.Ln, ("g_fp",), ("g_fp",), bias=1.0)
            MM(PS[5][:, 0:64], tri_le, GA["fp"][:, 0:64], ("g_fp",) + CONST, (PK[5],))
            MM(PS[5][:, 64:128], tri_ge, GA["fp"][:, 64:128], ("g_fp",) + CONST, (PK[5],))
            MM(PS[5][:, 128:256], ones_f, GA["fp"], ("g_fp",) + CONST, (PK[5],))
            CP("dve", GA["bn"], PS[5][:, 0:128], (PK[5],), ("g_bn",))
            CP("dve", GA["an"], PS[5][:, 128:256], (PK[5],), ("g_an",))
            TT("dve", GA["u"], GA["li"], GA["bn"], ALU.add, ("g_li", "g_bn"), ("g_u",))
            MM(PS[5][:, 256:384], GA["u"], ident_f, ("g_u",) + CONST, (PK[5],))
            CP("dve", GA["cmT"], PS[5][:, 256:384], (PK[5],), ("g_cmT",))
            k_ = 1
            src_, dst_ = "cmT", "t"
            while k_ < 128:
                S_, D_ = GA[src_], GA[dst_]
                ks, kd = "g_" + src_, "g_" + dst_
                TT("dve", D_[0:64, k_:128], S_[0:64, k_:128], S_[0:64, 0:128 - k_], ALU.max, (ks,), (kd,))
                CP("pool", D_[0:64, 0:k_], S_[0:64, 0:k_], (ks,), (kd,))
                TT("dve", D_[64:128, 0:128 - k_], S_[64:128, 0:128 - k_], S_[64:128, k_:128], ALU.max, (ks,), (kd,))
                CP("pool", D_[64:128, 128 - k_:128], S_[64:128, 128 - k_:128], (ks,), (kd,))
                src_, dst_ = dst_, src_
                k_ *= 2
            if src_ != "cmT":
                CP("dve", GA["cmT"], GA[src_], ("g_" + src_,), ("g_cmT",))
            CP("dve", umaxc[0:64, :], GA["cmT"][0:64, 127:128], ("g_cmT",), ("g_umc",))
            CP("dve", umaxc[64:128, :], GA["cmT"][64:128, 0:1], ("g_cmT",), ("g_umc",))
            TS("dve", GA["t"], ident_f, umaxc[:, 0:1], ALU.mult, ("g_umc",) + CONST, ("g_t",))
            MM(PS[5][:, 384:512], ones_f, GA["t"], ("g_t",) + CONST, (PK[5],))
            CP("dve", GA["umb"], PS[5][:, 384:512], (PK[5],), ("g_umb",))
            TT("dve", GA["g"], GA["umb"], GA["an"], ALU.subtract, ("g_umb", "g_an"), ("g_g",))
            MM(PS[5][:, 0:128], GA["cmT"], ident_f, ("g_cmT",) + CONST, (PK[5],))
            m17 = GA["M17"]
            m17b = GA["M17b"]
            mf = view(m17[:, 0:68], [17, 4])
            mbw = view(m17b[:, 0:68], [17, 4])
            anv = v3(GA["an"])
            gv_ = v3(GA["g"])
            CP("dve", mf[:, 0, :], mB[:, 0:4], ("mB",), ("g_m17",))
            CP("dve", mbw[:, 16, :], mB[:, 4:8], ("mB",), ("g_m17",))
            for i in range(16):
                TT("dve", tq[:, 0:4], mf[:, i, :], anv[:, 0, i, :], ALU.subtract, ("g_m17", "g_an"), ("tq",))
                TT("dve", mf[:, i + 1, :], tq[:, 0:4], gv_[:, 0, i, :], ALU.max, ("tq", "g_g"), ("g_m17",))
                c = 15 - i
                TT("dve", tq[:, 4:8], mbw[:, c + 1, :], anv[:, 1, c, :], ALU.subtract, ("g_m17", "g_an"), ("tq",))
                TT("dve", mbw[:, c, :], tq[:, 4:8], gv_[:, 1, c, :], ALU.max, ("tq", "g_g"), ("g_m17",))
            mprev = GA["mp"]
            mnext = GA["mn"]
            CP("dve", v3(mprev)[:, 0], mf[:, 0:16, :], ("g_m17",), ("g_mp",))
            CP("dve", v3(mprev)[:, 1], mbw[:, 1:17, :], ("g_m17",), ("g_mp",))
            CP("dve", v3(mnext)[:, 0], mf[:, 1:17, :], ("g_m17",), ("g_mn",))
            CP("dve", v3(mnext)[:, 1], mbw[:, 0:16, :], ("g_m17",), ("g_mn",))
            TT("dve", GA["Mt"], PS[5][:, 0:128], mprev, ALU.max, (PK[5], "g_mp"), ("g_Mt",))

            def expdiff(dst, dkey, a, akey, b, bkey):
                TT("dve", dst, a, b, ALU.subtract, (akey, bkey), (dkey,))
                ACT(dst, dst, AF.Exp, (dkey,), (dkey,))

            expdiff(GA["al"], "g_al", GA["umb"], "g_umb", GA["Mt"], "g_Mt")
            expdiff(GA["be"], "g_be", mprev, "g_mp", GA["Mt"], "g_Mt")
            expdiff(GA["flo"], "g_flo", GA["bn"], "g_bn", GA["Mt"], "g_Mt")
            expdiff(GA["wi"], "g_wi", GA["u"], "g_u", GA["umb"], "g_umb")
            TT("dve", GA["ws"], GA["u"], GA["an"], ALU.subtract, ("g_u", "g_an"), ("g_ws",))
            expdiff(GA["ws"], "g_ws", GA["ws"], "g_ws", mnext, "g_mn")
            TT("dve", GA["dec"], mprev, GA["an"], ALU.subtract, ("g_mp", "g_an"), ("g_dec",))
            expdiff(GA["dec"], "g_dec", GA["dec"], "g_dec", mnext, "g_mn")
            GK = ("g_al", "g_be", "g_flo", "g_wi", "g_ws", "g_dec")

            for g in range(2):
                for d in range(2):
                    CP("dve", stf[:, d, :], stB[:, d * 2 + g, :], ("stB",), (f"stf{d}_0", f"stf{d}_1"))
                    CP("act", stb[:, d, :], stf[:, d, :], (f"stf{d}_0", f"stf{d}_1"), (f"stb{d}_0", f"stb{d}_1"))
                QKB = {(0, 0): (1, 0), (0, 1): (7, 0), (1, 0): (0, 0), (1, 1): (6, 300)}
                for i in range(16):
                    U = []
                    for d in range(2):
                        for half in range(2):
                            c = i if d == 0 else 15 - i
                            h = 2 * g + half
                            U.append(dict(d=d, half=half, c=c, h=h, r=d * 64 + c * 4 + h, cols=slice(c * 128, (c + 1) * 128),
                                          rows=slice(half * 64, half * 64 + 64), u=d * 2 + half,
                                          mask=tri_le_b if d == 0 else tri_ge_b, bI=2 + 2 * d, bJ=3 + 2 * d))
                    for x in U:
                        qb, qc0 = QKB[(x["d"], x["half"])]
                        x["qk"] = PS[qb][:, qc0:qc0 + 128]
                        x["qkk"] = PK[qb]
                        MM(x["qk"], kcO[x["rows"], g, x["cols"]], qc[x["rows"], g, x["cols"]], ("kcO", "qc"), (x["qkk"],))
                    for x in U:
                        u, r = x["u"], x["r"]
                        STT("dve", Ab[u], x["qk"], GA["wi"][:, r:r + 1], x["mask"], ALU.mult, ALU.mult,
                            (x["qkk"], "g_wi") + CONST, (f"Ab{u}",))
                        TS("pool", wsk[u], ktok_o[:, x["c"], x["h"] * 64:(x["h"] + 1) * 64], GA["ws"][:, r:r + 1], ALU.mult,
                           ("ktok_o", "g_ws"), (f"wsk{u}",))
                    for x in U:
                        u, d, half, c, h = x["u"], x["d"], x["half"], x["c"], x["h"]
                        bI, bJ = x["bI"], x["bJ"]
                        x["pI"] = PS[bI][:, half * 129:half * 129 + 129]
                        x["kI"] = PK[bI]
                        x["pJ"] = PS[bI][:, 258:387] if half == 0 else PS[bJ][:, 0:129]
                        x["kJ"] = PK[bI] if half == 0 else PK[bJ]
                        MM(x["pI"], Ab[u], vext_o[:, c, h, :], (f"Ab{u}", "vext_o"), (x["kI"],))
                        MM(x["pJ"], qc[x["rows"], g, x["cols"]], stb[x["rows"], d, :], ("qc", f"stb{d}_{half}"), (x["kJ"],))
                        MM(PS[6][x["rows"], d * 129:d * 129 + 129], wsk[u], vext_o[:, c, h, :], (f"wsk{u}", "vext_o"),
                           (PK[6],))
                    for x in U:
                        u, r = x["u"], x["r"]
                        P.op("act", (lambda o_, i_, sc_: (lambda e: e.activation(out=o_, in_=i_, func=AF.Copy, scale=sc_)))(
                            cmbt[u], x["pJ"], GA["be"][:, r:r + 1]), (x["kJ"], "g_be"), (f"cmbt{u}",))
                    for x in U:
                        u, r = x["u"], x["r"]
                        STT("dve", cmbt[u], x["pI"], GA["al"][:, r:r + 1], cmbt[u], ALU.mult, ALU.add,
                            (x["kI"], "g_al", f"cmbt{u}"), (f"cmbt{u}",))
                    for x in U:
                        u, d, half, rows, r = x["u"], x["d"], x["half"], x["rows"], x["r"]
                        STT("dve", stf[rows, d, :], stf[rows, d, :], GA["dec"][rows, r:r + 1],
                            PS[6][rows, d * 129:d * 129 + 129], ALU.mult, ALU.add, (f"stf{d}_{half}", "g_dec", PK[6]),
                            (f"stf{d}_{half}",))
                        CP("act", stb[rows, d, :], stf[rows, d, :], (f"stf{d}_{half}",), (f"stb{d}_{half}",))
                    ck_all = tuple(f"cmbt{u_}" for u_ in range(4))
                    den_v = cmbt_all[:, :, 128]
                    STT("dve", dn_all[:, 0, :], den_v, -1.0, den_v, ALU.mult, ALU.max, ck_all, ("dn",))
                    for d in range(2):
                        r0 = U[2 * d]["r"]
                        TT("dve", dn_all[:, 1, 2 * d:2 * d + 2], dn_all[:, 0, 2 * d:2 * d + 2], GA["flo"][:, r0:r0 + 2],
                           ALU.max, ("dn", "g_flo"), ("dn",))
                    RECIP(dn_all[:, 2, :], dn_all[:, 1, :], ("dn",), ("dn",))
                    for x in U:
                        u, d, half, c = x["u"], x["d"], x["half"], x["c"]
                        first = (d == 0 and c <= 7) or (d == 1 and c >= 8)
                        if first:
                            TS("dve", hsum[:, c, half, :], cmbt[u][:, 0:128], dn_all[:, 2, u:u + 1], ALU.mult,
                               (f"cmbt{u}", "dn"), (f"hsum{c}_{half}",))
                        else:
                            STT("dve", hsum[:, c, half, :], cmbt[u][:, 0:128], dn_all[:, 2, u:u + 1], hsum[:, c, half, :],
                                ALU.mult, ALU.add, (f"cmbt{u}", "dn", f"hsum{c}_{half}"), (f"hsum{c}_{half}",))
                for cc in range(4):
                    hv = hsum[:, 4 * cc:4 * cc + 4].rearrange("p a b c -> p (a b c)")
                    hkeys = tuple(f"hsum{4 * cc + k4}_{half}" for k4 in range(4) for half in range(2))
                    TT("pool", sqw, hv, hv, ALU.mult, hkeys, ("sqw",))
                    RED("dve", ssq8, view(sqw, [8, 128]), ALU.add, ("sqw",), ("ssq8",))
                    TS("dve", ssq8, ssq8, 1.0 / 128.0, ALU.mult, ("ssq8",), ("ssq8",), s2=EPS, op1=ALU.add)
                    ACT(ssq8, ssq8, AF.Sqrt, ("ssq8",), ("ssq8",))
                    RECIP(ssq8, ssq8, ("ssq8",), ("ssq8",))
                    TT("dve", view(sqw, [8, 128]), view(hv, [8, 128]), ssq8.unsqueeze(2).to_broadcast([128, 8, 128]),
                       ALU.mult, hkeys + ("ssq8", "sqw"), ("sqw",))
                    gov = view(vrow("gout")[:, 2 * g * 128:(2 * g + 2) * 128], [2, 128]).unsqueeze(1).to_broadcast(
                        [128, 4, 2, 128])
                    TT("pool", view(sqw, [4, 2, 128]), view(sqw, [4, 2, 128]), gov, ALU.mult, ("sqw",) + CONST, ("sqw",))
                    sov = so[:, 4 * cc:4 * cc + 4, 2 * g * 128:(2 * g + 2) * 128]
                    TT("dve", sov, view(sqw, [4, 256]), sov, ALU.mult, ("sqw", "so"), ("so",))
                    for half in range(2):
                        h = 2 * g + half
                        for k4 in range(4):
                            c = 4 * cc + k4
                            MM(PS[7][:, k4 * 128:(k4 + 1) * 128], so[:, c, h * 128:(h + 1) * 128], ident_b, ("so",) + CONST,
                               (PK[7],))
                        CP("act", yT[:, h, cc * 512:(cc + 1) * 512], PS[7][:, 0:512], (PK[7],), ("yT",))
            P.barrier()
            if upto == "M":
                raise _Stop()

            A.reset(R5)
            QTn = A.alloc([4, 2048], BF16)
            QTr = A.alloc([4, 2048], BF16)
            KTn = A.alloc([8192], BF16)
            KTr = A.alloc([8192], BF16)
            Vh = A.alloc([64, 129], BF16)
            PT = [A.alloc([512], BF16) for _ in range(3)]
            cs_o = A.alloc([2048])
            racc = A.alloc([512])
            rinv = A.alloc([512])
            W[:] = [A.alloc([512]) for _ in range(4)]
            A_tmp["pi"] = W[2].bitcast(I32)
            A_tmp["kf"] = W[1]
            print("phase T arena end", A.off)
            MSET("pool", Vh[:, :, 128:129], 1.0, ("Vh",))
            MSET("pool", QTr[64:128, :, :], 0.0, ("QTr",))
            MSET("pool", KTr[64:128, :], 0.0, ("KTr",))
            SC = float(0.75 * 192.0 ** -0.5)
            EPSQ = 0.75 * EPS

            for T in range(4):
                tok = slice(T * 512, (T + 1) * 512)
                rope_table(cs_o[:, tok], "cs_o", poso[0:1, T * 512:(T + 1) * 512], 512)
                for hp_ in range(2):
                    units = []
                    for e in range(2):
                        units.append(dict(h=2 * hp_ + e, wq=W[(0, 1)[e]].bitcast(BF16), wqk=("W0", "W1")[e], wr=W[(3, 2)[e]],
                                          wrk=("W3", "W2")[e], bn=(1, 5)[e], br=(2, 6)[e], bs=(3, 7)[e], bf=(4, 0)[e]))
                    for x in units:
                        h = x["h"]
                        for m in range(2):
                            MM(PS[x["bn"]][:, 0:512], Wuqb[:, m, h * 192:h * 192 + 128], cqnT[:, m, tok], ("Wuq", "cqnT"),
                               (PK[x["bn"]],), start=(m == 0), stop=(m == 1))
                        for m in range(2):
                            MM(PS[x["br"]][:, 0:512], Wqr[:, m, h, :], cqnT[:, m, tok], ("Wqr", "cqnT"), (PK[x["br"]],),
                               start=(m == 0), stop=(m == 1))
                    for x in units:
                        ACT(x["wq"][:, 0:512], PS[x["bn"]][:, 0:512], AF.Square, (PK[x["bn"]],), (x["wqk"],))
                        ACT(x["wq"][:, 512:1024], PS[x["br"]][:, 0:512], AF.Square, (PK[x["br"]],), (x["wqk"],))
                    for x in units:
                        MM(PS[x["bs"]][:, 0:512], ones8, x["wq"][:, 0:512], (x["wqk"],) + CONST, (PK[x["bs"]],), start=True,
                           stop=False)
                        MM(PS[x["bs"]][:, 0:512], sel_lo, x["wq"][:, 512:1024], (x["wqk"],) + CONST, (PK[x["bs"]],),
                           start=False, stop=True)
                    for x in units:
                        TS("dve", x["wr"], PS[x["bs"]][:, 0:512], float(EPSQ), ALU.add, (PK[x["bs"]],), (x["wrk"],))
                    for x in units:
                        ACT(x["wr"], x["wr"], AF.Ln, (x["wrk"],), (x["wrk"],))
                    for x in units:
                        ACT(x["wr"], x["wr"], AF.Exp, (x["wrk"],), (x["wrk"],), scale=-0.5)
                    for x in units:
                        STT("dve", QTn[:, x["h"], tok], PS[x["bn"]][:, 0:512], vcol("gqn"), x["wr"], ALU.mult, ALU.mult,
                            (PK[x["bn"]], x["wrk"]) + CONST, ("QTn",))
                        STT("dve", x["wq"][:, 0:512], PS[x["br"]][:, 0:512], ropef[:, 1:2], cs_o[:, tok], ALU.mult, ALU.mult,
                            (PK[x["br"]], "cs_o", x["wqk"]) + CONST, (x["wqk"],))
                    for x in units:
                        MM(PS[x["bf"]][0:64, 0:512], fold_b, x["wq"][:, 0:512], (x["wqk"],) + CONST, (PK[x["bf"]],))
                    for x in units:
                        TT("dve", QTr[0:64, x["h"], tok], PS[x["bf"]][0:64, 0:512], x["wr"][0:64, :], ALU.mult,
                           (PK[x["bf"]], x["wrk"]), ("QTr",))

            for h in range(4):
                for tp_ in range(8):
                    units = []
                    for e in range(2):
                        t = 2 * tp_ + e
                        units.append(dict(t=t, tok=slice(t * 512, (t + 1) * 512), bk=(1, 4)[e], bs=(3, 5)[e], bv=(2, 6)[e],
                                          wq=W[(0, 1)[e]], wqk=("W0", "W1")[e], wr=W[(3, 2)[e]], wrk=("W3", "W2")[e]))
                    for x in units:
                        MM(PS[x["bk"]][:, 0:512], Wukvb[:, h * 256:h * 256 + 128], ckvT[:, x["tok"]],
                           ("Wukv", f"ckvT{x['t']}"), (PK[x["bk"]],))
                    for x in units:
                        ACT(x["wq"].bitcast(BF16)[:, 0:512], PS[x["bk"]][:, 0:512], AF.Square, (PK[x["bk"]],), (x["wqk"],))
                    for x in units:
                        MM(PS[x["bs"]][:, 0:512], ones8, x["wq"].bitcast(BF16)[:, 0:512], (x["wqk"],) + CONST,
                           (PK[x["bs"]],), start=True, stop=False)
                        MM(PS[x["bs"]][:, 0:512], sel_hi, RT[:, x["tok"]], (f"RTa{x['t']}", f"RTb{x['t']}") + CONST,
                           (PK[x["bs"]],), start=False, stop=True)
                    for x in units:
                        TS("dve", x["wr"], PS[x["bs"]][:, 0:512], float(EPSQ), ALU.add, (PK[x["bs"]],), (x["wrk"],))
                    for x in units:
                        ACT(x["wr"], x["wr"], AF.Ln, (x["wrk"],), (x["wrk"],))
                    for x in units:
                        ACT(x["wr"], x["wr"], AF.Exp, (x["wrk"],), (x["wrk"],), scale=-0.5)
                    for x in units:
                        STT("dve", KTn[:, x["tok"]], PS[x["bk"]][:, 0:512], vcol("gkn"), x["wr"], ALU.mult, ALU.mult,
                            (PK[x["bk"]], x["wrk"]) + CONST, ("KTn",))
                        TT("pool", KTr[0:64, x["tok"]], RT[0:64, x["tok"]], x["wr"][0:64, :], ALU.mult,
                           (f"RTa{x['t']}", x["wrk"]), ("KTr",))
                    for x in units:
                        t = x["t"]
                        for sub in range(4):
                            MM(PS[x["bv"]][:, sub * 128:(sub + 1) * 128],
                               ckvT[:, t * 512 + sub * 128:t * 512 + (sub + 1) * 128],
                               Wukvb[:, h * 256 + 128:h * 256 + 256], ("Wukv", f"ckvT{t}"), (PK[x["bv"]],))
                    for x in units:
                        t = x["t"]
                        CP("act", Vh[:, 4 * t:4 * t + 4, 0:128], view(PS[x["bv"]][:, 0:512], [4, 128]), (PK[x["bv"]],),
                           ("Vh",))
                SB = [0, 1, 7]
                for qT in range(4):
                    qtok = slice(qT * 512, (qT + 1) * 512)
                    ob = 2 + (h * 4 + qT) % 2

                    def s_mm(kb):
                        sb = SB[kb % 3]
                        kcols = slice(kb * 128, (kb + 1) * 128)
                        MM(PS[sb][:, 0:512], KTn[:, kcols], QTn[:, h, qtok], ("KTn", "QTn"), (PK[sb],), start=True,
                           stop=False)
                        MM(PS[sb][:, 0:512], KTr[:, kcols], QTr[:, h, qtok], ("KTr", "QTr"), (PK[sb],), start=False,
                           stop=True)
                        ACT(PT[kb % 3], PS[sb][:, 0:512], AF.Exp, (PK[sb],), (f"PT{kb % 3}",), scale=SC)

                    s_mm(0)
                    s_mm(1)
                    for kb in range(64):
                        if kb + 2 < 64:
                            s_mm(kb + 2)
                        MM(PS[ob][:, 0:512], Vh[:, kb, 0:128], PT[kb % 3], (f"PT{kb % 3}", "Vh"), (PK[ob],),
                           start=(kb == 0), stop=(kb == 63))
                        if kb == 0:
                            CP("dve", racc, PT[kb % 3], (f"PT{kb % 3}",), ("racc",))
                        else:
                            TT("dve", racc, racc, PT[kb % 3], ALU.add, (f"PT{kb % 3}", "racc"), ("racc",))
                    MM(PS[6][:, 0:512], ones_f, racc, ("racc",) + CONST, (PK[6],))
                    RECIP(rinv, PS[6][:, 0:512], (PK[6],), ("rinv",))
                    TT("dve", yT[:, 4 + h, qtok], PS[ob][:, 0:512], rinv, ALU.mult, (PK[ob], "rinv"), ("yT",))
            if dbg:
                DMA(dbg_d["yT"][:, :, :], yT, ("yT",), ("dbg_yT",), "dbg")
            P.barrier()
            if upto == "T":
                raise _Stop()

            A.reset(R2)
            Woutb = A.alloc([8, 1024], BF16)
            W2b = A.alloc([32, 1024], BF16)
            W1t = [A.alloc([8, 512], BF16) for _ in range(2)]
            x1s = [A.alloc([2, 1024]) for _ in range(2)]
            x1bs = [A.alloc([2, 1024], BF16) for _ in range(2)]
            x1gTs = [A.alloc([8, 256], BF16) for _ in range(2)]
            aT = A.alloc([32, 256], BF16)
            xrt = [A.alloc([1024]) for _ in range(2)]
            ost = [A.alloc([1024]) for _ in range(2)]
            rrs = [A.alloc([4]) for _ in range(2)]
            rtmp = [A.alloc([256], BF16) for _ in range(2)]
            junk = A.alloc([1024], BF16)
            print("phase F arena end", A.off)
            DMA(Woutb, w_out.rearrange("(c p) n -> p c n", p=128), (), ("Woutb",), "wf", queue="pool")
            w2_v = w_ff2.rearrange("(c p) n -> p c n", p=128)
            for q4 in range(4):
                DMA(W2b[:, q4 * 8:(q4 + 1) * 8, :], w2_v[:, q4 * 8:(q4 + 1) * 8, :], (), ("W2b",), "wf2", queue="pool")
            o_g, _ = VC["gffn"]

            def f_outproj(ft):
                sl = ft % 2
                for s2 in range(2):
                    c = 2 * ft + s2
                    DMA(xrt[s2], xr[c], (), (f"xrt{s2}",), f"xrt{s2}")
                    for half in range(2):
                        for mc in range(8):
                            MM(PS[half][:, 0:512], yT[:, mc, c * 128:(c + 1) * 128],
                               Woutb[:, mc, half * 512:(half + 1) * 512], ("yT", "Woutb"), (PK[half],), start=(mc == 0),
                               stop=(mc == 7))
                        TT("dve", x1s[sl][:, s2, half * 512:(half + 1) * 512], PS[half][:, 0:512],
                           xrt[s2][:, half * 512:(half + 1) * 512], ALU.add, (PK[half], f"xrt{s2}"), (f"x1_{sl}_{s2}",))
                    ACT(junk, x1s[sl][:, s2, :], AF.Square, (f"x1_{sl}_{s2}",), ("junk", f"rr{sl}"),
                        accum=rrs[sl][:, s2:s2 + 1])
                    TS("dve", rrs[sl][:, s2:s2 + 1], rrs[sl][:, s2:s2 + 1], 1.0 / 1024.0, ALU.mult, (f"rr{sl}",),
                       (f"rr{sl}",), s2=EPS, op1=ALU.add)
                    RECIP(rrs[sl][:, s2:s2 + 1], rrs[sl][:, s2:s2 + 1], (f"rr{sl}",), (f"rr{sl}",))
                    CP("pool", x1bs[sl][:, s2, :], x1s[sl][:, s2, :], (f"x1_{sl}_{s2}",), (f"x1b{sl}",))

            def f_transposes(ft):
                sl = ft % 2
                for s2 in range(2):
                    for dcg in range(2):
                        for k4 in range(4):
                            dc = 4 * dcg + k4
                            MM(PS[2][:, k4 * 128:(k4 + 1) * 128], x1bs[sl][:, s2, dc * 128:(dc + 1) * 128], ident_b,
                               (f"x1b{sl}",) + CONST, (PK[2],))
                        TT("dve", x1gTs[sl][:, 4 * dcg:4 * dcg + 4, s2 * 128:(s2 + 1) * 128],
                           view(PS[2][:, 0:512], [4, 128]),
                           vcs[:, o_g + 4 * dcg:o_g + 4 * dcg + 4].unsqueeze(2).to_broadcast([128, 4, 128]), ALU.mult,
                           (PK[2],) + CONST, (f"x1gT{sl}",))

            def f_phase1(ft):
                sl_ = ft % 2
                for gI in range(8):
                    sl = gI % 2
                    DMA(W1t[sl], w1s[gI], ("w1s",), (f"W1t{sl}",), f"W1t{sl}")
                    for f4 in range(4):
                        f = 4 * gI + f4
                        bk = 2 + f % 2
                        for dc in range(8):
                            MM(PS[bk][:, 0:256], W1t[sl][:, dc, f4 * 128:(f4 + 1) * 128], x1gTs[sl_][:, dc, :],
                               (f"W1t{sl}", f"x1gT{sl_}"), (PK[bk],), start=(dc == 0), stop=(dc == 7))
                        ACT(rtmp[f % 2], PS[bk][:, 0:256], AF.Relu, (PK[bk],), (f"rtmp{f % 2}",))
                        TT("pool", aT[:, f, :], rtmp[f % 2], rtmp[f % 2], ALU.mult, (f"rtmp{f % 2}",), ("aT",))

            def f_phase2(ft):
                sl = ft % 2
                for s2 in range(2):
                    c = 2 * ft + s2
                    for half in range(2):
                        bk = 4 + s2 * 2 + half
                        for f in range(32):
                            MM(PS[bk][:, 0:512], aT[:, f, s2 * 128:(s2 + 1) * 128], W2b[:, f, half * 512:(half + 1) * 512],
                               ("aT", "W2b"), (PK[bk],), start=(f == 0), stop=(f == 31))
                        STT("dve", ost[s2][:, half * 512:(half + 1) * 512], PS[bk][:, 0:512], rrs[sl][:, s2:s2 + 1],
                            x1s[sl][:, s2, half * 512:(half + 1) * 512], ALU.mult, ALU.add,
                            (PK[bk], f"rr{sl}", f"x1_{sl}_{s2}"), (f"ost{s2}",))
                    DMA(out_d[c], ost[s2], (f"ost{s2}",), (f"out{c}",), f"ost{s2}")

            f_outproj(0)
            f_transposes(0)
            for ft in range(8):
                f_phase1(ft)
                if ft + 1 < 8:
                    f_outproj(ft + 1)
                f_phase2(ft)
                if ft + 1 < 8:
                    f_transposes(ft + 1)
        except _Stop:
            pass

        with nc.Block() as block:
            P.emit(block, sems, dma_sems)
    return nc


_NC_CACHE = {}


def kernel(**inputs):
    maps = make_in_maps(inputs)
    if "nc" not in _NC_CACHE:
        _NC_CACHE["nc"] = build()
    nc = _NC_CACHE["nc"]
    res = run_bass_kernel_spmd(nc, maps, core_ids=list(range(8)))
    out = np.zeros((2, 8192, 1024), np.float32)
    for c in range(8):
        b, j = c // 4, c % 4
        out[b, 2048 * j:2048 * (j + 1), :] = np.asarray(res.results[c]["out"], np.float32).reshape(2048, 1024)
    return out
```

```python
import numpy as np
import ml_dtypes
import concourse.bass as bass
import concourse.mybir as mybir
from concourse.bass_utils import run_bass_kernel_spmd

F32 = mybir.dt.float32
BF16 = mybir.dt.bfloat16
I32 = mybir.dt.int32
ALU = mybir.AluOpType
AF = mybir.ActivationFunctionType
AX = mybir.AxisListType

EPS = 1e-6
NEG = -30000.0
TWO_PI = float(2.0 * np.pi)


class Prog:
    ENGS = ("pe", "act", "dve", "pool", "sp")

    def __init__(self, nc):
        self.nc = nc
        self.ops = []
        self.last_w = {}
        self.readers = {}
        self.dma_count = {}
        self.last_on_eng = {}
        self.dma_last = {}
        self.barrier_deps = None

    def _add(self, eng, fn, reads, writes, dma_key=None):
        writes = tuple(writes) + tuple(r for r in reads if r.startswith("ps") and r not in writes)
        oid = len(self.ops)
        deps = set()
        for r in reads:
            if r in self.last_w:
                deps.add(self.last_w[r])
        for w in writes:
            if w in self.last_w:
                deps.add(self.last_w[w])
            for rd in self.readers.get(w, ()):
                deps.add(rd)
        if self.barrier_deps is not None and eng not in self.barrier_deps[1]:
            deps |= self.barrier_deps[0]
            self.barrier_deps[1].add(eng)
        deps.discard(oid)
        op = dict(id=oid, eng=eng, fn=fn, deps=deps, dma_key=dma_key, signal=False)
        if dma_key is not None:
            self.dma_count[dma_key] = self.dma_count.get(dma_key, 0) + 16
            op["dma_val"] = self.dma_count[dma_key]
            self.dma_last[dma_key] = oid
        self.ops.append(op)
        for w in writes:
            self.last_w[w] = oid
            self.readers[w] = []
        for r in reads:
            self.readers.setdefault(r, []).append(oid)
        self.last_on_eng[eng if dma_key is None else ("dma", dma_key)] = oid
        return oid

    def op(self, eng, fn, reads=(), writes=()):
        return self._add(eng, fn, tuple(reads), tuple(writes))

    def dma(self, fn, reads=(), writes=(), key=None, queue="sp"):
        return self._add(queue, fn, tuple(reads), tuple(writes), dma_key=key)

    def barrier(self):
        deps = set()
        for k, oid in self.last_on_eng.items():
            deps.add(oid)
        self.barrier_deps = (deps, set())

    def emit(self, block, sems, dma_sems):
        ops = self.ops
        for o in ops:
            for d in o["deps"]:
                ops[d]["signal"] = True
        cnt = {e: 0 for e in self.ENGS}
        for o in ops:
            if o["dma_key"] is None and o["signal"]:
                cnt[o["eng"]] += 1
                o["sig_val"] = cnt[o["eng"]]
        by_eng = {e: [o for o in ops if o["eng"] == e] for e in self.ENGS}

        def run(eng_name, e):
            waited = {}
            for o in by_eng[eng_name]:
                need = {}
                for d in o["deps"]:
                    od = ops[d]
                    if od["dma_key"] is not None:
                        k = ("dma", od["dma_key"])
                        need[k] = max(need.get(k, 0), od["dma_val"])
                    else:
                        if od["eng"] == "pe" and eng_name == "pe":
                            continue
                        k = ("eng", od["eng"])
                        need[k] = max(need.get(k, 0), od["sig_val"])
                for k, v in need.items():
                    if waited.get(k, 0) >= v:
                        continue
                    waited[k] = v
                    if k[0] == "dma":
                        e.wait_ge(dma_sems[k[1]], v)
                    else:
                        e.wait_ge(sems[k[1]], v)
                ins = o["fn"](e)
                if o["dma_key"] is not None:
                    ins.then_inc(dma_sems[o["dma_key"]], 16)
                elif o["signal"]:
                    ins.then_inc(sems[o["eng"]], 1)
            if eng_name == "sp":
                for k, v in self.dma_count.items():
                    e.wait_ge(dma_sems[k], v)

        @block.tensor
        def _(e):
            run("pe", e)

        @block.scalar
        def _(e):
            run("act", e)

        @block.vector
        def _(e):
            run("dve", e)

        @block.gpsimd
        def _(e):
            run("pool", e)

        @block.sync
        def _(e):
            run("sp", e)


def _prod(s):
    r = 1
    for v in s:
        r *= int(v)
    return r


class Arena:
    def __init__(self, ap, words):
        self.ap = ap
        self.words = words
        self.off = 0

    def mark(self):
        return self.off

    def reset(self, off):
        self.off = off

    def alloc(self, shape, dtype=F32):
        n = _prod(shape)
        size = 4 if dtype in (F32, I32) else 2
        w = (n * size + 3) // 4
        w = (w + 7) // 8 * 8
        assert self.off + w <= self.words, ("arena overflow", self.off, w, self.words)
        v = self.ap[:, self.off:self.off + w]
        self.off += w
        if dtype != F32:
            v = v.bitcast(dtype)
        v = v[:, 0:n]
        return view(v, shape)


def view(ap, shape):
    if len(shape) == 1:
        return ap
    names = "abcdefg"[:len(shape)]
    kw = {names[i]: int(shape[i]) for i in range(len(shape) - 1)}
    return ap.rearrange("p (" + " ".join(names) + ") -> p " + " ".join(names), **kw)


def _consts():
    c = {}
    idx = np.arange(128)
    c["ident"] = np.eye(128, dtype=np.float32)
    k = idx[:, None]
    p = idx[None, :]
    c["tri_le"] = (k <= p).astype(np.float32)
    c["tri_lt"] = (k < p).astype(np.float32)
    c["tri_ge"] = (k >= p).astype(np.float32)
    inv = (10000.0 ** (-np.arange(0, 64, 2, dtype=np.float32) / 64.0)).astype(np.float32)
    col = np.zeros((128, 4), np.float32)
    for q in range(128):
        i = q % 64
        col[q, 0] = inv[i % 32]
        col[q, 1] = (np.pi / 2) if q < 64 else 0.0
        col[q, 2] = 1.0 if q < 64 else (-1.0 if i < 32 else 1.0)
    c["ropec"] = col
    fold = np.zeros((128, 64), np.float32)
    fold[np.arange(64), np.arange(64)] = 1.0
    fold[np.arange(64) + 64, np.arange(64)] = 1.0
    c["fold"] = fold
    return c


def _perm64():
    i = np.arange(64)
    return np.where(i < 32, i + 32, i - 32)


VC = {}
_o = 0
for _n, _w in (("gmix", 8), ("gffn", 8), ("cwq", 10), ("cwk", 10), ("cbq", 2), ("cbk", 2), ("gcq", 2),
               ("gckv", 1), ("gqn", 1), ("gkn", 1), ("gqr", 1), ("gkr", 1)):
    VC[_n] = (_o, _w)
    _o += _w
NVC = _o
VR = {}
_o = 0
for _n, _w in (("bg", 16), ("gout", 512), ("cm", 128)):
    VR[_n] = (_o, _w)
    _o += _w
NVR = _o


def make_in_maps(inp):
    f32 = np.float32
    x = np.asarray(inp["x"], f32)
    pos = np.asarray(inp["positions"]).astype(np.int32)
    perm = _perm64()
    vc = np.zeros((128, NVC), f32)

    def put(name, arr):
        o, w = VC[name]
        vc[:, o:o + w] = np.asarray(arr, f32).reshape(128, w)

    put("gmix", np.asarray(inp["g_mix_norm"], f32)[0].reshape(8, 128).T)
    put("gffn", np.asarray(inp["g_ffn_norm"], f32)[0].reshape(8, 128).T)
    cw = np.asarray(inp["conv_w"], f32)[0]
    cb = np.asarray(inp["conv_b"], f32)[0]
    put("cwq", cw[:, 0:256].reshape(5, 2, 128).transpose(2, 1, 0))
    put("cwk", cw[:, 256:512].reshape(5, 2, 128).transpose(2, 1, 0))
    put("cbq", cb[0:256].reshape(2, 128).T)
    put("cbk", cb[256:512].reshape(2, 128).T)
    put("gcq", np.asarray(inp["g_cq"], f32)[0].reshape(2, 128).T)
    put("gckv", np.asarray(inp["g_ckv"], f32)[0].reshape(128, 1))
    gq = np.asarray(inp["g_q"], f32)[0]
    gk = np.asarray(inp["g_k"], f32)[0]
    put("gqn", gq[0:128].reshape(128, 1))
    put("gkn", gk[0:128].reshape(128, 1))
    put("gqr", np.concatenate([gq[128:192], gq[128:192][perm]]).reshape(128, 1))
    put("gkr", np.concatenate([gk[128:192], gk[128:192][perm]]).reshape(128, 1))
    cst = _consts()
    maps = []
    for c in range(8):
        b, j = c // 4, c % 4
        xb = x[b]
        xT = xb.T
        xa = np.ascontiguousarray(xT.reshape(8, 128, 16, 512).transpose(2, 1, 0, 3))
        s, e = 2048 * j, 2048 * (j + 1)
        xo = np.ascontiguousarray(xT[:, s:e].reshape(8, 128, 4, 512).transpose(2, 1, 0, 3))
        halo = np.zeros((1024, 4), f32)
        if j > 0:
            halo[:, 0:2] = xT[:, s - 2:s]
        if j < 3:
            halo[:, 2:4] = xT[:, e:e + 2]
        xh = np.ascontiguousarray(halo.reshape(8, 128, 4).transpose(1, 0, 2))
        xr = np.ascontiguousarray(xb[s:e].reshape(16, 128, 1024))
        vr = np.zeros((128, NVR), f32)
        o, w = VR["bg"]
        vr[:, o:o + w] = np.asarray(inp["b_gates"], f32)[0][None, :]
        o, w = VR["gout"]
        vr[:, o:o + w] = np.asarray(inp["g_mlstm_out"], f32)[0][None, :]
        cm = np.zeros((2, 64), f32)
        cm[0, 0:16 * j] = 1.0
        cm[1, 16 * (j + 1):] = 1.0
        o, w = VR["cm"]
        vr[:, o:o + w] = cm.reshape(1, 128)
        m = {
            "xa": xa, "xo": xo, "xh": xh, "xr": xr,
            "posa": np.ascontiguousarray(pos[b].reshape(1, 8192)),
            "poso": np.ascontiguousarray(pos[b, s:e].reshape(1, 2048)),
            "vc": vc, "vr": vr,
            "w_in": np.ascontiguousarray(np.asarray(inp["w_in"], f32)[0]),
            "w_uq": np.ascontiguousarray(np.asarray(inp["w_uq"], f32)[0]),
            "w_ukv": np.ascontiguousarray(np.asarray(inp["w_ukv"], f32)[0]),
            "w_out": np.ascontiguousarray(np.asarray(inp["w_out"], f32)[0]),
            "w_ff1": np.ascontiguousarray(np.asarray(inp["w_ff1"], f32)[0]),
            "w_ff2": np.ascontiguousarray(np.asarray(inp["w_ff2"], f32)[0]),
            "c_ident": cst["ident"], "c_le": cst["tri_le"], "c_lt": cst["tri_lt"], "c_ge": cst["tri_ge"],
            "c_rope": cst["ropec"], "c_fold": cst["fold"],
        }
        maps.append(m)
    return maps


ARENA_WORDS = 52672
R1 = 2600
R2 = 10900
R5 = 23600


class _Stop(Exception):
    pass


def build(dbg=False, upto=None):
    nc = bass.Bass("TRN2", target_bir_lowering=False)
    P = Prog(nc)

    def din(name, shape, dt=F32):
        return nc.dram_tensor(name, list(shape), dt, kind="ExternalInput").ap()

    xa = din("xa", [16, 128, 8, 512])
    xo = din("xo", [4, 128, 8, 512])
    xh = din("xh", [128, 8, 4])
    xr = din("xr", [16, 128, 1024])
    posa = din("posa", [1, 8192], I32)
    poso = din("poso", [1, 2048], I32)
    vc_d = din("vc", [128, NVC])
    vr_d = din("vr", [128, NVR])
    w_in = din("w_in", [1024, 2000])
    w_uq = din("w_uq", [256, 768])
    w_ukv = din("w_ukv", [128, 1024])
    w_out = din("w_out", [1024, 1024])
    w_ff1 = din("w_ff1", [1024, 4096])
    w_ff2 = din("w_ff2", [4096, 1024])
    c_ident = din("c_ident", [128, 128])
    c_le = din("c_le", [128, 128])
    c_lt = din("c_lt", [128, 128])
    c_ge = din("c_ge", [128, 128])
    c_rope = din("c_rope", [128, 4])
    c_fold = din("c_fold", [128, 64])
    out_d = nc.dram_tensor("out", [16, 128, 1024], F32, kind="ExternalOutput").ap()
    w1s = nc.dram_tensor("w1s", [8, 128, 8, 512], BF16, kind="Internal").ap()
    dbg_d = {}
    if dbg:
        dbg_d["yT"] = nc.dram_tensor("d_yT", [128, 8, 2048], BF16, kind="ExternalOutput").ap()
        dbg_d["misc"] = nc.dram_tensor("d_misc", [128, 8192], F32, kind="ExternalOutput").ap()

    import contextlib
    es = contextlib.ExitStack()
    with es:
        arena_t = es.enter_context(nc.sbuf_tensor("arena", [128, ARENA_WORDS], F32))
        ipos_t = es.enter_context(nc.sbuf_tensor("ipos", [128, 512], I32))
        ps = [es.enter_context(nc.psum_tensor(f"ps{i}", [128, 512], F32)) for i in range(8)]
        sems = {e: es.enter_context(nc.semaphore(f"s_{e}")) for e in ("pe", "act", "dve", "pool")}
        dma_sems = {}
        A = Arena(arena_t[:, :], ARENA_WORDS)

        def MM(out, lhsT, rhs, R, W, start=True, stop=True, skip=False):
            P.op("pe", lambda e: e.matmul(out, lhsT=lhsT, rhs=rhs, start=start, stop=stop,
                                          skip_group_check=skip), R, W)

        def ACT(out, in_, func, R, W, bias=0.0, scale=1.0, accum=None):
            if accum is None:
                P.op("act", lambda e: e.activation(out=out, in_=in_, func=func, bias=bias, scale=scale), R, W)
            else:
                P.op("act", lambda e: e.activation(out=out, in_=in_, func=func, bias=bias, scale=scale,
                                                   accum_out=accum), R, W)

        def _eng(e, name):
            return e

        def TT(eng, out, in0, in1, op, R, W):
            P.op(eng, lambda e: e.tensor_tensor(out=out, in0=in0, in1=in1, op=op), R, W)

        def TS(eng, out, in0, s1, op0, R, W, s2=None, op1=None):
            if op1 is None:
                P.op(eng, lambda e: e.tensor_scalar(out=out, in0=in0, scalar1=s1, scalar2=None, op0=op0), R, W)
            else:
                P.op(eng, lambda e: e.tensor_scalar(out=out, in0=in0, scalar1=s1, scalar2=s2, op0=op0, op1=op1),
                     R, W)

        def STT(eng, out, in0, scalar, in1, op0, op1, R, W):
            P.op(eng, lambda e: e.scalar_tensor_tensor(out=out, in0=in0, scalar=scalar, in1=in1, op0=op0, op1=op1),
                 R, W)

        def CP(eng, out, in_, R, W):
            if eng == "act":
                P.op(eng, lambda e: e.activation(out=out, in_=in_, func=AF.Copy), R, W)
            else:
                P.op(eng, lambda e: e.tensor_copy(out=out, in_=in_), R, W)

        def RED(eng, out, in_, op, R, W):
            P.op(eng, lambda e: e.tensor_reduce(out=out, in_=in_, axis=AX.X, op=op), R, W)

        def RECIP(out, in_, R, W):
            P.op("dve", lambda e: e.reciprocal(out=out, in_=in_), R, W)

        def MSET(eng, ap, val, W):
            P.op(eng, lambda e: e.memset(ap, val), (), W)

        def DMA(out, in_, R, W, key, queue="sp"):
            if key not in dma_sems:
                dma_sems[key] = es.enter_context(nc.semaphore("d_" + key))
            P.dma(lambda e: e.dma_start(out=out, in_=in_), R, W, key=key, queue=queue)

        PS = [p[:, :] for p in ps]
        PK = [f"ps{i}" for i in range(8)]

        try:
            ident_f = A.alloc([128])
            tri_le = A.alloc([128])
            tri_lt = A.alloc([128])
            tri_ge = A.alloc([128])
            ones_f = A.alloc([128])
            ropec = A.alloc([8])
            fold_f = A.alloc([64])
            vcs = A.alloc([NVC])
            vrs = A.alloc([NVR])
            ident_b = A.alloc([128], BF16)
            ones10 = A.alloc([128], BF16)
            ones8 = A.alloc([128], BF16)
            ones7 = A.alloc([128], BF16)
            fold_b = A.alloc([64], BF16)
            tri_le_b = A.alloc([128], BF16)
            tri_ge_b = A.alloc([128], BF16)
            sel_lo = A.alloc([128], BF16)
            sel_hi = A.alloc([128], BF16)
            cmb = A.alloc([128])
            ropef = A.alloc([4])
            assert A.off <= R1, A.off

            def vcol(name, i=0, n=1):
                o, w = VC[name]
                return vcs[:, o + i:o + i + n]

            def vrow(name):
                o, w = VR[name]
                return vrs[:, o:o + w]

            for dst, src, nm in ((ident_f, c_ident, "id"), (tri_le, c_le, "le"), (tri_lt, c_lt, "lt"),
                                 (tri_ge, c_ge, "ge"), (ropec[:, 0:4], c_rope, "rp"), (fold_f, c_fold, "fo"),
                                 (vcs, vc_d, "vc"), (vrs, vr_d, "vr")):
                DMA(dst, src[:, :], (), ("const",), "const")
            MSET("pool", ones_f, 1.0, ("const2",))
            MSET("pool", ones10, 2.0 ** -10, ("const2",))
            MSET("pool", ones8, 2.0 ** -8, ("const2",))
            MSET("pool", ones7, 2.0 ** -7, ("const2",))
            MSET("pool", sel_lo, 0.0, ("const2",))
            MSET("pool", sel_hi, 0.0, ("const2",))
            MSET("pool", sel_lo[0:64, :], 2.0 ** -8, ("const2",))
            MSET("pool", sel_hi[64:128, :], 2.0 ** -8, ("const2",))
            CP("dve", ident_b, ident_f, ("const",), ("const3",))
            CP("dve", fold_b, fold_f, ("const",), ("const3",))
            CP("dve", tri_le_b, tri_le, ("const",), ("const3",))
            CP("dve", tri_ge_b, tri_ge, ("const",), ("const3",))
            TS("dve", cmb, vrow("cm"), -1.0, ALU.add, ("const",), ("const3",), s2=30000.0, op1=ALU.mult)
            TT("dve", ropef[:, 0:1], vcol("gkr"), ropec[:, 2:3], ALU.mult, ("const",), ("const3",))
            TT("dve", ropef[:, 1:2], vcol("gqr"), ropec[:, 2:3], ALU.mult, ("const",), ("const3",))
            CONST = ("const", "const2", "const3")
            if upto == "consts":
                raise _Stop()

            A.reset(R1)
            Wb = A.alloc([8, 2064], BF16)
            A.reset(R2)
            ckvT = A.alloc([8192], BF16)
            RT = A.alloc([8192], BF16)
            stB = A.alloc([4, 129])
            mB = A.alloc([8])
            cqnT = A.alloc([2, 2048], BF16)
            Wuqb = A.alloc([2, 768], BF16)
            Wqr = A.alloc([2, 4, 128], BF16)
            Wukvb = A.alloc([1024], BF16)
            assert A.off <= R5, A.off

            w_in_v = w_in.rearrange("(c p) n -> p c n", p=128)
            for dc in range(8):
                DMA(Wb[:, dc, 0:2000], w_in_v[:, dc, :], (), ("Wb",), "wb", queue="pool")
            DMA(Wuqb, w_uq.rearrange("(c p) n -> p c n", p=128), (), ("Wuq",), "wu", queue="pool")
            DMA(Wukvb, w_ukv[:, :], (), ("Wukv",), "wukv", queue="pool")
            CP("pool", Wb[:, :, 2000:2032], Wb[:, :, 1968:2000], ("Wb",), ("Wb2",))
            CP("pool", Wb[:, :, 2032:2064], Wb[:, :, 1936:1968], ("Wb",), ("Wb2",))
            for h in range(4):
                CP("pool", Wqr[:, :, h, 0:64], Wuqb[:, :, h * 192 + 128:h * 192 + 192], ("Wuq",), ("Wqr",))
                CP("pool", Wqr[:, :, h, 64:96], Wuqb[:, :, h * 192 + 160:h * 192 + 192], ("Wuq",), ("Wqr",))
                CP("pool", Wqr[:, :, h, 96:128], Wuqb[:, :, h * 192 + 128:h * 192 + 160], ("Wuq",), ("Wqr",))
            w1_v = w_ff1.rearrange("(c p) (g n) -> g p c n", p=128, n=512)
            for g in range(8):
                DMA(w1s[g], w1_v[g], (), ("w1s",), "w1s", queue="pool")

            def rstd(out_ap, ps_ap, okey, pkey, eps=EPS):
                TS("dve", out_ap, ps_ap, float(eps), ALU.add, (pkey,), (okey,))
                ACT(out_ap, out_ap, AF.Ln, (okey,), (okey,))
                ACT(out_ap, out_ap, AF.Exp, (okey,), (okey,), scale=-0.5)

            def tile_p1(xt, xkey, N, xb, xbk, xsq):
                o, w = VC["gmix"]
                TT("pool", xb[:, :, 0:N], xt[:, :, 0:N], vcs[:, o:o + 8].unsqueeze(2).to_broadcast([128, 8, N]),
                   ALU.mult, (xkey,) + CONST, (xbk,))
                ACT(xsq[:, :, 0:N], xt[:, :, 0:N], AF.Square, (xkey,), ("xsq",))

            def tile_p2a(N, xsq, rxbc, rk):
                for dc in range(8):
                    MM(PS[0][:, 0:N], ones10, xsq[:, dc, 0:N], ("xsq",) + CONST, (PK[0],), start=(dc == 0), stop=(dc == 7))
                rstd(rxbc[:, 0:N], PS[0][:, 0:N], rk, PK[0])

            def tile_p2b(N, rxbc, rk, rxcol, ck):
                nsub = max(N // 128, 1)
                for sub in range(nsub):
                    MM(PS[0][:, 508 + sub:509 + sub], rxbc[0:1, sub * 128:(sub + 1) * 128], ones_f[0:1, 0:1],
                       (rk,) + CONST, (PK[0],))
                CP("dve", rxcol[:, 0:nsub], PS[0][:, 508:508 + nsub], (PK[0],), (ck,))

            def tile_body(N, xb, xbk, fm, tm, psrot):
                nsub = max(N // 128, 1)
                for (c0, M, handler) in fm:
                    b = psrot[0] % len(psrot[1])
                    bank = psrot[1][b]
                    psrot[0] += 1
                    for dc in range(8):
                        MM(PS[bank][0:M, 0:N], Wb[:, dc, c0:c0 + M], xb[:, dc, 0:N], (xbk, "Wb", "Wb2"), (PK[bank],),
                           start=(dc == 0), stop=(dc == 7))
                    handler(PS[bank], PK[bank])
                for (c0, ncol, handler) in tm:
                    for sub in range(nsub):
                        b = psrot[0] % len(psrot[1])
                        bank = psrot[1][b]
                        psrot[0] += 1
                        for dc in range(8):
                            MM(PS[bank][:, 0:ncol], xb[:, dc, sub * 128:(sub + 1) * 128], Wb[:, dc, c0:c0 + ncol],
                               (xbk, "Wb", "Wb2"), (PK[bank],), start=(dc == 0), stop=(dc == 7))
                        handler(sub, PS[bank], PK[bank])

            def sweep(ntiles, Ns, xt, xbs, xsq_, rxbcs, rxcols, dma_fn, handlers_fn, lag_fn, psrot):
                def p1(t):
                    dma_fn(t)
                    tile_p1(xt, "xt0", Ns[t], xbs[t % 2], f"xb{t % 2}", xsq_)

                p1(0)
                tile_p2a(Ns[0], xsq_, rxbcs[0], "rxbc0")
                tile_p2b(Ns[0], rxbcs[0], "rxbc0", rxcols[0], "rxcol0")
                for t in range(ntiles):
                    if t + 1 < ntiles:
                        p1(t + 1)
                    fm, tm = handlers_fn(t)
                    tile_body(Ns[t], xbs[t % 2], f"xb{t % 2}", fm, tm, psrot)
                    s1 = (t + 1) % 2
                    if t + 1 < ntiles:
                        tile_p2a(Ns[t + 1], xsq_, rxbcs[s1], f"rxbc{s1}")
                    if lag_fn is not None and t >= 1:
                        lag_fn(t - 1)
                    if t + 1 < ntiles:
                        tile_p2b(Ns[t + 1], rxbcs[s1], f"rxbc{s1}", rxcols[s1], f"rxcol{s1}")
                if lag_fn is not None:
                    lag_fn(ntiles - 1)

            A.reset(R5)
            xt_a = [A.alloc([8, 512])] * 2
            xb_a = [A.alloc([8, 512], BF16) for _ in range(2)]
            xsq = A.alloc([8, 512], BF16)
            rxbcs = [A.alloc([512]) for _ in range(2)]
            rxcols = [A.alloc([4]) for _ in range(2)]
            W = [A.alloc([512]) for _ in range(4)]
            G_all = A.alloc([64, 16])
            w_all = A.alloc([512])
            sm = A.alloc([64])
            rxc_all = A.alloc([64])
            markA = A.mark()
            GB = {n: A.alloc([512]) for n in ("li", "fp", "lim", "pfa", "pfb", "t1", "ell")}
            Wkp = A.alloc([512])
            Wsq = A.alloc([512])
            print("pass A1 arena end", A.off)

            posf = W[3]

            def rope_table(tab, tkey, pos_src, N):
                ip = ipos_t[:, 0:N]
                kf = A_tmp["kf"]
                DMA(ip, pos_src.partition_broadcast(128), (), ("ipos",), "posi")
                CP("dve", tab[:, 0:N], ip, ("ipos",), (tkey,))
                TS("dve", tab[:, 0:N], tab[:, 0:N], ropec[:, 0:1], ALU.mult, (tkey,) + CONST, (tkey,), s2=ropec[:, 1:2],
                   op1=ALU.add)
                TS("dve", kf[:, 0:N], tab[:, 0:N], 1.0 / TWO_PI, ALU.mult, (tkey,), ("W1",))
                CP("dve", ip, kf[:, 0:N], ("W1",), ("ipos",))
                CP("dve", kf[:, 0:N], ip, ("ipos",), ("W1",))
                STT("dve", tab[:, 0:N], kf[:, 0:N], -6.28125, tab[:, 0:N], ALU.mult, ALU.add, ("W1", tkey), (tkey,))
                STT("dve", tab[:, 0:N], kf[:, 0:N], -(TWO_PI - 6.28125), tab[:, 0:N], ALU.mult, ALU.add, ("W1", tkey),
                    (tkey,))
                TS("dve", tab[:, 0:N], tab[:, 0:N], 3.1415925, ALU.min, (tkey,), (tkey,), s2=-3.1415925, op1=ALU.max)
                ACT(tab[:, 0:N], tab[:, 0:N], AF.Sin, (tkey,), (tkey,))

            A_tmp = {"pi": W[2].bitcast(I32), "kf": W[1]}
            psrotA = [0, [1, 2, 3]]

            def passA1_tile(t):
                slot = t % 2
                rxbc, rxcol, rk, ck = rxbcs[slot], rxcols[slot], f"rxbc{slot}", f"rxcol{slot}"
                tok = slice(t * 512, (t + 1) * 512)
                rope_table(posf, "posf", posa[0:1, t * 512:(t + 1) * 512], 512)

                def h_ckv(psb, pk):
                    TT("dve", W[0], psb[:, 0:512], rxbc, ALU.mult, (pk, rk), ("W0",))
                    ACT(W[1].bitcast(BF16)[:, 0:512], W[0], AF.Square, ("W0",), ("W1",))
                    MM(PS[4][:, 0:512], ones7, W[1].bitcast(BF16)[:, 0:512], ("W1",) + CONST, (PK[4],))
                    rstd(W[2], PS[4][:, 0:512], "W2", PK[4])
                    STT("dve", ckvT[:, tok], W[0], vcol("gckv"), W[2], ALU.mult, ALU.mult, ("W0", "W2") + CONST,
                        (f"ckvT{t}",))

                def h_kp(psb, pk):
                    STT("dve", Wkp.bitcast(BF16)[:, 0:512], psb[:, 0:512], ropef[:, 0:1], posf, ALU.mult, ALU.mult,
                        (pk, "posf") + CONST, ("Wkp",))
                    MM(PS[5][0:64, 0:512], fold_b, Wkp.bitcast(BF16)[:, 0:512], ("Wkp",) + CONST, (PK[5],))
                    TT("dve", RT[0:64, tok], PS[5][0:64, 0:512], rxbc[0:64, :], ALU.mult, (PK[5], rk), (f"RTa{t}",))

                def h_kpsq(psb, pk):
                    TT("dve", Wsq[64:128, :], psb[64:128, 0:512], rxbc[64:128, :], ALU.mult, (pk, rk), ("Wsq",))
                    ACT(RT[64:128, tok], Wsq[64:128, :], AF.Square, ("Wsq",), (f"RTb{t}",))

                def h_g(sub, psb, pk):
                    TS("dve", G_all[:, 4 * t + sub, :], psb[:, 0:16], rxcol[:, sub:sub + 1], ALU.mult, (pk, ck),
                       ("G_all",))

                CP("pool", rxc_all[:, 4 * t:4 * t + 4], rxcol[:, 0:4], (ck,), ("rxc_all",))
                fm = [(1808, 128, h_ckv), (1936, 128, h_kp), (1872, 128, h_kpsq)]
                tm = [(1536, 16, h_g)]
                return fm, tm

            if upto == "W":
                raise _Stop()
            sweep(16, [512] * 16, xt_a[0], xb_a, xsq, rxbcs, rxcols,
                  lambda t: DMA(xt_a[0], xa[t], (), ("xt0",), "xt0"), passA1_tile, None, psrotA)

            if upto == "A1":
                raise _Stop()
            def d3(ap):
                return view(ap, [2, 64, 4])

            Gv = view(G_all.rearrange("p a b -> p (a b)"), [64, 2, 2, 4])
            bgv = view(vrow("bg"), [2, 2, 4])
            for two, nm in ((0, "li"), (1, "fp")):
                TT("dve", GB[nm].rearrange("p (d c h) -> p c d h", d=2, c=64), Gv[:, :, :, two, :],
                   bgv[:, :, two, :].unsqueeze(1).to_broadcast([128, 64, 2, 4]), ALU.add, ("G_all",) + CONST, ("b_" + nm,))
            ACT(GB["fp"], GB["fp"], AF.Exp, ("b_fp",), ("b_fp",), scale=-1.0)
            ACT(GB["fp"], GB["fp"], AF.Ln, ("b_fp",), ("b_fp",), bias=1.0)
            cmv = view(vrow("cm"), [2, 64]).unsqueeze(3).to_broadcast([128, 2, 64, 4])
            cmbv = view(cmb, [2, 64]).unsqueeze(3).to_broadcast([128, 2, 64, 4])
            TT("dve", d3(GB["fp"]), d3(GB["fp"]), cmv, ALU.mult, ("b_fp",) + CONST, ("b_fp",))
            TT("dve", d3(GB["lim"]), d3(GB["li"]), cmv, ALU.mult, ("b_li",) + CONST, ("b_lim",))
            TT("dve", d3(GB["lim"]), d3(GB["lim"]), cmbv, ALU.add, ("b_lim",) + CONST, ("b_lim",))
            for q4 in range(4):
                cs_ = slice(q4 * 128, (q4 + 1) * 128)
                MM(PS[5][:, cs_], tri_le if q4 < 2 else tri_lt, GB["fp"][:, cs_], ("b_fp",) + CONST, (PK[5],))
                MM(PS[4][:, cs_], ones_f, GB["fp"][:, cs_], ("b_fp",) + CONST, (PK[4],))
            CP("dve", GB["pfa"], PS[4][:, 0:512], (PK[4],), ("b_pfa",))
            CP("act", GB["li"], PS[4][:, 0:512], (PK[4], "b_li", "b_lim"), ("b_tot", "b_li"))
            src, dst = "pfa", "pfb"
            k_ = 1
            while k_ < 64:
                TT("dve", d3(GB[dst])[:, :, k_:64, :], d3(GB[src])[:, :, k_:64, :], d3(GB[src])[:, :, 0:64 - k_, :],
                   ALU.add, ("b_" + src,), ("b_" + dst,))
                CP("pool", d3(GB[dst])[:, :, 0:k_, :], d3(GB[src])[:, :, 0:k_, :], ("b_" + src,), ("b_" + dst,))
                src, dst = dst, src
                k_ *= 2
            incl = src
            carry = sm[:, 0:8]
            CP("dve", view(carry, [2, 4]), d3(GB[incl])[:, :, 63, :], ("b_" + incl,), ("sm",))
            TT("dve", GB["t1"], GB[incl], GB["li"], ALU.subtract, ("b_" + incl, "b_tot"), ("b_t1",))
            TT("dve", GB["t1"], GB["t1"], PS[5][:, 0:512], ALU.add, ("b_t1", PK[5]), ("b_t1",))
            TT("dve", GB["ell"][:, 0:256], GB["lim"][:, 0:256], GB["t1"][:, 0:256], ALU.add, ("b_lim", "b_t1"), ("b_ell",))
            TT("dve", GB["ell"][:, 256:512], GB["lim"][:, 256:512], GB["t1"][:, 256:512], ALU.subtract,
               ("b_lim", "b_t1"), ("b_ell",))
            gm1_ = sm[:, 8:16]
            mx_ = sm[:, 16:17]
            dg_ = sm[:, 24:32]
            mref = sm[:, 32:40]
            RED("dve", view(gm1_, [2, 4]), GB["ell"].rearrange("p (d c h) -> p d h c", d=2, c=64), ALU.max, ("b_ell",),
                ("sm",))
            MM(PS[5][0:8, 0:128], gm1_, ident_f, ("sm",) + CONST, (PK[5],))
            RED("dve", mx_[0:8, :], PS[5][0:8, 0:128], ALU.max, (PK[5], "sm"), ("sm",))
            TS("dve", dg_[0:8, :], ident_f[0:8, 0:8], mx_[0:8, 0:1], ALU.mult, ("sm",) + CONST, ("sm",))
            MM(PS[5][:, 256:264], ones_f[0:8, :], dg_[0:8, :], ("sm",) + CONST, (PK[5],))
            TS("dve", mref[:, 0:4], PS[5][:, 256:260], 0.0, ALU.max, (PK[5], "sm"), ("sm",))
            STT("dve", mref[:, 4:8], carry[:, 4:8], -1.0, PS[5][:, 260:264], ALU.mult, ALU.max, (PK[5], "sm"), ("sm",))
            TT("dve", mB[:, 0:4], mref[:, 0:4], carry[:, 0:4], ALU.subtract, ("sm",), ("mB",))
            CP("dve", mB[:, 4:8], mref[:, 4:8], ("sm",), ("mB",))
            TT("dve", d3(w_all), d3(GB["ell"]), view(mref, [2, 4]).unsqueeze(2).to_broadcast([128, 2, 64, 4]),
               ALU.subtract, ("b_ell", "sm"), ("w_all",))
            ACT(w_all, w_all, AF.Exp, ("w_all",), ("w_all",))
            P.barrier()
            if upto == "A1g":
                raise _Stop()

            A.reset(markA)
            kpre = A.alloc([2, 8196], BF16)
            vext = [A.alloc([4, 4, 129], BF16) for _ in range(2)]
            ktok = A.alloc([4, 256], BF16)
            kc = A.alloc([2, 512], BF16)
            wk = A.alloc([2, 4, 256], BF16)
            dgk = A.alloc([2, 5, 128], BF16)
            print("pass A2 arena end", A.off)
            o_cw, _ = VC["cwk"]
            for m_ in range(2):
                for j_ in range(5):
                    TS("dve", dgk[:, m_, j_, :], ident_f, vcs[:, o_cw + m_ * 5 + j_:o_cw + m_ * 5 + j_ + 1], ALU.mult, CONST,
                       ("dgk",))
            MSET("dve", kpre[:, :, 0:2], 0.0, ("kpre_l",))
            MSET("dve", kpre[:, :, 8194:8196], 0.0, ("kpre_r",))
            for s_ in range(2):
                MSET("pool", vext[s_][:, :, :, 128:129], 1.0, (f"vext{s_}",))
            SQ = [(6, 0), (6, 129), (6, 258), (7, 0)]

            def conv_silu(dst, dkey, src, skeys, cwname, cbname, m, n0, N, scale=None, eng="dve"):
                o, _ = VC[cwname]
                ob, _ = VC[cbname]
                acc = W[0]
                TS(eng, acc[:, 0:N], src[:, m, n0:n0 + N], vcs[:, o + m * 5:o + m * 5 + 1], ALU.mult, tuple(skeys) + CONST,
                   ("W0",))
                for j in range(1, 5):
                    STT(eng, acc[:, 0:N], src[:, m, n0 + j:n0 + j + N], vcs[:, o + m * 5 + j:o + m * 5 + j + 1],
                        acc[:, 0:N], ALU.mult, ALU.add, tuple(skeys) + ("W0",) + CONST, ("W0",))
                ACT(dst, acc[:, 0:N], AF.Silu, ("W0",) + CONST, (dkey,), bias=vcs[:, ob + m:ob + m + 1])
                if scale is not None:
                    TS("pool", dst, dst, scale, ALU.mult, (dkey,), (dkey,))

            def passA2_tile(t):
                slot = t % 2
                rxbc, rxcol, rk, ck = rxbcs[slot], rxcols[slot], f"rxbc{slot}", f"rxcol{slot}"

                def h_km(m):
                    def f(psb, pk):
                        TT("dve", kpre[:, m, 2 + t * 512:2 + (t + 1) * 512], psb[:, 0:512], rxbc, ALU.mult,
                           (pk, rk), (f"kpre{t}",))
                    return f

                def h_v(sub, psb, pk):
                    TS("dve", vext[slot][:, sub, :, 0:128], view(psb[:, 0:512], [4, 128]), rxcol[:, sub:sub + 1], ALU.mult,
                       (pk, ck), (f"vext{slot}",))

                fm = [(256, 128, h_km(0)), (384, 128, h_km(1))]
                tm = [(512, 512, h_v)]
                return fm, tm

            started = set()

            def passA2_lag(t):
                slot = t % 2
                ob_k, _ = VC["cbk"]
                kkeys = (f"kpre{t}", f"kpre{max(t - 1, 0)}", f"kpre{min(t + 1, 15)}", "kpre_l", "kpre_r", "dgk")
                for m in range(2):
                    for j in range(5):
                        MM(PS[5][:, 0:512], dgk[:, m, j, :], kpre[:, m, t * 512 + j:t * 512 + j + 512], kkeys, (PK[5],),
                           start=(j == 0), stop=(j == 4))
                    ACT(kc[:, m, :], PS[5][:, 0:512], AF.Silu, (PK[5],) + CONST, ("kc",), bias=vcs[:, ob_k + m:ob_k + m + 1])
                for m in range(2):
                    for sub in range(4):
                        MM(PS[4][:, sub * 128:(sub + 1) * 128], kc[:, m, sub * 128:(sub + 1) * 128], ident_b,
                           ("kc",) + CONST, (PK[4],))
                    CP("act", ktok[:, :, m * 128:(m + 1) * 128], view(PS[4][:, 0:512], [4, 128]), (PK[4],), ("ktok",))
                for d in range(2):
                    TT("pool" if d else "dve", view(wk[:, d].rearrange("p a b -> p (a b)"), [16, 64]),
                       view(ktok.rearrange("p a b -> p (a b)"), [16, 64]),
                       w_all[:, d * 256 + 16 * t:d * 256 + 16 * t + 16].unsqueeze(2).to_broadcast([128, 16, 64]), ALU.mult,
                       ("ktok", "w_all"), (f"wk{d}",))
                for q in (0, 3, 1, 2):
                    d, hp = q // 2, q % 2
                    bank, col = SQ[q]
                    for half in range(2):
                        h = 2 * hp + half
                        for c in range(4):
                            first = (bank, half) not in started
                            started.add((bank, half))
                            MM(PS[bank][half * 64:half * 64 + 64, col:col + 129], wk[:, d, c, h * 64:(h + 1) * 64],
                               vext[slot][:, c, h, :], (f"wk{d}", f"vext{slot}"), (PK[bank],), start=first,
                               stop=(t == 15 and c == 3), skip=True)

            dgt = A.alloc([512])
            o_gm, _ = VC["gmix"]

            def a2_p1(t):
                DMA(xt_a[0], xa[t], (), ("xt0",), "xt0")
                TT("pool", xb_a[t % 2], xt_a[0], vcs[:, o_gm:o_gm + 8].unsqueeze(2).to_broadcast([128, 8, 512]), ALU.mult,
                   ("xt0",) + CONST, (f"xb{t % 2}",))

            def a2_rx(t):
                sl = t % 2
                for sub in range(4):
                    TS("dve", dgt[:, sub * 128:(sub + 1) * 128], ident_f, rxc_all[:, 4 * t + sub:4 * t + sub + 1], ALU.mult,
                       ("rxc_all",) + CONST, ("dgt",))
                for sub in range(4):
                    MM(PS[0][:, sub * 128:(sub + 1) * 128], ones_f, dgt[:, sub * 128:(sub + 1) * 128], ("dgt",) + CONST,
                       (PK[0],))
                CP("act", rxbcs[sl], PS[0][:, 0:512], (PK[0],), (f"rxbc{sl}",))
                CP("pool", rxcols[sl][:, 0:4], rxc_all[:, 4 * t:4 * t + 4], ("rxc_all",), (f"rxcol{sl}",))

            a2_p1(0)
            a2_rx(0)
            for t in range(16):
                if t + 1 < 16:
                    a2_p1(t + 1)
                fm, tm = passA2_tile(t)
                tile_body(512, xb_a[t % 2], f"xb{t % 2}", fm, tm, psrotA)
                if t + 1 < 16:
                    a2_rx(t + 1)
                if t >= 1:
                    passA2_lag(t - 1)
            passA2_lag(15)
            for q in range(4):
                bank, col = SQ[q]
                CP("dve" if q % 2 else "act", stB[:, q, :], PS[bank][:, col:col + 129], (PK[bank],), ("stB",))
            P.barrier()
            if upto == "A":
                raise _Stop()

            A.reset(R5)
            qkpre = A.alloc([4, 2052], BF16)
            vext_o = A.alloc([16, 4, 129], BF16)
            so = A.alloc([16, 512], BF16)
            G_o = A.alloc([16, 16])
            markB = A.mark()
            xt_b = [A.alloc([8, 512])] * 2
            xb_b = [A.alloc([8, 512], BF16) for _ in range(2)]
            xsq = A.alloc([8, 512], BF16)
            rxbcs = [A.alloc([512]) for _ in range(2)]
            rxcols = [A.alloc([4]) for _ in range(2)]
            W[:] = [A.alloc([512]) for _ in range(4)]
            print("pass B arena end", A.off)
            MSET("pool", vext_o[:, :, :, 128:129], 1.0, ("vext_o",))
            psrotB = [0, [1, 2, 3]]

            def passB_tile(T):
                N = 512 if T < 4 else 4
                slot = T % 2
                rxbc, rxcol, rk, ck = rxbcs[slot], rxcols[slot], f"rxbc{slot}", f"rxcol{slot}"
                if N == 512:
                    lo = 2 + T * 512
                    dsts = [(lo, 0, 512)]
                else:
                    dsts = [(0, 0, 2), (2050, 2, 4)]

                def h_qk(idx):
                    def f(psb, pk):
                        for (d0, s0, s1) in dsts:
                            TT("dve", qkpre[:, idx, d0:d0 + (s1 - s0)], psb[:, s0:s1], rxbc[:, s0:s1], ALU.mult,
                               (pk, rk), ("qkpre",))
                    return f

                def h_cq(m):
                    def f(psb, pk):
                        TT("dve", W[m], psb[:, 0:512], rxbc, ALU.mult, (pk, rk), (f"W{m}",))
                        ACT(W[3].bitcast(BF16)[:, m * 512:(m + 1) * 512], W[m], AF.Square, (f"W{m}",), ("W3",))
                        if m == 1:
                            for mm_ in range(2):
                                MM(PS[4][:, 0:512], ones8, W[3].bitcast(BF16)[:, mm_ * 512:(mm_ + 1) * 512],
                                   ("W3",) + CONST, (PK[4],), start=(mm_ == 0), stop=(mm_ == 1))
                            rstd(W[2], PS[4][:, 0:512], "W2", PK[4])
                            for mm_ in range(2):
                                STT("dve", cqnT[:, mm_, T * 512:(T + 1) * 512], W[mm_], vcol("gcq", mm_), W[2], ALU.mult,
                                    ALU.mult, (f"W{mm_}", "W2") + CONST, ("cqnT",))
                    return f

                def h_v(sub, psb, pk):
                    TS("dve", vext_o[:, 4 * T + sub, :, 0:128], view(psb[:, 0:512], [4, 128]), rxcol[:, sub:sub + 1],
                       ALU.mult, (pk, ck), ("vext_o",))

                def h_o(sub, psb, pk):
                    ACT(so[:, 4 * T + sub, :], psb[:, 0:512], AF.Sigmoid, (pk, ck), ("so",), scale=rxcol[:, sub:sub + 1])

                def h_g(sub, psb, pk):
                    TS("dve", G_o[:, 4 * T + sub, :], psb[:, 0:16], rxcol[:, sub:sub + 1], ALU.mult, (pk, ck),
                       ("G_o",))

                fm = [(0, 128, h_qk(0)), (128, 128, h_qk(1)), (256, 128, h_qk(2)), (384, 128, h_qk(3))]
                tm = []
                if N == 512:
                    fm += [(1552, 128, h_cq(0)), (1680, 128, h_cq(1))]
                    tm = [(512, 512, h_v), (1024, 512, h_o), (1536, 16, h_g)]
                return fm, tm

            def dmaB(T):
                if T < 4:
                    DMA(xt_b[0], xo[T], (), ("xt0",), "xt0")
                else:
                    DMA(xt_b[0][:, :, 0:4], xh[:, :, :], (), ("xt0",), "xt0")

            sweep(5, [512] * 4 + [4], xt_b[0], xb_b, xsq, rxbcs, rxcols, dmaB, passB_tile, None, psrotB)
            P.barrier()
            if upto == "B":
                raise _Stop()

            A.reset(markB)
            qc = A.alloc([2, 2048], BF16)
            kcO = A.alloc([2, 2048], BF16)
            ktok_o = A.alloc([16, 256], BF16)
            hsum = A.alloc([16, 2, 128])
            W[:] = [None, None, None, None]
            dgc = A.alloc([4, 5, 128], BF16)
            tq = A.alloc([8])
            GA = {n: A.alloc([128]) for n in ("li", "fp", "bn", "an", "u", "cmT", "umb", "g", "M17", "M17b", "Mt", "al",
                                              "be", "flo", "wi", "ws", "dec", "t", "mp", "mn")}
            umaxc = A.alloc([1])
            Ab = [A.alloc([128], BF16) for _ in range(4)]
            wsk = [A.alloc([64], BF16) for _ in range(4)]
            stf = A.alloc([2, 129])
            stb = A.alloc([2, 129], BF16)
            cmbt_all = A.alloc([4, 129])
            cmbt = [cmbt_all[:, u_, :] for u_ in range(4)]
            dn_all = A.alloc([3, 4])
            sqw = A.alloc([1024])
            ssq8 = A.alloc([8])
            print("phase M arena end", A.off)
            A.reset(R1)
            yT = A.alloc([8, 2048], BF16)

            for idx in range(4):
                o_c, _ = VC["cwq" if idx < 2 else "cwk"]
                for j_ in range(5):
                    TS("dve", dgc[:, idx, j_, :], ident_f, vcs[:, o_c + (idx % 2) * 5 + j_:o_c + (idx % 2) * 5 + j_ + 1],
                       ALU.mult, CONST, ("dgc",))
            ob_q, _ = VC["cbq"]
            ob_k2, _ = VC["cbk"]
            for piece in range(4):
                n0 = piece * 512
                for idx in range(4):
                    bk = (0, 1, 2, 3)[idx]
                    m = idx % 2
                    for j_ in range(5):
                        MM(PS[bk][:, 0:512], dgc[:, idx, j_, :], qkpre[:, idx, n0 + j_:n0 + j_ + 512], ("qkpre", "dgc"),
                           (PK[bk],), start=(j_ == 0), stop=(j_ == 4))
                    if idx < 2:
                        ACT(qc[:, m, n0:n0 + 512], PS[bk][:, 0:512], AF.Silu, (PK[bk],) + CONST, ("qc",),
                            bias=vcs[:, ob_q + m:ob_q + m + 1])
                        TS("pool", qc[:, m, n0:n0 + 512], qc[:, m, n0:n0 + 512], 0.125, ALU.mult, ("qc",), ("qc",))
                    else:
                        ACT(kcO[:, m, n0:n0 + 512], PS[bk][:, 0:512], AF.Silu, (PK[bk],) + CONST, ("kcO",),
                            bias=vcs[:, ob_k2 + m:ob_k2 + m + 1])
            for cc in range(4):
                for m in range(2):
                    for k4 in range(4):
                        c = cc * 4 + k4
                        MM(PS[4][:, k4 * 128:(k4 + 1) * 128], kcO[:, m, c * 128:(c + 1) * 128], ident_b, ("kcO",) + CONST,
                           (PK[4],))
                    CP("act", ktok_o[:, cc * 4:cc * 4 + 4, m * 128:(m + 1) * 128], view(PS[4][:, 0:512], [4, 128]),
                       (PK[4],), ("ktok_o",))

            def dch16(ap):
                return ap.rearrange("p (d c h) -> p c d h", d=2, c=16)

            def v3(ap):
                return view(ap, [2, 16, 4])

            Gv = view(G_o.rearrange("p a b -> p (a b)"), [16, 2, 2, 4])
            bgv = view(vrow("bg"), [2, 2, 4])
            for two, nm in ((0, "li"), (1, "fp")):
                TT("dve", dch16(GA[nm]), Gv[:, :, :, two, :], bgv[:, :, two, :].unsqueeze(1).to_broadcast([128, 16, 2, 4]),
                   ALU.add, ("G_o",) + CONST, ("g_" + nm,))
            ACT(GA["fp"], GA["fp"], AF.Exp, ("g_fp",), ("g_fp",), scale=-1.0)
            ACT(GA["fp"], GA["fp"], AF.Ln, ("g_fp",), ("g_fp",), bias=1.0)
            MM(PS[5][:, 0:64], tri_le, GA["fp"][:, 0:64], ("g_fp",) + CONST, (PK[5],))
            MM(PS[5][:, 64:128], tri_ge, GA["fp"][:, 64:128], ("g_fp",) + CONST, (PK[5],))
            MM(PS[5][:, 128:256], ones_f, GA["fp"], ("g_fp",) + CONST, (PK[5],))
            CP("dve", GA["bn"], PS[5][:, 0:128], (PK[5],), ("g_bn",))
            CP("dve", GA["an"], PS[5][:, 128:256], (PK[5],), ("g_an",))
            TT("dve", GA["u"], GA["li"], GA["bn"], ALU.add, ("g_li", "g_bn"), ("g_u",))
            MM(PS[5][:, 256:384], GA["u"], ident_f, ("g_u",) + CONST, (PK[5],))
            CP("dve", GA["cmT"], PS[5][:, 256:384], (PK[5],), ("g_cmT",))
            k_ = 1
            src_, dst_ = "cmT", "t"
            while k_ < 128:
                S_, D_ = GA[src_], GA[dst_]
                ks, kd = "g_" + src_, "g_" + dst_
                TT("dve", D_[0:64, k_:128], S_[0:64, k_:128], S_[0:64, 0:128 - k_], ALU.max, (ks,), (kd,))
                CP("pool", D_[0:64, 0:k_], S_[0:64, 0:k_], (ks,), (kd,))
                TT("dve", D_[64:128, 0:128 - k_], S_[64:128, 0:128 - k_], S_[64:128, k_:128], ALU.max, (ks,), (kd,))
                CP("pool", D_[64:128, 128 - k_:128], S_[64:128, 128 - k_:128], (ks,), (kd,))
                src_, dst_ = dst_, src_
                k_ *= 2
            if src_ != "cmT":
                CP("dve", GA["cmT"], GA[src_], ("g_" + src_,), ("g_cmT",))
            CP("dve", umaxc[0:64, :], GA["cmT"][0:64, 127:128], ("g_cmT",), ("g_umc",))
            CP("dve", umaxc[64:128, :], GA["cmT"][64:128, 0:1], ("g_cmT",), ("g_umc",))
            TS("dve", GA["t"], ident_f, umaxc[:, 0:1], ALU.mult, ("g_umc",) + CONST, ("g_t",))
            MM(PS[5][:, 384:512], ones_f, GA["t"], ("g_t",) + CONST, (PK[5],))
            CP("dve", GA["umb"], PS[5][:, 384:512], (PK[5],), ("g_umb",))
            TT("dve", GA["g"], GA["umb"], GA["an"], ALU.subtract, ("g_umb", "g_an"), ("g_g",))
            MM(PS[5][:, 0:128], GA["cmT"], ident_f, ("g_cmT",) + CONST, (PK[5],))
            m17 = GA["M17"]
            m17b = GA["M17b"]
            mf = view(m17[:, 0:68], [17, 4])
            mbw = view(m17b[:, 0:68], [17, 4])
            anv = v3(GA["an"])
            gv_ = v3(GA["g"])
            CP("dve", mf[:, 0, :], mB[:, 0:4], ("mB",), ("g_m17",))
            CP("dve", mbw[:, 16, :], mB[:, 4:8], ("mB",), ("g_m17",))
            for i in range(16):
                TT("dve", tq[:, 0:4], mf[:, i, :], anv[:, 0, i, :], ALU.subtract, ("g_m17", "g_an"), ("tq",))
                TT("dve", mf[:, i + 1, :], tq[:, 0:4], gv_[:, 0, i, :], ALU.max, ("tq", "g_g"), ("g_m17",))
                c = 15 - i
                TT("dve", tq[:, 4:8], mbw[:, c + 1, :], anv[:, 1, c, :], ALU.subtract, ("g_m17", "g_an"), ("tq",))
                TT("dve", mbw[:, c, :], tq[:, 4:8], gv_[:, 1, c, :], ALU.max, ("tq", "g_g"), ("g_m17",))
            mprev = GA["mp"]
            mnext = GA["mn"]
            CP("dve", v3(mprev)[:, 0], mf[:, 0:16, :], ("g_m17",), ("g_mp",))
            CP("dve", v3(mprev)[:, 1], mbw[:, 1:17, :], ("g_m17",), ("g_mp",))
            CP("dve", v3(mnext)[:, 0], mf[:, 1:17, :], ("g_m17",), ("g_mn",))
            CP("dve", v3(mnext)[:, 1], mbw[:, 0:16, :], ("g_m17",), ("g_mn",))
            TT("dve", GA["Mt"], PS[5][:, 0:128], mprev, ALU.max, (PK[5], "g_mp"), ("g_Mt",))

            def expdiff(dst, dkey, a, akey, b, bkey):
                TT("dve", dst, a, b, ALU.subtract, (akey, bkey), (dkey,))
                ACT(dst, dst, AF.Exp, (dkey,), (dkey,))

            expdiff(GA["al"], "g_al", GA["umb"], "g_umb", GA["Mt"], "g_Mt")
            expdiff(GA["be"], "g_be", mprev, "g_mp", GA["Mt"], "g_Mt")
            expdiff(GA["flo"], "g_flo", GA["bn"], "g_bn", GA["Mt"], "g_Mt")
            expdiff(GA["wi"], "g_wi", GA["u"], "g_u", GA["umb"], "g_umb")
            TT("dve", GA["ws"], GA["u"], GA["an"], ALU.subtract, ("g_u", "g_an"), ("g_ws",))
            expdiff(GA["ws"], "g_ws", GA["ws"], "g_ws", mnext, "g_mn")
            TT("dve", GA["dec"], mprev, GA["an"], ALU.subtract, ("g_mp", "g_an"), ("g_dec",))
            expdiff(GA["dec"], "g_dec", GA["dec"], "g_dec", mnext, "g_mn")
            GK = ("g_al", "g_be", "g_flo", "g_wi", "g_ws", "g_dec")

            for g in range(2):
                for d in range(2):
                    CP("dve", stf[:, d, :], stB[:, d * 2 + g, :], ("stB",), (f"stf{d}_0", f"stf{d}_1"))
                    CP("act", stb[:, d, :], stf[:, d, :], (f"stf{d}_0", f"stf{d}_1"), (f"stb{d}_0", f"stb{d}_1"))
                QKB = {(0, 0): (1, 0), (0, 1): (7, 0), (1, 0): (0, 0), (1, 1): (6, 300)}
                for i in range(16):
                    U = []
                    for d in range(2):
                        for half in range(2):
                            c = i if d == 0 else 15 - i
                            h = 2 * g + half
                            U.append(dict(d=d, half=half, c=c, h=h, r=d * 64 + c * 4 + h, cols=slice(c * 128, (c + 1) * 128),
                                          rows=slice(half * 64, half * 64 + 64), u=d * 2 + half,
                                          mask=tri_le_b if d == 0 else tri_ge_b, bI=2 + 2 * d, bJ=3 + 2 * d))
                    for x in U:
                        qb, qc0 = QKB[(x["d"], x["half"])]
                        x["qk"] = PS[qb][:, qc0:qc0 + 128]
                        x["qkk"] = PK[qb]
                        MM(x["qk"], kcO[x["rows"], g, x["cols"]], qc[x["rows"], g, x["cols"]], ("kcO", "qc"), (x["qkk"],))
                    for x in U:
                        u, r = x["u"], x["r"]
                        STT("dve", Ab[u], x["qk"], GA["wi"][:, r:r + 1], x["mask"], ALU.mult, ALU.mult,
                            (x["qkk"], "g_wi") + CONST, (f"Ab{u}",))
                        TS("pool", wsk[u], ktok_o[:, x["c"], x["h"] * 64:(x["h"] + 1) * 64], GA["ws"][:, r:r + 1], ALU.mult,
                           ("ktok_o", "g_ws"), (f"wsk{u}",))
                    for x in U:
                        u, d, half, c, h = x["u"], x["d"], x["half"], x["c"], x["h"]
                        bI, bJ = x["bI"], x["bJ"]
                        x["pI"] = PS[bI][:, half * 129:half * 129 + 129]
                        x["kI"] = PK[bI]
                        x["pJ"] = PS[bI][:, 258:387] if half == 0 else PS[bJ][:, 0:129]
                        x["kJ"] = PK[bI] if half == 0 else PK[bJ]
                        MM(x["pI"], Ab[u], vext_o[:, c, h, :], (f"Ab{u}", "vext_o"), (x["kI"],))
                        MM(x["pJ"], qc[x["rows"], g, x["cols"]], stb[x["rows"], d, :], ("qc", f"stb{d}_{half}"), (x["kJ"],))
                        MM(PS[6][x["rows"], d * 129:d * 129 + 129], wsk[u], vext_o[:, c, h, :], (f"wsk{u}", "vext_o"),
                           (PK[6],))
                    for x in U:
                        u, r = x["u"], x["r"]
                        P.op("act", (lambda o_, i_, sc_: (lambda e: e.activation(out=o_, in_=i_, func=AF.Copy, scale=sc_)))(
                            cmbt[u], x["pJ"], GA["be"][:, r:r + 1]), (x["kJ"], "g_be"), (f"cmbt{u}",))
                    for x in U:
                        u, r = x["u"], x["r"]
                        STT("dve", cmbt[u], x["pI"], GA["al"][:, r:r + 1], cmbt[u], ALU.mult, ALU.add,
                            (x["kI"], "g_al", f"cmbt{u}"), (f"cmbt{u}",))
                    for x in U:
                        u, d, half, rows, r = x["u"], x["d"], x["half"], x["rows"], x["r"]
                        STT("dve", stf[rows, d, :], stf[rows, d, :], GA["dec"][rows, r:r + 1],
                            PS[6][rows, d * 129:d * 129 + 129], ALU.mult, ALU.add, (f"stf{d}_{half}", "g_dec", PK[6]),
                            (f"stf{d}_{half}",))
                        CP("act", stb[rows, d, :], stf[rows, d, :], (f"stf{d}_{half}",), (f"stb{d}_{half}",))
                    ck_all = tuple(f"cmbt{u_}" for u_ in range(4))
                    den_v = cmbt_all[:, :, 128]
                    STT("dve", dn_all[:, 0, :], den_v, -1.0, den_v, ALU.mult, ALU.max, ck_all, ("dn",))
                    for d in range(2):
                        r0 = U[2 * d]["r"]
                        TT("dve", dn_all[:, 1, 2 * d:2 * d + 2], dn_all[:, 0, 2 * d:2 * d + 2], GA["flo"][:, r0:r0 + 2],
                           ALU.max, ("dn", "g_flo"), ("dn",))
                    RECIP(dn_all[:, 2, :], dn_all[:, 1, :], ("dn",), ("dn",))
                    for x in U:
                        u, d, half, c = x["u"], x["d"], x["half"], x["c"]
                        first = (d == 0 and c <= 7) or (d == 1 and c >= 8)
                        if first:
                            TS("dve", hsum[:, c, half, :], cmbt[u][:, 0:128], dn_all[:, 2, u:u + 1], ALU.mult,
                               (f"cmbt{u}", "dn"), (f"hsum{c}_{half}",))
                        else:
                            STT("dve", hsum[:, c, half, :], cmbt[u][:, 0:128], dn_all[:, 2, u:u + 1], hsum[:, c, half, :],
                                ALU.mult, ALU.add, (f"cmbt{u}", "dn", f"hsum{c}_{half}"), (f"hsum{c}_{half}",))
                for cc in range(4):
                    hv = hsum[:, 4 * cc:4 * cc + 4].rearrange("p a b c -> p (a b c)")
                    hkeys = tuple(f"hsum{4 * cc + k4}_{half}" for k4 in range(4) for half in range(2))
                    TT("pool", sqw, hv, hv, ALU.mult, hkeys, ("sqw",))
                    RED("dve", ssq8, view(sqw, [8, 128]), ALU.add, ("sqw",), ("ssq8",))
                    TS("dve", ssq8, ssq8, 1.0 / 128.0, ALU.mult, ("ssq8",), ("ssq8",), s2=EPS, op1=ALU.add)
                    ACT(ssq8, ssq8, AF.Sqrt, ("ssq8",), ("ssq8",))
                    RECIP(ssq8, ssq8, ("ssq8",), ("ssq8",))
                    TT("dve", view(sqw, [8, 128]), view(hv, [8, 128]), ssq8.unsqueeze(2).to_broadcast([128, 8, 128]),
                       ALU.mult, hkeys + ("ssq8", "sqw"), ("sqw",))
                    gov = view(vrow("gout")[:, 2 * g * 128:(2 * g + 2) * 128], [2, 128]).unsqueeze(1).to_broadcast(
                        [128, 4, 2, 128])
                    TT("pool", view(sqw, [4, 2, 128]), view(sqw, [4, 2, 128]), gov, ALU.mult, ("sqw",) + CONST, ("sqw",))
                    sov = so[:, 4 * cc:4 * cc + 4, 2 * g * 128:(2 * g + 2) * 128]
                    TT("dve", sov, view(sqw, [4, 256]), sov, ALU.mult, ("sqw", "so"), ("so",))
                    for half in range(2):
                        h = 2 * g + half
                        for k4 in range(4):
                            c = 4 * cc + k4
                            MM(PS[7][:, k4 * 128:(k4 + 1) * 128], so[:, c, h * 128:(h + 1) * 128], ident_b, ("so",) + CONST,
                               (PK[7],))
                        CP("act", yT[:, h, cc * 512:(cc + 1) * 512], PS[7][:, 0:512], (PK[7],), ("yT",))
            P.barrier()
            if upto == "M":
                raise _Stop()

            A.reset(R5)
            QTn = A.alloc([4, 2048], BF16)
            QTr = A.alloc([4, 2048], BF16)
            KTn = A.alloc([8192], BF16)
            KTr = A.alloc([8192], BF16)
            Vh = A.alloc([64, 129], BF16)
            PT = [A.alloc([512], BF16) for _ in range(3)]
            cs_o = A.alloc([2048])
            racc = A.alloc([512])
            rinv = A.alloc([512])
            W[:] = [A.alloc([512]) for _ in range(4)]
            A_tmp["pi"] = W[2].bitcast(I32)
            A_tmp["kf"] = W[1]
            print("phase T arena end", A.off)
            MSET("pool", Vh[:, :, 128:129], 1.0, ("Vh",))
            MSET("pool", QTr[64:128, :, :], 0.0, ("QTr",))
            MSET("pool", KTr[64:128, :], 0.0, ("KTr",))
            SC = float(0.75 * 192.0 ** -0.5)
            EPSQ = 0.75 * EPS

            for T in range(4):
                tok = slice(T * 512, (T + 1) * 512)
                rope_table(cs_o[:, tok], "cs_o", poso[0:1, T * 512:(T + 1) * 512], 512)
                for hp_ in range(2):
                    units = []
                    for e in range(2):
                        units.append(dict(h=2 * hp_ + e, wq=W[(0, 1)[e]].bitcast(BF16), wqk=("W0", "W1")[e], wr=W[(3, 2)[e]],
                                          wrk=("W3", "W2")[e], bn=(1, 5)[e], br=(2, 6)[e], bs=(3, 7)[e], bf=(4, 0)[e]))
                    for x in units:
                        h = x["h"]
                        for m in range(2):
                            MM(PS[x["bn"]][:, 0:512], Wuqb[:, m, h * 192:h * 192 + 128], cqnT[:, m, tok], ("Wuq", "cqnT"),
                               (PK[x["bn"]],), start=(m == 0), stop=(m == 1))
                        for m in range(2):
                            MM(PS[x["br"]][:, 0:512], Wqr[:, m, h, :], cqnT[:, m, tok], ("Wqr", "cqnT"), (PK[x["br"]],),
                               start=(m == 0), stop=(m == 1))
                    for x in units:
                        ACT(x["wq"][:, 0:512], PS[x["bn"]][:, 0:512], AF.Square, (PK[x["bn"]],), (x["wqk"],))
                        ACT(x["wq"][:, 512:1024], PS[x["br"]][:, 0:512], AF.Square, (PK[x["br"]],), (x["wqk"],))
                    for x in units:
                        MM(PS[x["bs"]][:, 0:512], ones8, x["wq"][:, 0:512], (x["wqk"],) + CONST, (PK[x["bs"]],), start=True,
                           stop=False)
                        MM(PS[x["bs"]][:, 0:512], sel_lo, x["wq"][:, 512:1024], (x["wqk"],) + CONST, (PK[x["bs"]],),
                           start=False, stop=True)
                    for x in units:
                        TS("dve", x["wr"], PS[x["bs"]][:, 0:512], float(EPSQ), ALU.add, (PK[x["bs"]],), (x["wrk"],))
                    for x in units:
                        ACT(x["wr"], x["wr"], AF.Ln, (x["wrk"],), (x["wrk"],))
                    for x in units:
                        ACT(x["wr"], x["wr"], AF.Exp, (x["wrk"],), (x["wrk"],), scale=-0.5)
                    for x in units:
                        STT("dve", QTn[:, x["h"], tok], PS[x["bn"]][:, 0:512], vcol("gqn"), x["wr"], ALU.mult, ALU.mult,
                            (PK[x["bn"]], x["wrk"]) + CONST, ("QTn",))
                        STT("dve", x["wq"][:, 0:512], PS[x["br"]][:, 0:512], ropef[:, 1:2], cs_o[:, tok], ALU.mult, ALU.mult,
                            (PK[x["br"]], "cs_o", x["wqk"]) + CONST, (x["wqk"],))
                    for x in units:
                        MM(PS[x["bf"]][0:64, 0:512], fold_b, x["wq"][:, 0:512], (x["wqk"],) + CONST, (PK[x["bf"]],))
                    for x in units:
                        TT("dve", QTr[0:64, x["h"], tok], PS[x["bf"]][0:64, 0:512], x["wr"][0:64, :], ALU.mult,
                           (PK[x["bf"]], x["wrk"]), ("QTr",))

            for h in range(4):
                for tp_ in range(8):
                    units = []
                    for e in range(2):
                        t = 2 * tp_ + e
                        units.append(dict(t=t, tok=slice(t * 512, (t + 1) * 512), bk=(1, 4)[e], bs=(3, 5)[e], bv=(2, 6)[e],
                                          wq=W[(0, 1)[e]], wqk=("W0", "W1")[e], wr=W[(3, 2)[e]], wrk=("W3", "W2")[e]))
                    for x in units:
                        MM(PS[x["bk"]][:, 0:512], Wukvb[:, h * 256:h * 256 + 128], ckvT[:, x["tok"]],
                           ("Wukv", f"ckvT{x['t']}"), (PK[x["bk"]],))
                    for x in units:
                        ACT(x["wq"].bitcast(BF16)[:, 0:512], PS[x["bk"]][:, 0:512], AF.Square, (PK[x["bk"]],), (x["wqk"],))
                    for x in units:
                        MM(PS[x["bs"]][:, 0:512], ones8, x["wq"].bitcast(BF16)[:, 0:512], (x["wqk"],) + CONST,
                           (PK[x["bs"]],), start=True, stop=False)
                        MM(PS[x["bs"]][:, 0:512], sel_hi, RT[:, x["tok"]], (f"RTa{x['t']}", f"RTb{x['t']}") + CONST,
                           (PK[x["bs"]],), start=False, stop=True)
                    for x in units:
                        TS("dve", x["wr"], PS[x["bs"]][:, 0:512], float(EPSQ), ALU.add, (PK[x["bs"]],), (x["wrk"],))
                    for x in units:
                        ACT(x["wr"], x["wr"], AF.Ln, (x["wrk"],), (x["wrk"],))
                    for x in units:
                        ACT(x["wr"], x["wr"], AF.Exp, (x["wrk"],), (x["wrk"],), scale=-0.5)
                    for x in units:
                        STT("dve", KTn[:, x["tok"]], PS[x["bk"]][:, 0:512], vcol("gkn"), x["wr"], ALU.mult, ALU.mult,
                            (PK[x["bk"]], x["wrk"]) + CONST, ("KTn",))
                        TT("pool", KTr[0:64, x["tok"]], RT[0:64, x["tok"]], x["wr"][0:64, :], ALU.mult,
                           (f"RTa{x['t']}", x["wrk"]), ("KTr",))
                    for x in units:
                        t = x["t"]
                        for sub in range(4):
                            MM(PS[x["bv"]][:, sub * 128:(sub + 1) * 128],
                               ckvT[:, t * 512 + sub * 128:t * 512 + (sub + 1) * 128],
                               Wukvb[:, h * 256 + 128:h * 256 + 256], ("Wukv", f"ckvT{t}"), (PK[x["bv"]],))
                    for x in units:
                        t = x["t"]
                        CP("act", Vh[:, 4 * t:4 * t + 4, 0:128], view(PS[x["bv"]][:, 0:512], [4, 128]), (PK[x["bv"]],),
                           ("Vh",))
                SB = [0, 1, 7]
                for qT in range(4):
                    qtok = slice(qT * 512, (qT + 1) * 512)
                    ob = 2 + (h * 4 + qT) % 2

                    def s_mm(kb):
                        sb = SB[kb % 3]
                        kcols = slice(kb * 128, (kb + 1) * 128)
                        MM(PS[sb][:, 0:512], KTn[:, kcols], QTn[:, h, qtok], ("KTn", "QTn"), (PK[sb],), start=True,
                           stop=False)
                        MM(PS[sb][:, 0:512], KTr[:, kcols], QTr[:, h, qtok], ("KTr", "QTr"), (PK[sb],), start=False,
                           stop=True)
                        ACT(PT[kb % 3], PS[sb][:, 0:512], AF.Exp, (PK[sb],), (f"PT{kb % 3}",), scale=SC)

                    s_mm(0)
                    s_mm(1)
                    for kb in range(64):
                        if kb + 2 < 64:
                            s_mm(kb + 2)
                        MM(PS[ob][:, 0:512], Vh[:, kb, 0:128], PT[kb % 3], (f"PT{kb % 3}", "Vh"), (PK[ob],),
                           start=(kb == 0), stop=(kb == 63))
                        if kb == 0:
                            CP("dve", racc, PT[kb % 3], (f"PT{kb % 3}",), ("racc",))
                        else:
                            TT("dve", racc, racc, PT[kb % 3], ALU.add, (f"PT{kb % 3}", "racc"), ("racc",))
                    MM(PS[6][:, 0:512], ones_f, racc, ("racc",) + CONST, (PK[6],))
                    RECIP(rinv, PS[6][:, 0:512], (PK[6],), ("rinv",))
                    TT("dve", yT[:, 4 + h, qtok], PS[ob][:, 0:512], rinv, ALU.mult, (PK[ob], "rinv"), ("yT",))
            if dbg:
                DMA(dbg_d["yT"][:, :, :], yT, ("yT",), ("dbg_yT",), "dbg")
            P.barrier()
            if upto == "T":
                raise _Stop()

            A.reset(R2)
            Woutb = A.alloc([8, 1024], BF16)
            W2b = A.alloc([32, 1024], BF16)
            W1t = [A.alloc([8, 512], BF16) for _ in range(2)]
            x1s = [A.alloc([2, 1024]) for _ in range(2)]
            x1bs = [A.alloc([2, 1024], BF16) for _ in range(2)]
            x1gTs = [A.alloc([8, 256], BF16) for _ in range(2)]
            aT = A.alloc([32, 256], BF16)
            xrt = [A.alloc([1024]) for _ in range(2)]
            ost = [A.alloc([1024]) for _ in range(2)]
            rrs = [A.alloc([4]) for _ in range(2)]
            rtmp = [A.alloc([256], BF16) for _ in range(2)]
            junk = A.alloc([1024], BF16)
            print("phase F arena end", A.off)
            DMA(Woutb, w_out.rearrange("(c p) n -> p c n", p=128), (), ("Woutb",), "wf", queue="pool")
            w2_v = w_ff2.rearrange("(c p) n -> p c n", p=128)
            for q4 in range(4):
                DMA(W2b[:, q4 * 8:(q4 + 1) * 8, :], w2_v[:, q4 * 8:(q4 + 1) * 8, :], (), ("W2b",), "wf2", queue="pool")
            o_g, _ = VC["gffn"]

            def f_outproj(ft):
                sl = ft % 2
                for s2 in range(2):
                    c = 2 * ft + s2
                    DMA(xrt[s2], xr[c], (), (f"xrt{s2}",), f"xrt{s2}")
                    for half in range(2):
                        for mc in range(8):
                            MM(PS[half][:, 0:512], yT[:, mc, c * 128:(c + 1) * 128],
                               Woutb[:, mc, half * 512:(half + 1) * 512], ("yT", "Woutb"), (PK[half],), start=(mc == 0),
                               stop=(mc == 7))
                        TT("dve", x1s[sl][:, s2, half * 512:(half + 1) * 512], PS[half][:, 0:512],
                           xrt[s2][:, half * 512:(half + 1) * 512], ALU.add, (PK[half], f"xrt{s2}"), (f"x1_{sl}_{s2}",))
                    ACT(junk, x1s[sl][:, s2, :], AF.Square, (f"x1_{sl}_{s2}",), ("junk", f"rr{sl}"),
                        accum=rrs[sl][:, s2:s2 + 1])
                    TS("dve", rrs[sl][:, s2:s2 + 1], rrs[sl][:, s2:s2 + 1], 1.0 / 1024.0, ALU.mult, (f"rr{sl}",),
                       (f"rr{sl}",), s2=EPS, op1=ALU.add)
                    RECIP(rrs[sl][:, s2:s2 + 1], rrs[sl][:, s2:s2 + 1], (f"rr{sl}",), (f"rr{sl}",))
                    CP("pool", x1bs[sl][:, s2, :], x1s[sl][:, s2, :], (f"x1_{sl}_{s2}",), (f"x1b{sl}",))

            def f_transposes(ft):
                sl = ft % 2
                for s2 in range(2):
                    for dcg in range(2):
                        for k4 in range(4):
                            dc = 4 * dcg + k4
                            MM(PS[2][:, k4 * 128:(k4 + 1) * 128], x1bs[sl][:, s2, dc * 128:(dc + 1) * 128], ident_b,
                               (f"x1b{sl}",) + CONST, (PK[2],))
                        TT("dve", x1gTs[sl][:, 4 * dcg:4 * dcg + 4, s2 * 128:(s2 + 1) * 128],
                           view(PS[2][:, 0:512], [4, 128]),
                           vcs[:, o_g + 4 * dcg:o_g + 4 * dcg + 4].unsqueeze(2).to_broadcast([128, 4, 128]), ALU.mult,
                           (PK[2],) + CONST, (f"x1gT{sl}",))

            def f_phase1(ft):
                sl_ = ft % 2
                for gI in range(8):
                    sl = gI % 2
                    DMA(W1t[sl], w1s[gI], ("w1s",), (f"W1t{sl}",), f"W1t{sl}")
                    for f4 in range(4):
                        f = 4 * gI + f4
                        bk = 2 + f % 2
                        for dc in range(8):
                            MM(PS[bk][:, 0:256], W1t[sl][:, dc, f4 * 128:(f4 + 1) * 128], x1gTs[sl_][:, dc, :],
                               (f"W1t{sl}", f"x1gT{sl_}"), (PK[bk],), start=(dc == 0), stop=(dc == 7))
                        ACT(rtmp[f % 2], PS[bk][:, 0:256], AF.Relu, (PK[bk],), (f"rtmp{f % 2}",))
                        TT("pool", aT[:, f, :], rtmp[f % 2], rtmp[f % 2], ALU.mult, (f"rtmp{f % 2}",), ("aT",))

            def f_phase2(ft):
                sl = ft % 2
                for s2 in range(2):
                    c = 2 * ft + s2
                    for half in range(2):
                        bk = 4 + s2 * 2 + half
                        for f in range(32):
                            MM(PS[bk][:, 0:512], aT[:, f, s2 * 128:(s2 + 1) * 128], W2b[:, f, half * 512:(half + 1) * 512],
                               ("aT", "W2b"), (PK[bk],), start=(f == 0), stop=(f == 31))
                        STT("dve", ost[s2][:, half * 512:(half + 1) * 512], PS[bk][:, 0:512], rrs[sl][:, s2:s2 + 1],
                            x1s[sl][:, s2, half * 512:(half + 1) * 512], ALU.mult, ALU.add,
                            (PK[bk], f"rr{sl}", f"x1_{sl}_{s2}"), (f"ost{s2}",))
                    DMA(out_d[c], ost[s2], (f"ost{s2}",), (f"out{c}",), f"ost{s2}")

            f_outproj(0)
            f_transposes(0)
            for ft in range(8):
                f_phase1(ft)
                if ft + 1 < 8:
                    f_outproj(ft + 1)
                f_phase2(ft)
                if ft + 1 < 8:
                    f_transposes(ft + 1)
        except _Stop:
            pass

        with nc.Block() as block:
            P.emit(block, sems, dma_sems)
    return nc


_NC_CACHE = {}


def kernel(**inputs):
    maps = make_in_maps(inputs)
    if "nc" not in _NC_CACHE:
        _NC_CACHE["nc"] = build()
    nc = _NC_CACHE["nc"]
    res = run_bass_kernel_spmd(nc, maps, core_ids=list(range(8)))
    out = np.zeros((2, 8192, 1024), np.float32)
    for c in range(8):
        b, j = c // 4, c % 4
        out[b, 2048 * j:2048 * (j + 1), :] = np.asarray(res.results[c]["out"], np.float32).reshape(2048, 1024)
    return out
```

```python
import numpy as np
import ml_dtypes
import concourse.bass as bass
import concourse.mybir as mybir
from concourse.bass_utils import run_bass_kernel_spmd

F32 = mybir.dt.float32
BF16 = mybir.dt.bfloat16
I32 = mybir.dt.int32
ALU = mybir.AluOpType
AF = mybir.ActivationFunctionType
AX = mybir.AxisListType

EPS = 1e-6
NEG = -30000.0
TWO_PI = float(2.0 * np.pi)


class Prog:
    ENGS = ("pe", "act", "dve", "pool", "sp")

    def __init__(self, nc):
        self.nc = nc
        self.ops = []
        self.last_w = {}
        self.readers = {}
        self.dma_count = {}
        self.last_on_eng = {}
        self.dma_last = {}
        self.barrier_deps = None

    def _add(self, eng, fn, reads, writes, dma_key=None):
        writes = tuple(writes) + tuple(r for r in reads if r.startswith("ps") and r not in writes)
        oid = len(self.ops)
        deps = set()
        for r in reads:
            if r in self.last_w:
                deps.add(self.last_w[r])
        for w in writes:
            if w in self.last_w:
                deps.add(self.last_w[w])
            for rd in self.readers.get(w, ()):
                deps.add(rd)
        if self.barrier_deps is not None and eng not in self.barrier_deps[1]:
            deps |= self.barrier_deps[0]
            self.barrier_deps[1].add(eng)
        deps.discard(oid)
        op = dict(id=oid, eng=eng, fn=fn, deps=deps, dma_key=dma_key, signal=False)
        if dma_key is not None:
            self.dma_count[dma_key] = self.dma_count.get(dma_key, 0) + 16
            op["dma_val"] = self.dma_count[dma_key]
            self.dma_last[dma_key] = oid
        self.ops.append(op)
        for w in writes:
            self.last_w[w] = oid
            self.readers[w] = []
        for r in reads:
            self.readers.setdefault(r, []).append(oid)
        self.last_on_eng[eng if dma_key is None else ("dma", dma_key)] = oid
        return oid

    def op(self, eng, fn, reads=(), writes=()):
        return self._add(eng, fn, tuple(reads), tuple(writes))

    def dma(self, fn, reads=(), writes=(), key=None, queue="sp"):
        return self._add(queue, fn, tuple(reads), tuple(writes), dma_key=key)

    def barrier(self):
        deps = set()
        for k, oid in self.last_on_eng.items():
            deps.add(oid)
        self.barrier_deps = (deps, set())

    def emit(self, block, sems, dma_sems):
        ops = self.ops
        for o in ops:
            for d in o["deps"]:
                ops[d]["signal"] = True
        cnt = {e: 0 for e in self.ENGS}
        for o in ops:
            if o["dma_key"] is None and o["signal"]:
                cnt[o["eng"]] += 1
                o["sig_val"] = cnt[o["eng"]]
        by_eng = {e: [o for o in ops if o["eng"] == e] for e in self.ENGS}

        def run(eng_name, e):
            waited = {}
            for o in by_eng[eng_name]:
                need = {}
                for d in o["deps"]:
                    od = ops[d]
                    if od["dma_key"] is not None:
                        k = ("dma", od["dma_key"])
                        need[k] = max(need.get(k, 0), od["dma_val"])
                    else:
                        if od["eng"] == "pe" and eng_name == "pe":
                            continue
                        k = ("eng", od["eng"])
                        need[k] = max(need.get(k, 0), od["sig_val"])
                for k, v in need.items():
                    if waited.get(k, 0) >= v:
                        continue
                    waited[k] = v
                    if k[0] == "dma":
                        e.wait_ge(dma_sems[k[1]], v)
                    else:
                        e.wait_ge(sems[k[1]], v)
                ins = o["fn"](e)
                if o["dma_key"] is not None:
                    ins.then_inc(dma_sems[o["dma_key"]], 16)
                elif o["signal"]:
                    ins.then_inc(sems[o["eng"]], 1)
            if eng_name == "sp":
                for k, v in self.dma_count.items():
                    e.wait_ge(dma_sems[k], v)

        @block.tensor
        def _(e):
            run("pe", e)

        @block.scalar
        def _(e):
            run("act", e)

        @block.vector
        def _(e):
            run("dve", e)

        @block.gpsimd
        def _(e):
            run("pool", e)

        @block.sync
        def _(e):
            run("sp", e)


def _prod(s):
    r = 1
    for v in s:
        r *= int(v)
    return r


class Arena:
    def __init__(self, ap, words):
        self.ap = ap
        self.words = words
        self.off = 0

    def mark(self):
        return self.off

    def reset(self, off):
        self.off = off

    def alloc(self, shape, dtype=F32):
        n = _prod(shape)
        size = 4 if dtype in (F32, I32) else 2
        w = (n * size + 3) // 4
        w = (w + 7) // 8 * 8
        assert self.off + w <= self.words, ("arena overflow", self.off, w, self.words)
        v = self.ap[:, self.off:self.off + w]
        self.off += w
        if dtype != F32:
            v = v.bitcast(dtype)
        v = v[:, 0:n]
        return view(v, shape)


def view(ap, shape):
    if len(shape) == 1:
        return ap
    names = "abcdefg"[:len(shape)]
    kw = {names[i]: int(shape[i]) for i in range(len(shape) - 1)}
    return ap.rearrange("p (" + " ".join(names) + ") -> p " + " ".join(names), **kw)


def _consts():
    c = {}
    idx = np.arange(128)
    c["ident"] = np.eye(128, dtype=np.float32)
    k = idx[:, None]
    p = idx[None, :]
    c["tri_le"] = (k <= p).astype(np.float32)
    c["tri_lt"] = (k < p).astype(np.float32)
    c["tri_ge"] = (k >= p).astype(np.float32)
    inv = (10000.0 ** (-np.arange(0, 64, 2, dtype=np.float32) / 64.0)).astype(np.float32)
    col = np.zeros((128, 4), np.float32)
    for q in range(128):
        i = q % 64
        col[q, 0] = inv[i % 32]
        col[q, 1] = (np.pi / 2) if q < 64 else 0.0
        col[q, 2] = 1.0 if q < 64 else (-1.0 if i < 32 else 1.0)
    c["ropec"] = col
    fold = np.zeros((128, 64), np.float32)
    fold[np.arange(64), np.arange(64)] = 1.0
    fold[np.arange(64) + 64, np.arange(64)] = 1.0
    c["fold"] = fold
    return c


def _perm64():
    i = np.arange(64)
    return np.where(i < 32, i + 32, i - 32)


VC = {}
_o = 0
for _n, _w in (("gmix", 8), ("gffn", 8), ("cwq", 10), ("cwk", 10), ("cbq", 2), ("cbk", 2), ("gcq", 2),
               ("gckv", 1), ("gqn", 1), ("gkn", 1), ("gqr", 1), ("gkr", 1)):
    VC[_n] = (_o, _w)
    _o += _w
NVC = _o
VR = {}
_o = 0
for _n, _w in (("bg", 16), ("gout", 512), ("cm", 128)):
    VR[_n] = (_o, _w)
    _o += _w
NVR = _o


def make_in_maps(inp):
    f32 = np.float32
    x = np.asarray(inp["x"], f32)
    pos = np.asarray(inp["positions"]).astype(np.int32)
    perm = _perm64()
    vc = np.zeros((128, NVC), f32)

    def put(name, arr):
        o, w = VC[name]
        vc[:, o:o + w] = np.asarray(arr, f32).reshape(128, w)

    put("gmix", np.asarray(inp["g_mix_norm"], f32)[0].reshape(8, 128).T)
    put("gffn", np.asarray(inp["g_ffn_norm"], f32)[0].reshape(8, 128).T)
    cw = np.asarray(inp["conv_w"], f32)[0]
    cb = np.asarray(inp["conv_b"], f32)[0]
    put("cwq", cw[:, 0:256].reshape(5, 2, 128).transpose(2, 1, 0))
    put("cwk", cw[:, 256:512].reshape(5, 2, 128).transpose(2, 1, 0))
    put("cbq", cb[0:256].reshape(2, 128).T)
    put("cbk", cb[256:512].reshape(2, 128).T)
    put("gcq", np.asarray(inp["g_cq"], f32)[0].reshape(2, 128).T)
    put("gckv", np.asarray(inp["g_ckv"], f32)[0].reshape(128, 1))
    gq = np.asarray(inp["g_q"], f32)[0]
    gk = np.asarray(inp["g_k"], f32)[0]
    put("gqn", gq[0:128].reshape(128, 1))
    put("gkn", gk[0:128].reshape(128, 1))
    put("gqr", np.concatenate([gq[128:192], gq[128:192][perm]]).reshape(128, 1))
    put("gkr", np.concatenate([gk[128:192], gk[128:192][perm]]).reshape(128, 1))
    cst = _consts()
    maps = []
    for c in range(8):
        b, j = c // 4, c % 4
        xb = x[b]
        xT = xb.T
        xa = np.ascontiguousarray(xT.reshape(8, 128, 16, 512).transpose(2, 1, 0, 3))
        s, e = 2048 * j, 2048 * (j + 1)
        xo = np.ascontiguousarray(xT[:, s:e].reshape(8, 128, 4, 512).transpose(2, 1, 0, 3))
        halo = np.zeros((1024, 4), f32)
        if j > 0:
            halo[:, 0:2] = xT[:, s - 2:s]
        if j < 3:
            halo[:, 2:4] = xT[:, e:e + 2]
        xh = np.ascontiguousarray(halo.reshape(8, 128, 4).transpose(1, 0, 2))
        xr = np.ascontiguousarray(xb[s:e].reshape(16, 128, 1024))
        vr = np.zeros((128, NVR), f32)
        o, w = VR["bg"]
        vr[:, o:o + w] = np.asarray(inp["b_gates"], f32)[0][None, :]
        o, w = VR["gout"]
        vr[:, o:o + w] = np.asarray(inp["g_mlstm_out"], f32)[0][None, :]
        cm = np.zeros((2, 64), f32)
        cm[0, 0:16 * j] = 1.0
        cm[1, 16 * (j + 1):] = 1.0
        o, w = VR["cm"]
        vr[:, o:o + w] = cm.reshape(1, 128)
        m = {
            "xa": xa, "xo": xo, "xh": xh, "xr": xr,
            "posa": np.ascontiguousarray(pos[b].reshape(1, 8192)),
            "poso": np.ascontiguousarray(pos[b, s:e].reshape(1, 2048)),
            "vc": vc, "vr": vr,
            "w_in": np.ascontiguousarray(np.asarray(inp["w_in"], f32)[0]),
            "w_uq": np.ascontiguousarray(np.asarray(inp["w_uq"], f32)[0]),
            "w_ukv": np.ascontiguousarray(np.asarray(inp["w_ukv"], f32)[0]),
            "w_out": np.ascontiguousarray(np.asarray(inp["w_out"], f32)[0]),
            "w_ff1": np.ascontiguousarray(np.asarray(inp["w_ff1"], f32)[0]),
            "w_ff2": np.ascontiguousarray(np.asarray(inp["w_ff2"], f32)[0]),
            "c_ident": cst["ident"], "c_le": cst["tri_le"], "c_lt": cst["tri_lt"], "c_ge": cst["tri_ge"],
            "c_rope": cst["ropec"], "c_fold": cst["fold"],
        }
        maps.append(m)
    return maps


ARENA_WORDS = 52672
R1 = 2600
R2 = 10900
R5 = 23600


class _Stop(Exception):
    pass


def build(dbg=False, upto=None):
    nc = bass.Bass("TRN2", target_bir_lowering=False)
    P = Prog(nc)

    def din(name, shape, dt=F32):
        return nc.dram_tensor(name, list(shape), dt, kind="ExternalInput").ap()

    xa = din("xa", [16, 128, 8, 512])
    xo = din("xo", [4, 128, 8, 512])
    xh = din("xh", [128, 8, 4])
    xr = din("xr", [16, 128, 1024])
    posa = din("posa", [1, 8192], I32)
    poso = din("poso", [1, 2048], I32)
    vc_d = din("vc", [128, NVC])
    vr_d = din("vr", [128, NVR])
    w_in = din("w_in", [1024, 2000])
    w_uq = din("w_uq", [256, 768])
    w_ukv = din("w_ukv", [128, 1024])
    w_out = din("w_out", [1024, 1024])
    w_ff1 = din("w_ff1", [1024, 4096])
    w_ff2 = din("w_ff2", [4096, 1024])
    c_ident = din("c_ident", [128, 128])
    c_le = din("c_le", [128, 128])
    c_lt = din("c_lt", [128, 128])
    c_ge = din("c_ge", [128, 128])
    c_rope = din("c_rope", [128, 4])
    c_fold = din("c_fold", [128, 64])
    out_d = nc.dram_tensor("out", [16, 128, 1024], F32, kind="ExternalOutput").ap()
    w1s = nc.dram_tensor("w1s", [8, 128, 8, 512], BF16, kind="Internal").ap()
    dbg_d = {}
    if dbg:
        dbg_d["yT"] = nc.dram_tensor("d_yT", [128, 8, 2048], BF16, kind="ExternalOutput").ap()
        dbg_d["misc"] = nc.dram_tensor("d_misc", [128, 8192], F32, kind="ExternalOutput").ap()

    import contextlib
    es = contextlib.ExitStack()
    with es:
        arena_t = es.enter_context(nc.sbuf_tensor("arena", [128, ARENA_WORDS], F32))
        ipos_t = es.enter_context(nc.sbuf_tensor("ipos", [128, 512], I32))
        ps = [es.enter_context(nc.psum_tensor(f"ps{i}", [128, 512], F32)) for i in range(8)]
        sems = {e: es.enter_context(nc.semaphore(f"s_{e}")) for e in ("pe", "act", "dve", "pool")}
        dma_sems = {}
        A = Arena(arena_t[:, :], ARENA_WORDS)

        def MM(out, lhsT, rhs, R, W, start=True, stop=True, skip=False):
            P.op("pe", lambda e: e.matmul(out, lhsT=lhsT, rhs=rhs, start=start, stop=stop,
                                          skip_group_check=skip), R, W)

        def ACT(out, in_, func, R, W, bias=0.0, scale=1.0, accum=None):
            if accum is None:
                P.op("act", lambda e: e.activation(out=out, in_=in_, func=func, bias=bias, scale=scale), R, W)
            else:
                P.op("act", lambda e: e.activation(out=out, in_=in_, func=func, bias=bias, scale=scale,
                                                   accum_out=accum), R, W)

        def _eng(e, name):
            return e

        def TT(eng, out, in0, in1, op, R, W):
            P.op(eng, lambda e: e.tensor_tensor(out=out, in0=in0, in1=in1, op=op), R, W)

        def TS(eng, out, in0, s1, op0, R, W, s2=None, op1=None):
            if op1 is None:
                P.op(eng, lambda e: e.tensor_scalar(out=out, in0=in0, scalar1=s1, scalar2=None, op0=op0), R, W)
            else:
                P.op(eng, lambda e: e.tensor_scalar(out=out, in0=in0, scalar1=s1, scalar2=s2, op0=op0, op1=op1),
                     R, W)

        def STT(eng, out, in0, scalar, in1, op0, op1, R, W):
            P.op(eng, lambda e: e.scalar_tensor_tensor(out=out, in0=in0, scalar=scalar, in1=in1, op0=op0, op1=op1),
                 R, W)

        def CP(eng, out, in_, R, W):
            if eng == "act":
                P.op(eng, lambda e: e.activation(out=out, in_=in_, func=AF.Copy), R, W)
            else:
                P.op(eng, lambda e: e.tensor_copy(out=out, in_=in_), R, W)

        def RED(eng, out, in_, op, R, W):
            P.op(eng, lambda e: e.tensor_reduce(out=out, in_=in_, axis=AX.X, op=op), R, W)

        def RECIP(out, in_, R, W):
            P.op("dve", lambda e: e.reciprocal(out=out, in_=in_), R, W)

        def MSET(eng, ap, val, W):
            P.op(eng, lambda e: e.memset(ap, val), (), W)

        def DMA(out, in_, R, W, key, queue="sp"):
            if key not in dma_sems:
                dma_sems[key] = es.enter_context(nc.semaphore("d_" + key))
            P.dma(lambda e: e.dma_start(out=out, in_=in_), R, W, key=key, queue=queue)

        PS = [p[:, :] for p in ps]
        PK = [f"ps{i}" for i in range(8)]

        try:
            ident_f = A.alloc([128])
            tri_le = A.alloc([128])
            tri_lt = A.alloc([128])
            tri_ge = A.alloc([128])
            ones_f = A.alloc([128])
            ropec = A.alloc([8])
            fold_f = A.alloc([64])
            vcs = A.alloc([NVC])
            vrs = A.alloc([NVR])
            ident_b = A.alloc([128], BF16)
            ones10 = A.alloc([128], BF16)
            ones8 = A.alloc([128], BF16)
            ones7 = A.alloc([128], BF16)
            fold_b = A.alloc([64], BF16)
            tri_le_b = A.alloc([128], BF16)
            tri_ge_b = A.alloc([128], BF16)
            sel_lo = A.alloc([128], BF16)
            sel_hi = A.alloc([128], BF16)
            cmb = A.alloc([128])
            ropef = A.alloc([4])
            assert A.off <= R1, A.off

            def vcol(name, i=0, n=1):
                o, w = VC[name]
                return vcs[:, o + i:o + i + n]

            def vrow(name):
                o, w = VR[name]
                return vrs[:, o:o + w]

            for dst, src, nm in ((ident_f, c_ident, "id"), (tri_le, c_le, "le"), (tri_lt, c_lt, "lt"),
                                 (tri_ge, c_ge, "ge"), (ropec[:, 0:4], c_rope, "rp"), (fold_f, c_fold, "fo"),
                                 (vcs, vc_d, "vc"), (vrs, vr_d, "vr")):
                DMA(dst, src[:, :], (), ("const",), "const")
            MSET("pool", ones_f, 1.0, ("const2",))
            MSET("pool", ones10, 2.0 ** -10, ("const2",))
            MSET("pool", ones8, 2.0 ** -8, ("const2",))
            MSET("pool", ones7, 2.0 ** -7, ("const2",))
            MSET("pool", sel_lo, 0.0, ("const2",))
            MSET("pool", sel_hi, 0.0, ("const2",))
            MSET("pool", sel_lo[0:64, :], 2.0 ** -8, ("const2",))
            MSET("pool", sel_hi[64:128, :], 2.0 ** -8, ("const2",))
            CP("dve", ident_b, ident_f, ("const",), ("const3",))
            CP("dve", fold_b, fold_f, ("const",), ("const3",))
            CP("dve", tri_le_b, tri_le, ("const",), ("const3",))
            CP("dve", tri_ge_b, tri_ge, ("const",), ("const3",))
            TS("dve", cmb, vrow("cm"), -1.0, ALU.add, ("const",), ("const3",), s2=30000.0, op1=ALU.mult)
            TT("dve", ropef[:, 0:1], vcol("gkr"), ropec[:, 2:3], ALU.mult, ("const",), ("const3",))
            TT("dve", ropef[:, 1:2], vcol("gqr"), ropec[:, 2:3], ALU.mult, ("const",), ("const3",))
            CONST = ("const", "const2", "const3")
            if upto == "consts":
                raise _Stop()

            A.reset(R1)
            Wb = A.alloc([8, 2064], BF16)
            A.reset(R2)
            ckvT = A.alloc([8192], BF16)
            RT = A.alloc([8192], BF16)
            stB = A.alloc([4, 129])
            mB = A.alloc([8])
            cqnT = A.alloc([2, 2048], BF16)
            Wuqb = A.alloc([2, 768], BF16)
            Wqr = A.alloc([2, 4, 128], BF16)
            Wukvb = A.alloc([1024], BF16)
            assert A.off <= R5, A.off

            w_in_v = w_in.rearrange("(c p) n -> p c n", p=128)
            for dc in range(8):
                DMA(Wb[:, dc, 0:2000], w_in_v[:, dc, :], (), ("Wb",), "wb", queue="pool")
            DMA(Wuqb, w_uq.rearrange("(c p) n -> p c n", p=128), (), ("Wuq",), "wu", queue="pool")
            DMA(Wukvb, w_ukv[:, :], (), ("Wukv",), "wukv", queue="pool")
            CP("pool", Wb[:, :, 2000:2032], Wb[:, :, 1968:2000], ("Wb",), ("Wb2",))
            CP("pool", Wb[:, :, 2032:2064], Wb[:, :, 1936:1968], ("Wb",), ("Wb2",))
            for h in range(4):
                CP("pool", Wqr[:, :, h, 0:64], Wuqb[:, :, h * 192 + 128:h * 192 + 192], ("Wuq",), ("Wqr",))
                CP("pool", Wqr[:, :, h, 64:96], Wuqb[:, :, h * 192 + 160:h * 192 + 192], ("Wuq",), ("Wqr",))
                CP("pool", Wqr[:, :, h, 96:128], Wuqb[:, :, h * 192 + 128:h * 192 + 160], ("Wuq",), ("Wqr",))
            w1_v = w_ff1.rearrange("(c p) (g n) -> g p c n", p=128, n=512)
            for g in range(8):
                DMA(w1s[g], w1_v[g], (), ("w1s",), "w1s", queue="pool")

            def rstd(out_ap, ps_ap, okey, pkey, eps=EPS):
                TS("dve", out_ap, ps_ap, float(eps), ALU.add, (pkey,), (okey,))
                ACT(out_ap, out_ap, AF.Ln, (okey,), (okey,))
                ACT(out_ap, out_ap, AF.Exp, (okey,), (okey,), scale=-0.5)

            def tile_p1(xt, xkey, N, xb, xbk, xsq):
                o, w = VC["gmix"]
                TT("pool", xb[:, :, 0:N], xt[:, :, 0:N], vcs[:, o:o + 8].unsqueeze(2).to_broadcast([128, 8, N]),
                   ALU.mult, (xkey,) + CONST, (xbk,))
                ACT(xsq[:, :, 0:N], xt[:, :, 0:N], AF.Square, (xkey,), ("xsq",))

            def tile_p2a(N, xsq, rxbc, rk):
                for dc in range(8):
                    MM(PS[0][:, 0:N], ones10, xsq[:, dc, 0:N], ("xsq",) + CONST, (PK[0],), start=(dc == 0), stop=(dc == 7))
                rstd(rxbc[:, 0:N], PS[0][:, 0:N], rk, PK[0])

            def tile_p2b(N, rxbc, rk, rxcol, ck):
                nsub = max(N // 128, 1)
                for sub in range(nsub):
                    MM(PS[0][:, 508 + sub:509 + sub], rxbc[0:1, sub * 128:(sub + 1) * 128], ones_f[0:1, 0:1],
                       (rk,) + CONST, (PK[0],))
                CP("dve", rxcol[:, 0:nsub], PS[0][:, 508:508 + nsub], (PK[0],), (ck,))

            def tile_body(N, xb, xbk, fm, tm, psrot):
                nsub = max(N // 128, 1)
                for (c0, M, handler) in fm:
                    b = psrot[0] % len(psrot[1])
                    bank = psrot[1][b]
                    psrot[0] += 1
                    for dc in range(8):
                        MM(PS[bank][0:M, 0:N], Wb[:, dc, c0:c0 + M], xb[:, dc, 0:N], (xbk, "Wb", "Wb2"), (PK[bank],),
                           start=(dc == 0), stop=(dc == 7))
                    handler(PS[bank], PK[bank])
                for (c0, ncol, handler) in tm:
                    for sub in range(nsub):
                        b = psrot[0] % len(psrot[1])
                        bank = psrot[1][b]
                        psrot[0] += 1
                        for dc in range(8):
                            MM(PS[bank][:, 0:ncol], xb[:, dc, sub * 128:(sub + 1) * 128], Wb[:, dc, c0:c0 + ncol],
                               (xbk, "Wb", "Wb2"), (PK[bank],), start=(dc == 0), stop=(dc == 7))
                        handler(sub, PS[bank], PK[bank])

            def sweep(ntiles, Ns, xt, xbs, xsq_, rxbcs, rxcols, dma_fn, handlers_fn, lag_fn, psrot):
                def p1(t):
                    dma_fn(t)
                    tile_p1(xt, "xt0", Ns[t], xbs[t % 2], f"xb{t % 2}", xsq_)

                p1(0)
                tile_p2a(Ns[0], xsq_, rxbcs[0], "rxbc0")
                tile_p2b(Ns[0], rxbcs[0], "rxbc0", rxcols[0], "rxcol0")
                for t in range(ntiles):
                    if t + 1 < ntiles:
                        p1(t + 1)
                    fm, tm = handlers_fn(t)
                    tile_body(Ns[t], xbs[t % 2], f"xb{t % 2}", fm, tm, psrot)
                    s1 = (t + 1) % 2
                    if t + 1 < ntiles:
                        tile_p2a(Ns[t + 1], xsq_, rxbcs[s1], f"rxbc{s1}")
                    if lag_fn is not None and t >= 1:
                        lag_fn(t - 1)
                    if t + 1 < ntiles:
                        tile_p2b(Ns[t + 1], rxbcs[s1], f"rxbc{s1}", rxcols[s1], f"rxcol{s1}")
                if lag_fn is not None:
                    lag_fn(ntiles - 1)

            A.reset(R5)
            xt_a = [A.alloc([8, 512])] * 2
            xb_a = [A.alloc([8, 512], BF16) for _ in range(2)]
            xsq = A.alloc([8, 512], BF16)
            rxbcs = [A.alloc([512]) for _ in range(2)]
            rxcols = [A.alloc([4]) for _ in range(2)]
            W = [A.alloc([512]) for _ in range(4)]
            G_all = A.alloc([64, 16])
            w_all = A.alloc([512])
            sm = A.alloc([64])
            rxc_all = A.alloc([64])
            markA = A.mark()
            GB = {n: A.alloc([512]) for n in ("li", "fp", "lim", "pfa", "pfb", "t1", "ell")}
            Wkp = A.alloc([512])
            Wsq = A.alloc([512])
            print("pass A1 arena end", A.off)

            posf = W[3]

            def rope_table(tab, tkey, pos_src, N):
                ip = ipos_t[:, 0:N]
                kf = A_tmp["kf"]
                DMA(ip, pos_src.partition_broadcast(128), (), ("ipos",), "posi")
                CP("dve", tab[:, 0:N], ip, ("ipos",), (tkey,))
                TS("dve", tab[:, 0:N], tab[:, 0:N], ropec[:, 0:1], ALU.mult, (tkey,) + CONST, (tkey,), s2=ropec[:, 1:2],
                   op1=ALU.add)
                TS("dve", kf[:, 0:N], tab[:, 0:N], 1.0 / TWO_PI, ALU.mult, (tkey,), ("W1",))
                CP("dve", ip, kf[:, 0:N], ("W1",), ("ipos",))
                CP("dve", kf[:, 0:N], ip, ("ipos",), ("W1",))
                STT("dve", tab[:, 0:N], kf[:, 0:N], -6.28125, tab[:, 0:N], ALU.mult, ALU.add, ("W1", tkey), (tkey,))
                STT("dve", tab[:, 0:N], kf[:, 0:N], -(TWO_PI - 6.28125), tab[:, 0:N], ALU.mult, ALU.add, ("W1", tkey),
                    (tkey,))
                TS("dve", tab[:, 0:N], tab[:, 0:N], 3.1415925, ALU.min, (tkey,), (tkey,), s2=-3.1415925, op1=ALU.max)
                ACT(tab[:, 0:N], tab[:, 0:N], AF.Sin, (tkey,), (tkey,))

            A_tmp = {"pi": W[2].bitcast(I32), "kf": W[1]}
            psrotA = [0, [1, 2, 3]]

            def passA1_tile(t):
                slot = t % 2
                rxbc, rxcol, rk, ck = rxbcs[slot], rxcols[slot], f"rxbc{slot}", f"rxcol{slot}"
                tok = slice(t * 512, (t + 1) * 512)
                rope_table(posf, "posf", posa[0:1, t * 512:(t + 1) * 512], 512)

                def h_ckv(psb, pk):
                    TT("dve", W[0], psb[:, 0:512], rxbc, ALU.mult, (pk, rk), ("W0",))
                    ACT(W[1].bitcast(BF16)[:, 0:512], W[0], AF.Square, ("W0",), ("W1",))
                    MM(PS[4][:, 0:512], ones7, W[1].bitcast(BF16)[:, 0:512], ("W1",) + CONST, (PK[4],))
                    rstd(W[2], PS[4][:, 0:512], "W2", PK[4])
                    STT("dve", ckvT[:, tok], W[0], vcol("gckv"), W[2], ALU.mult, ALU.mult, ("W0", "W2") + CONST,
                        (f"ckvT{t}",))

                def h_kp(psb, pk):
                    STT("dve", Wkp.bitcast(BF16)[:, 0:512], psb[:, 0:512], ropef[:, 0:1], posf, ALU.mult, ALU.mult,
                        (pk, "posf") + CONST, ("Wkp",))
                    MM(PS[5][0:64, 0:512], fold_b, Wkp.bitcast(BF16)[:, 0:512], ("Wkp",) + CONST, (PK[5],))
                    TT("dve", RT[0:64, tok], PS[5][0:64, 0:512], rxbc[0:64, :], ALU.mult, (PK[5], rk), (f"RTa{t}",))

                def h_kpsq(psb, pk):
                    TT("dve", Wsq[64:128, :], psb[64:128, 0:512], rxbc[64:128, :], ALU.mult, (pk, rk), ("Wsq",))
                    ACT(RT[64:128, tok], Wsq[64:128, :], AF.Square, ("Wsq",), (f"RTb{t}",))

                def h_g(sub, psb, pk):
                    TS("dve", G_all[:, 4 * t + sub, :], psb[:, 0:16], rxcol[:, sub:sub + 1], ALU.mult, (pk, ck),
                       ("G_all",))

                CP("pool", rxc_all[:, 4 * t:4 * t + 4], rxcol[:, 0:4], (ck,), ("rxc_all",))
                fm = [(1808, 128, h_ckv), (1936, 128, h_kp), (1872, 128, h_kpsq)]
                tm = [(1536, 16, h_g)]
                return fm, tm

            if upto == "W":
                raise _Stop()
            sweep(16, [512] * 16, xt_a[0], xb_a, xsq, rxbcs, rxcols,
                  lambda t: DMA(xt_a[0], xa[t], (), ("xt0",), "xt0"), passA1_tile, None, psrotA)

            if upto == "A1":
                raise _Stop()
            def d3(ap):
                return view(ap, [2, 64, 4])

            Gv = view(G_all.rearrange("p a b -> p (a b)"), [64, 2, 2, 4])
            bgv = view(vrow("bg"), [2, 2, 4])
            for two, nm in ((0, "li"), (1, "fp")):
                TT("dve", GB[nm].rearrange("p (d c h) -> p c d h", d=2, c=64), Gv[:, :, :, two, :],
                   bgv[:, :, two, :].unsqueeze(1).to_broadcast([128, 64, 2, 4]), ALU.add, ("G_all",) + CONST, ("b_" + nm,))
            ACT(GB["fp"], GB["fp"], AF.Exp, ("b_fp",), ("b_fp",), scale=-1.0)
            ACT(GB["fp"], GB["fp"], AF.Ln, ("b_fp",), ("b_fp",), bias=1.0)
            cmv = view(vrow("cm"), [2, 64]).unsqueeze(3).to_broadcast([128, 2, 64, 4])
            cmbv = view(cmb, [2, 64]).unsqueeze(3).to_broadcast([128, 2, 64, 4])
            TT("dve", d3(GB["fp"]), d3(GB["fp"]), cmv, ALU.mult, ("b_fp",) + CONST, ("b_fp",))
            TT("dve", d3(GB["lim"]), d3(GB["li"]), cmv, ALU.mult, ("b_li",) + CONST, ("b_lim",))
            TT("dve", d3(GB["lim"]), d3(GB["lim"]), cmbv, ALU.add, ("b_lim",) + CONST, ("b_lim",))
            for q4 in range(4):
                cs_ = slice(q4 * 128, (q4 + 1) * 128)
                MM(PS[5][:, cs_], tri_le if q4 < 2 else tri_lt, GB["fp"][:, cs_], ("b_fp",) + CONST, (PK[5],))
                MM(PS[4][:, cs_], ones_f, GB["fp"][:, cs_], ("b_fp",) + CONST, (PK[4],))
            CP("dve", GB["pfa"], PS[4][:, 0:512], (PK[4],), ("b_pfa",))
            CP("act", GB["li"], PS[4][:, 0:512], (PK[4], "b_li", "b_lim"), ("b_tot", "b_li"))
            src, dst = "pfa", "pfb"
            k_ = 1
            while k_ < 64:
                TT("dve", d3(GB[dst])[:, :, k_:64, :], d3(GB[src])[:, :, k_:64, :], d3(GB[src])[:, :, 0:64 - k_, :],
                   ALU.add, ("b_" + src,), ("b_" + dst,))
                CP("pool", d3(GB[dst])[:, :, 0:k_, :], d3(GB[src])[:, :, 0:k_, :], ("b_" + src,), ("b_" + dst,))
                src, dst = dst, src
                k_ *= 2
            incl = src
            carry = sm[:, 0:8]
            CP("dve", view(carry, [2, 4]), d3(GB[incl])[:, :, 63, :], ("b_" + incl,), ("sm",))
            TT("dve", GB["t1"], GB[incl], GB["li"], ALU.subtract, ("b_" + incl, "b_tot"), ("b_t1",))
            TT("dve", GB["t1"], GB["t1"], PS[5][:, 0:512], ALU.add, ("b_t1", PK[5]), ("b_t1",))
            TT("dve", GB["ell"][:, 0:256], GB["lim"][:, 0:256], GB["t1"][:, 0:256], ALU.add, ("b_lim", "b_t1"), ("b_ell",))
            TT("dve", GB["ell"][:, 256:512], GB["lim"][:, 256:512], GB["t1"][:, 256:512], ALU.subtract,
               ("b_lim", "b_t1"), ("b_ell",))
            gm1_ = sm[:, 8:16]
            mx_ = sm[:, 16:17]
            dg_ = sm[:, 24:32]
            mref = sm[:, 32:40]
            RED("dve", view(gm1_, [2, 4]), GB["ell"].rearrange("p (d c h) -> p d h c", d=2, c=64), ALU.max, ("b_ell",),
                ("sm",))
            MM(PS[5][0:8, 0:128], gm1_, ident_f, ("sm",) + CONST, (PK[5],))
            RED("dve", mx_[0:8, :], PS[5][0:8, 0:128], ALU.max, (PK[5], "sm"), ("sm",))
            TS("dve", dg_[0:8, :], ident_f[0:8, 0:8], mx_[0:8, 0:1], ALU.mult, ("sm",) + CONST, ("sm",))
            MM(PS[5][:, 256:264], ones_f[0:8, :], dg_[0:8, :], ("sm",) + CONST, (PK[5],))
            TS("dve", mref[:, 0:4], PS[5][:, 256:260], 0.0, ALU.max, (PK[5], "sm"), ("sm",))
            STT("dve", mref[:, 4:8], carry[:, 4:8], -1.0, PS[5][:, 260:264], ALU.mult, ALU.max, (PK[5], "sm"), ("sm",))
            TT("dve", mB[:, 0:4], mref[:, 0:4], carry[:, 0:4], ALU.subtract, ("sm",), ("mB",))
            CP("dve", mB[:, 4:8], mref[:, 4:8], ("sm",), ("mB",))
            TT("dve", d3(w_all), d3(GB["ell"]), view(mref, [2, 4]).unsqueeze(2).to_broadcast([128, 2, 64, 4]),
               ALU.subtract, ("b_ell", "sm"), ("w_all",))
            ACT(w_all, w_all, AF.Exp, ("w_all",), ("w_all",))
            P.barrier()
            if upto == "A1g":
                raise _Stop()

            A.reset(markA)
            kpre = A.alloc([2, 8196], BF16)
            vext = [A.alloc([4, 4, 129], BF16) for _ in range(2)]
            ktok = A.alloc([4, 256], BF16)
            kc = A.alloc([2, 512], BF16)
            wk = A.alloc([2, 4, 256], BF16)
            dgk = A.alloc([2, 5, 128], BF16)
            print("pass A2 arena end", A.off)
            o_cw, _ = VC["cwk"]
            for m_ in range(2):
                for j_ in range(5):
                    TS("dve", dgk[:, m_, j_, :], ident_f, vcs[:, o_cw + m_ * 5 + j_:o_cw + m_ * 5 + j_ + 1], ALU.mult, CONST,
                       ("dgk",))
            MSET("dve", kpre[:, :, 0:2], 0.0, ("kpre_l",))
            MSET("dve", kpre[:, :, 8194:8196], 0.0, ("kpre_r",))
            for s_ in range(2):
                MSET("pool", vext[s_][:, :, :, 128:129], 1.0, (f"vext{s_}",))
            SQ = [(6, 0), (6, 129), (6, 258), (7, 0)]

            def conv_silu(dst, dkey, src, skeys, cwname, cbname, m, n0, N, scale=None, eng="dve"):
                o, _ = VC[cwname]
                ob, _ = VC[cbname]
                acc = W[0]
                TS(eng, acc[:, 0:N], src[:, m, n0:n0 + N], vcs[:, o + m * 5:o + m * 5 + 1], ALU.mult, tuple(skeys) + CONST,
                   ("W0",))
                for j in range(1, 5):
                    STT(eng, acc[:, 0:N], src[:, m, n0 + j:n0 + j + N], vcs[:, o + m * 5 + j:o + m * 5 + j + 1],
                        acc[:, 0:N], ALU.mult, ALU.add, tuple(skeys) + ("W0",) + CONST, ("W0",))
                ACT(dst, acc[:, 0:N], AF.Silu, ("W0",) + CONST, (dkey,), bias=vcs[:, ob + m:ob + m + 1])
                if scale is not None:
                    TS("pool", dst, dst, scale, ALU.mult, (dkey,), (dkey,))

            def passA2_tile(t):
                slot = t % 2
                rxbc, rxcol, rk, ck = rxbcs[slot], rxcols[slot], f"rxbc{slot}", f"rxcol{slot}"

                def h_km(m):
                    def f(psb, pk):
                        TT("dve", kpre[:, m, 2 + t * 512:2 + (t + 1) * 512], psb[:, 0:512], rxbc, ALU.mult,
                           (pk, rk), (f"kpre{t}",))
                    return f

                def h_v(sub, psb, pk):
                    TS("dve", vext[slot][:, sub, :, 0:128], view(psb[:, 0:512], [4, 128]), rxcol[:, sub:sub + 1], ALU.mult,
                       (pk, ck), (f"vext{slot}",))

                fm = [(256, 128, h_km(0)), (384, 128, h_km(1))]
                tm = [(512, 512, h_v)]
                return fm, tm

            started = set()

            def passA2_lag(t):
                slot = t % 2
                ob_k, _ = VC["cbk"]
                kkeys = (f"kpre{t}", f"kpre{max(t - 1, 0)}", f"kpre{min(t + 1, 15)}", "kpre_l", "kpre_r", "dgk")
                for m in range(2):
                    for j in range(5):
                        MM(PS[5][:, 0:512], dgk[:, m, j, :], kpre[:, m, t * 512 + j:t * 512 + j + 512], kkeys, (PK[5],),
                           start=(j == 0), stop=(j == 4))
                    ACT(kc[:, m, :], PS[5][:, 0:512], AF.Silu, (PK[5],) + CONST, ("kc",), bias=vcs[:, ob_k + m:ob_k + m + 1])
                for m in range(2):
                    for sub in range(4):
                        MM(PS[4][:, sub * 128:(sub + 1) * 128], kc[:, m, sub * 128:(sub + 1) * 128], ident_b,
                           ("kc",) + CONST, (PK[4],))
                    CP("act", ktok[:, :, m * 128:(m + 1) * 128], view(PS[4][:, 0:512], [4, 128]), (PK[4],), ("ktok",))
                for d in range(2):
                    TT("pool" if d else "dve", view(wk[:, d].rearrange("p a b -> p (a b)"), [16, 64]),
                       view(ktok.rearrange("p a b -> p (a b)"), [16, 64]),
                       w_all[:, d * 256 + 16 * t:d * 256 + 16 * t + 16].unsqueeze(2).to_broadcast([128, 16, 64]), ALU.mult,
                       ("ktok", "w_all"), (f"wk{d}",))
                for q in (0, 3, 1, 2):
                    d, hp = q // 2, q % 2
                    bank, col = SQ[q]
                    for half in range(2):
                        h = 2 * hp + half
                        for c in range(4):
                            first = (bank, half) not in started
                            started.add((bank, half))
                            MM(PS[bank][half * 64:half * 64 + 64, col:col + 129], wk[:, d, c, h * 64:(h + 1) * 64],
                               vext[slot][:, c, h, :], (f"wk{d}", f"vext{slot}"), (PK[bank],), start=first,
                               stop=(t == 15 and c == 3), skip=True)

            dgt = A.alloc([512])
            o_gm, _ = VC["gmix"]

            def a2_p1(t):
                DMA(xt_a[0], xa[t], (), ("xt0",), "xt0")
                TT("pool", xb_a[t % 2], xt_a[0], vcs[:, o_gm:o_gm + 8].unsqueeze(2).to_broadcast([128, 8, 512]), ALU.mult,
                   ("xt0",) + CONST, (f"xb{t % 2}",))

            def a2_rx(t):
                sl = t % 2
                for sub in range(4):
                    TS("dve", dgt[:, sub * 128:(sub + 1) * 128], ident_f, rxc_all[:, 4 * t + sub:4 * t + sub + 1], ALU.mult,
                       ("rxc_all",) + CONST, ("dgt",))
                for sub in range(4):
                    MM(PS[0][:, sub * 128:(sub + 1) * 128], ones_f, dgt[:, sub * 128:(sub + 1) * 128], ("dgt",) + CONST,
                       (PK[0],))
                CP("act", rxbcs[sl], PS[0][:, 0:512], (PK[0],), (f"rxbc{sl}",))
                CP("pool", rxcols[sl][:, 0:4], rxc_all[:, 4 * t:4 * t + 4], ("rxc_all",), (f"rxcol{sl}",))

            a2_p1(0)
            a2_rx(0)
            for t in range(16):
                if t + 1 < 16:
                    a2_p1(t + 1)
                fm, tm = passA2_tile(t)
                tile_body(512, xb_a[t % 2], f"xb{t % 2}", fm, tm, psrotA)
                if t + 1 < 16:
                    a2_rx(t + 1)
                if t >= 1:
                    passA2_lag(t - 1)
            passA2_lag(15)
            for q in range(4):
                bank, col = SQ[q]
                CP("dve" if q % 2 else "act", stB[:, q, :], PS[bank][:, col:col + 129], (PK[bank],), ("stB",))
            P.barrier()
            if upto == "A":
                raise _Stop()

            A.reset(R5)
            qkpre = A.alloc([4, 2052], BF16)
            vext_o = A.alloc([16, 4, 129], BF16)
            so = A.alloc([16, 512], BF16)
            G_o = A.alloc([16, 16])
            markB = A.mark()
            xt_b = [A.alloc([8, 512])] * 2
            xb_b = [A.alloc([8, 512], BF16) for _ in range(2)]
            xsq = A.alloc([8, 512], BF16)
            rxbcs = [A.alloc([512]) for _ in range(2)]
            rxcols = [A.alloc([4]) for _ in range(2)]
            W[:] = [A.alloc([512]) for _ in range(4)]
            print("pass B arena end", A.off)
            MSET("pool", vext_o[:, :, :, 128:129], 1.0, ("vext_o",))
            psrotB = [0, [1, 2, 3]]

            def passB_tile(T):
                N = 512 if T < 4 else 4
                slot = T % 2
                rxbc, rxcol, rk, ck = rxbcs[slot], rxcols[slot], f"rxbc{slot}", f"rxcol{slot}"
                if N == 512:
                    lo = 2 + T * 512
                    dsts = [(lo, 0, 512)]
                else:
                    dsts = [(0, 0, 2), (2050, 2, 4)]

                def h_qk(idx):
                    def f(psb, pk):
                        for (d0, s0, s1) in dsts:
                            TT("dve", qkpre[:, idx, d0:d0 + (s1 - s0)], psb[:, s0:s1], rxbc[:, s0:s1], ALU.mult,
                               (pk, rk), ("qkpre",))
                    return f

                def h_cq(m):
                    def f(psb, pk):
                        TT("dve", W[m], psb[:, 0:512], rxbc, ALU.mult, (pk, rk), (f"W{m}",))
                        ACT(W[3].bitcast(BF16)[:, m * 512:(m + 1) * 512], W[m], AF.Square, (f"W{m}",), ("W3",))
                        if m == 1:
                            for mm_ in range(2):
                                MM(PS[4][:, 0:512], ones8, W[3].bitcast(BF16)[:, mm_ * 512:(mm_ + 1) * 512],
                                   ("W3",) + CONST, (PK[4],), start=(mm_ == 0), stop=(mm_ == 1))
                            rstd(W[2], PS[4][:, 0:512], "W2", PK[4])
                            for mm_ in range(2):
                                STT("dve", cqnT[:, mm_, T * 512:(T + 1) * 512], W[mm_], vcol("gcq", mm_), W[2], ALU.mult,
                                    ALU.mult, (f"W{mm_}", "W2") + CONST, ("cqnT",))
                    return f

                def h_v(sub, psb, pk):
                    TS("dve", vext_o[:, 4 * T + sub, :, 0:128], view(psb[:, 0:512], [4, 128]), rxcol[:, sub:sub + 1],
                       ALU.mult, (pk, ck), ("vext_o",))

                def h_o(sub, psb, pk):
                    ACT(so[:, 4 * T + sub, :], psb[:, 0:512], AF.Sigmoid, (pk, ck), ("so",), scale=rxcol[:, sub:sub + 1])

                def h_g(sub, psb, pk):
                    TS("dve", G_o[:, 4 * T + sub, :], psb[:, 0:16], rxcol[:, sub:sub + 1], ALU.mult, (pk, ck),
                       ("G_o",))

                fm = [(0, 128, h_qk(0)), (128, 128, h_qk(1)), (256, 128, h_qk(2)), (384, 128, h_qk(3))]
                tm = []
                if N == 512:
                    fm += [(1552, 128, h_cq(0)), (1680, 128, h_cq(1))]
                    tm = [(512, 512, h_v), (1024, 512, h_o), (1536, 16, h_g)]
                return fm, tm

            def dmaB(T):
                if T < 4:
                    DMA(xt_b[0], xo[T], (), ("xt0",), "xt0")
                else:
                    DMA(xt_b[0][:, :, 0:4], xh[:, :, :], (), ("xt0",), "xt0")

            sweep(5, [512] * 4 + [4], xt_b[0], xb_b, xsq, rxbcs, rxcols, dmaB, passB_tile, None, psrotB)
            P.barrier()
            if upto == "B":
                raise _Stop()

            A.reset(markB)
            qc = A.alloc([2, 2048], BF16)
            kcO = A.alloc([2, 2048], BF16)
            ktok_o = A.alloc([16, 256], BF16)
            hsum = A.alloc([16, 2, 128])
            W[:] = [None, None, None, None]
            dgc = A.alloc([4, 5, 128], BF16)
            tq = A.alloc([8])
            GA = {n: A.alloc([128]) for n in ("li", "fp", "bn", "an", "u", "cmT", "umb", "g", "M17", "M17b", "Mt", "al",
                                              "be", "flo", "wi", "ws", "dec", "t", "mp", "mn")}
            umaxc = A.alloc([1])
            Ab = [A.alloc([128], BF16) for _ in range(4)]
            wsk = [A.alloc([64], BF16) for _ in range(4)]
            stf = A.alloc([2, 129])
            stb = A.alloc([2, 129], BF16)
            cmbt_all = A.alloc([4, 129])
            cmbt = [cmbt_all[:, u_, :] for u_ in range(4)]
            dn_all = A.alloc([3, 4])
            sqw = A.alloc([1024])
            ssq8 = A.alloc([8])
            print("phase M arena end", A.off)
            A.reset(R1)
            yT = A.alloc([8, 2048], BF16)

            for idx in range(4):
                o_c, _ = VC["cwq" if idx < 2 else "cwk"]
                for j_ in range(5):
                    TS("dve", dgc[:, idx, j_, :], ident_f, vcs[:, o_c + (idx % 2) * 5 + j_:o_c + (idx % 2) * 5 + j_ + 1],
                       ALU.mult, CONST, ("dgc",))
            ob_q, _ = VC["cbq"]
            ob_k2, _ = VC["cbk"]
            for piece in range(4):
                n0 = piece * 512
                for idx in range(4):
                    bk = (0, 1, 2, 3)[idx]
                    m = idx % 2
                    for j_ in range(5):
                        MM(PS[bk][:, 0:512], dgc[:, idx, j_, :], qkpre[:, idx, n0 + j_:n0 + j_ + 512], ("qkpre", "dgc"),
                           (PK[bk],), start=(j_ == 0), stop=(j_ == 4))
                    if idx < 2:
                        ACT(qc[:, m, n0:n0 + 512], PS[bk][:, 0:512], AF.Silu, (PK[bk],) + CONST, ("qc",),
                            bias=vcs[:, ob_q + m:ob_q + m + 1])
                        TS("pool", qc[:, m, n0:n0 + 512], qc[:, m, n0:n0 + 512], 0.125, ALU.mult, ("qc",), ("qc",))
                    else:
                        ACT(kcO[:, m, n0:n0 + 512], PS[bk][:, 0:512], AF.Silu, (PK[bk],) + CONST, ("kcO",),
                            bias=vcs[:, ob_k2 + m:ob_k2 + m + 1])
            for cc in range(4):
                for m in range(2):
                    for k4 in range(4):
                        c = cc * 4 + k4
                        MM(PS[4][:, k4 * 128:(k4 + 1) * 128], kcO[:, m, c * 128:(c + 1) * 128], ident_b, ("kcO",) + CONST,
                           (PK[4],))
                    CP("act", ktok_o[:, cc * 4:cc * 4 + 4, m * 128:(m + 1) * 128], view(PS[4][:, 0:512], [4, 128]),
                       (PK[4],), ("ktok_o",))

            def dch16(ap):
                return ap.rearrange("p (d c h) -> p c d h", d=2, c=16)

            def v3(ap):
                return view(ap, [2, 16, 4])

            Gv = view(G_o.rearrange("p a b -> p (a b)"), [16, 2, 2, 4])
            bgv = view(vrow("bg"), [2, 2, 4])
            for two, nm in ((0, "li"), (1, "fp")):
                TT("dve", dch16(GA[nm]), Gv[:, :, :, two, :], bgv[:, :, two, :].unsqueeze(1).to_broadcast([128, 16, 2, 4]),
                   ALU.add, ("G_o",) + CONST, ("g_" + nm,))
            ACT(GA["fp"], GA["fp"], AF.Exp, ("g_fp",), ("g_fp",), scale=-1.0)
            ACT(GA["fp"], GA["fp"], AF.Ln, ("g_fp",), ("g_fp",), bias=1.0)
            MM(PS[5][:, 0:64], tri_le, GA["fp"][:, 0:64], ("g_fp",) + CONST, (PK[5],))
            MM(PS[5][:, 64:128], tri_ge, GA["fp"][:, 64:128], ("g_fp",) + CONST, (PK[5],))
            MM(PS[5][:, 128:256], ones_f, GA["fp"], ("g_fp",) + CONST, (PK[5],))
            CP("dve", GA["bn"], PS[5][:, 0:128], (PK[5],), ("g_bn",))
            CP("dve", GA["an"], PS[5][:, 128:256], (PK[5],), ("g_an",))
            TT("dve", GA["u"], GA["li"], GA["bn"], ALU.add, ("g_li", "g_bn"), ("g_u",))
            MM(PS[5][:, 256:384], GA["u"], ident_f, ("g_u",) + CONST, (PK[5],))
            CP("dve", GA["cmT"], PS[5][:, 256:384], (PK[5],), ("g_cmT",))
            k_ = 1
            src_, dst_ = "cmT", "t"
            while k_ < 128:
                S_, D_ = GA[src_], GA[dst_]
                ks, kd = "g_" + src_, "g_" + dst_
                TT("dve", D_[0:64, k_:128], S_[0:64, k_:128], S_[0:64, 0:128 - k_], ALU.max, (ks,), (kd,))
                CP("pool", D_[0:64, 0:k_], S_[0:64, 0:k_], (ks,), (kd,))
                TT("dve", D_[64:128, 0:128 - k_], S_[64:128, 0:128 - k_], S_[64:128, k_:128], ALU.max, (ks,), (kd,))
                CP("pool", D_[64:128, 128 - k_:128], S_[64:128, 128 - k_:128], (ks,), (kd,))
                src_, dst_ = dst_, src_
                k_ *= 2
            if src_ != "cmT":
                CP("dve", GA["cmT"], GA[src_], ("g_" + src_,), ("g_cmT",))
            CP("dve", umaxc[0:64, :], GA["cmT"][0:64, 127:128], ("g_cmT",), ("g_umc",))
            CP("dve", umaxc[64:128, :], GA["cmT"][64:128, 0:1], ("g_cmT",), ("g_umc",))
            TS("dve", GA["t"], ident_f, umaxc[:, 0:1], ALU.mult, ("g_umc",) + CONST, ("g_t",))
            MM(PS[5][:, 384:512], ones_f, GA["t"], ("g_t",) + CONST, (PK[5],))
            CP("dve", GA["umb"], PS[5][:, 384:512], (PK[5],), ("g_umb",))
            TT("dve", GA["g"], GA["umb"], GA["an"], ALU.subtract, ("g_umb", "g_an"), ("g_g",))
            MM(PS[5][:, 0:128], GA["cmT"], ident_f, ("g_cmT",) + CONST, (PK[5],))
            m17 = GA["M17"]
            m17b = GA["M17b"]
            mf = view(m17[:, 0:68], [17, 4])
            mbw = view(m17b[:, 0:68], [17, 4])
            anv = v3(GA["an"])
            gv_ = v3(GA["g"])
            CP("dve", mf[:, 0, :], mB[:, 0:4], ("mB",), ("g_m17",))
            CP("dve", mbw[:, 16, :], mB[:, 4:8], ("mB",), ("g_m17",))
            for i in range(16):
                TT("dve", tq[:, 0:4], mf[:, i, :], anv[:, 0, i, :], ALU.subtract, ("g_m17", "g_an"), ("tq",))
                TT("dve", mf[:, i + 1, :], tq[:, 0:4], gv_[:, 0, i, :], ALU.max, ("tq", "g_g"), ("g_m17",))
                c = 15 - i
                TT("dve", tq[:, 4:8], mbw[:, c + 1, :], anv[:, 1, c, :], ALU.subtract, ("g_m17", "g_an"), ("tq",))
                TT("dve", mbw[:, c, :], tq[:, 4:8], gv_[:, 1, c, :], ALU.max, ("tq", "g_g"), ("g_m17",))
            mprev = GA["mp"]
            mnext = GA["mn"]
            CP("dve", v3(mprev)[:, 0], mf[:, 0:16, :], ("g_m17",), ("g_mp",))
            CP("dve", v3(mprev)[:, 1], mbw[:, 1:17, :], ("g_m17",), ("g_mp",))
            CP("dve", v3(mnext)[:, 0], mf[:, 1:17, :], ("g_m17",), ("g_mn",))
            CP("dve", v3(mnext)[:, 1], mbw[:, 0:16, :], ("g_m17",), ("g_mn",))
            TT("dve", GA["Mt"], PS[5][:, 0:128], mprev, ALU.max, (PK[5], "g_mp"), ("g_Mt",))

            def expdiff(dst, dkey, a, akey, b, bkey):
                TT("dve", dst, a, b, ALU.subtract, (akey, bkey), (dkey,))
                ACT(dst, dst, AF.Exp, (dkey,), (dkey,))

            expdiff(GA["al"], "g_al", GA["umb"], "g_umb", GA["Mt"], "g_Mt")
            expdiff(GA["be"], "g_be", mprev, "g_mp", GA["Mt"], "g_Mt")
            expdiff(GA["flo"], "g_flo", GA["bn"], "g_bn", GA["Mt"], "g_Mt")
            expdiff(GA["wi"], "g_wi", GA["u"], "g_u", GA["umb"], "g_umb")
            TT("dve", GA["ws"], GA["u"], GA["an"], ALU.subtract, ("g_u", "g_an"), ("g_ws",))
            expdiff(GA["ws"], "g_ws", GA["ws"], "g_ws", mnext, "g_mn")
            TT("dve", GA["dec"], mprev, GA["an"], ALU.subtract, ("g_mp", "g_an"), ("g_dec",))
            expdiff(GA["dec"], "g_dec", GA["dec"], "g_dec", mnext, "g_mn")
            GK = ("g_al", "g_be", "g_flo", "g_wi", "g_ws", "g_dec")

            for g in range(2):
                for d in range(2):
                    CP("dve", stf[:, d, :], stB[:, d * 2 + g, :], ("stB",), (f"stf{d}_0", f"stf{d}_1"))
                    CP("act", stb[:, d, :], stf[:, d, :], (f"stf{d}_0", f"stf{d}_1"), (f"stb{d}_0", f"stb{d}_1"))
                QKB = {(0, 0): (1, 0), (0, 1): (7, 0), (1, 0): (0, 0), (1, 1): (6, 300)}
                for i in range(16):
                    U = []
                    for d in range(2):
                        for half in range(2):
                            c = i if d == 0 else 15 - i
                            h = 2 * g + half
                            U.append(dict(d=d, half=half, c=c, h=h, r=d * 64 + c * 4 + h, cols=slice(c * 128, (c + 1) * 128),
                                          rows=slice(half * 64, half * 64 + 64), u=d * 2 + half,
                                          mask=tri_le_b if d == 0 else tri_ge_b, bI=2 + 2 * d, bJ=3 + 2 * d))
                    for x in U:
                        qb, qc0 = QKB[(x["d"], x["half"])]
                        x["qk"] = PS[qb][:, qc0:qc0 + 128]
                        x["qkk"] = PK[qb]
                        MM(x["qk"], kcO[x["rows"], g, x["cols"]], qc[x["rows"], g, x["cols"]], ("kcO", "qc"), (x["qkk"],))
                    for x in U:
                        u, r = x["u"], x["r"]
                        STT("dve", Ab[u], x["qk"], GA["wi"][:, r:r + 1], x["mask"], ALU.mult, ALU.mult,
                            (x["qkk"], "g_wi") + CONST, (f"Ab{u}",))
                        TS("pool", wsk[u], ktok_o[:, x["c"], x["h"] * 64:(x["h"] + 1) * 64], GA["ws"][:, r:r + 1], ALU.mult,
                           ("ktok_o", "g_ws"), (f"wsk{u}",))
                    for x in U:
                        u, d, half, c, h = x["u"], x["d"], x["half"], x["c"], x["h"]
                        bI, bJ = x["bI"], x["bJ"]
                        x["pI"] = PS[bI][:, half * 129:half * 129 + 129]
                        x["kI"] = PK[bI]
                        x["pJ"] = PS[bI][:, 258:387] if half == 0 else PS[bJ][:, 0:129]
                        x["kJ"] = PK[bI] if half == 0 else PK[bJ]
                        MM(x["pI"], Ab[u], vext_o[:, c, h, :], (f"Ab{u}", "vext_o"), (x["kI"],))
                        MM(x["pJ"], qc[x["rows"], g, x["cols"]], stb[x["rows"], d, :], ("qc", f"stb{d}_{half}"), (x["kJ"],))
                        MM(PS[6][x["rows"], d * 129:d * 129 + 129], wsk[u], vext_o[:, c, h, :], (f"wsk{u}", "vext_o"),
                           (PK[6],))
                    for x in U:
                        u, r = x["u"], x["r"]
                        P.op("act", (lambda o_, i_, sc_: (lambda e: e.activation(out=o_, in_=i_, func=AF.Copy, scale=sc_)))(
                            cmbt[u], x["pJ"], GA["be"][:, r:r + 1]), (x["kJ"], "g_be"), (f"cmbt{u}",))
                    for x in U:
                        u, r = x["u"], x["r"]
                        STT("dve", cmbt[u], x["pI"], GA["al"][:, r:r + 1], cmbt[u], ALU.mult, ALU.add,
                            (x["kI"], "g_al", f"cmbt{u}"), (f"cmbt{u}",))
                    for x in U:
                        u, d, half, rows, r = x["u"], x["d"], x["half"], x["rows"], x["r"]
                        STT("dve", stf[rows, d, :], stf[rows, d, :], GA["dec"][rows, r:r + 1],
                            PS[6][rows, d * 129:d * 129 + 129], ALU.mult, ALU.add, (f"stf{d}_{half}", "g_dec", PK[6]),
                            (f"stf{d}_{half}",))
                        CP("act", stb[rows, d, :], stf[rows, d, :], (f"stf{d}_{half}",), (f"stb{d}_{half}",))
                    ck_all = tuple(f"cmbt{u_}" for u_ in range(4))
                    den_v = cmbt_all[:, :, 128]
                    STT("dve", dn_all[:, 0, :], den_v, -1.0, den_v, ALU.mult, ALU.max, ck_all, ("dn",))
                    for d in range(2):
                        r0 = U[2 * d]["r"]
                        TT("dve", dn_all[:, 1, 2 * d:2 * d + 2], dn_all[:, 0, 2 * d:2 * d + 2], GA["flo"][:, r0:r0 + 2],
                           ALU.max, ("dn", "g_flo"), ("dn",))
                    RECIP(dn_all[:, 2, :], dn_all[:, 1, :], ("dn",), ("dn",))
                    for x in U:
                        u, d, half, c = x["u"], x["d"], x["half"], x["c"]
                        first = (d == 0 and c <= 7) or (d == 1 and c >= 8)
                        if first:
                            TS("dve", hsum[:, c, half, :], cmbt[u][:, 0:128], dn_all[:, 2, u:u + 1], ALU.mult,
                               (f"cmbt{u}", "dn"), (f"hsum{c}_{half}",))
                        else:
                            STT("dve", hsum[:, c, half, :], cmbt[u][:, 0:128], dn_all[:, 2, u:u + 1], hsum[:, c, half, :],
                                ALU.mult, ALU.add, (f"cmbt{u}", "dn", f"hsum{c}_{half}"), (f"hsum{c}_{half}",))
                for cc in range(4):
                    hv = hsum[:, 4 * cc:4 * cc + 4].rearrange("p a b c -> p (a b c)")
                    hkeys = tuple(f"hsum{4 * cc + k4}_{half}" for k4 in range(4) for half in range(2))
                    TT("pool", sqw, hv, hv, ALU.mult, hkeys, ("sqw",))
                    RED("dve", ssq8, view(sqw, [8, 128]), ALU.add, ("sqw",), ("ssq8",))
                    TS("dve", ssq8, ssq8, 1.0 / 128.0, ALU.mult, ("ssq8",), ("ssq8",), s2=EPS, op1=ALU.add)
                    ACT(ssq8, ssq8, AF.Sqrt, ("ssq8",), ("ssq8",))
                    RECIP(ssq8, ssq8, ("ssq8",), ("ssq8",))
                    TT("dve", view(sqw, [8, 128]), view(hv, [8, 128]), ssq8.unsqueeze(2).to_broadcast([128, 8, 128]),
                       ALU.mult, hkeys + ("ssq8", "sqw"), ("sqw",))
                    gov = view(vrow("gout")[:, 2 * g * 128:(2 * g + 2) * 128], [2, 128]).unsqueeze(1).to_broadcast(
                        [128, 4, 2, 128])
                    TT("pool", view(sqw, [4, 2, 128]), view(sqw, [4, 2, 128]), gov, ALU.mult, ("sqw",) + CONST, ("sqw",))
                    sov = so[:, 4 * cc:4 * cc + 4, 2 * g * 128:(2 * g + 2) * 128]
                    TT("dve", sov, view(sqw, [4, 256]), sov, ALU.mult, ("sqw", "so"), ("so",))
                    for half in range(2):
                        h = 2 * g + half
                        for k4 in range(4):
                            c = 4 * cc + k4
                            MM(PS[7][:, k4 * 128:(k4 + 1) * 128], so[:, c, h * 128:(h + 1) * 128], ident_b, ("so",) + CONST,
                               (PK[7],))
                        CP("act", yT[:, h, cc * 512:(cc + 1) * 512], PS[7][:, 0:512], (PK[7],), ("yT",))
            P.barrier()
            if upto == "M":
                raise _Stop()

            A.reset(R5)
            QTn = A.alloc([4, 2048], BF16)
            QTr = A.alloc([4, 2048], BF16)
            KTn = A.alloc([8192], BF16)
            KTr = A.alloc([8192], BF16)
            Vh = A.alloc([64, 129], BF16)
            PT = [A.alloc([512], BF16) for _ in range(3)]
            cs_o = A.alloc([2048])
            racc = A.alloc([512])
            rinv = A.alloc([512])
            W[:] = [A.alloc([512]) for _ in range(4)]
            A_tmp["pi"] = W[2].bitcast(I32)
            A_tmp["kf"] = W[1]
            print("phase T arena end", A.off)
            MSET("pool", Vh[:, :, 128:129], 1.0, ("Vh",))
            MSET("pool", QTr[64:128, :, :], 0.0, ("QTr",))
            MSET("pool", KTr[64:128, :], 0.0, ("KTr",))
            SC = float(0.75 * 192.0 ** -0.5)
            EPSQ = 0.75 * EPS

            for T in range(4):
                tok = slice(T * 512, (T + 1) * 512)
                rope_table(cs_o[:, tok], "cs_o", poso[0:1, T * 512:(T + 1) * 512], 512)
                for hp_ in range(2):
                    units = []
                    for e in range(2):
                        units.append(dict(h=2 * hp_ + e, wq=W[(0, 1)[e]].bitcast(BF16), wqk=("W0", "W1")[e], wr=W[(3, 2)[e]],
                                          wrk=("W3", "W2")[e], bn=(1, 5)[e], br=(2, 6)[e], bs=(3, 7)[e], bf=(4, 0)[e]))
                    for x in units:
                        h = x["h"]
                        for m in range(2):
                            MM(PS[x["bn"]][:, 0:512], Wuqb[:, m, h * 192:h * 192 + 128], cqnT[:, m, tok], ("Wuq", "cqnT"),
                               (PK[x["bn"]],), start=(m == 0), stop=(m == 1))
                        for m in range(2):
                            MM(PS[x["br"]][:, 0:512], Wqr[:, m, h, :], cqnT[:, m, tok], ("Wqr", "cqnT"), (PK[x["br"]],),
                               start=(m == 0), stop=(m == 1))
                    for x in units:
                        ACT(x["wq"][:, 0:512], PS[x["bn"]][:, 0:512], AF.Square, (PK[x["bn"]],), (x["wqk"],))
                        ACT(x["wq"][:, 512:1024], PS[x["br"]][:, 0:512], AF.Square, (PK[x["br"]],), (x["wqk"],))
                    for x in units:
                        MM(PS[x["bs"]][:, 0:512], ones8, x["wq"][:, 0:512], (x["wqk"],) + CONST, (PK[x["bs"]],), start=True,
                           stop=False)
                        MM(PS[x["bs"]][:, 0:512], sel_lo, x["wq"][:, 512:1024], (x["wqk"],) + CONST, (PK[x["bs"]],),
                           start=False, stop=True)
                    for x in units:
                        TS("dve", x["wr"], PS[x["bs"]][:, 0:512], float(EPSQ), ALU.add, (PK[x["bs"]],), (x["wrk"],))
                    for x in units:
                        ACT(x["wr"], x["wr"], AF.Ln, (x["wrk"],), (x["wrk"],))
                    for x in units:
                        ACT(x["wr"], x["wr"], AF.Exp, (x["wrk"],), (x["wrk"],), scale=-0.5)
                    for x in units:
                        STT("dve", QTn[:, x["h"], tok], PS[x["bn"]][:, 0:512], vcol("gqn"), x["wr"], ALU.mult, ALU.mult,
                            (PK[x["bn"]], x["wrk"]) + CONST, ("QTn",))
                        STT("dve", x["wq"][:, 0:512], PS[x["br"]][:, 0:512], ropef[:, 1:2], cs_o[:, tok], ALU.mult, ALU.mult,
                            (PK[x["br"]], "cs_o", x["wqk"]) + CONST, (x["wqk"],))
                    for x in units:
                        MM(PS[x["bf"]][0:64, 0:512], fold_b, x["wq"][:, 0:512], (x["wqk"],) + CONST, (PK[x["bf"]],))
                    for x in units:
                        TT("dve", QTr[0:64, x["h"], tok], PS[x["bf"]][0:64, 0:512], x["wr"][0:64, :], ALU.mult,
                           (PK[x["bf"]], x["wrk"]), ("QTr",))

            A_f = Arena(arena_t[:, :], ARENA_WORDS)
            A_f.reset(R2)
            Woutb = A_f.alloc([8, 1024], BF16)
            W2b = A_f.alloc([32, 1024], BF16)
            assert R2 + 4096 + 16 * 512 <= R5
            w2_v = w_ff2.rearrange("(c p) n -> p c n", p=128)
            for h in range(4):
                for tp_ in range(8):
                    units = []
                    for e in range(2):
                        t = 2 * tp_ + e
                        units.append(dict(t=t, tok=slice(t * 512, (t + 1) * 512), bk=(1, 4)[e], bs=(3, 5)[e], bv=(2, 6)[e],
                                          wq=W[(0, 1)[e]], wqk=("W0", "W1")[e], wr=W[(3, 2)[e]], wrk=("W3", "W2")[e]))
                    for x in units:
                        MM(PS[x["bk"]][:, 0:512], Wukvb[:, h * 256:h * 256 + 128], ckvT[:, x["tok"]],
                           ("Wukv", f"ckvT{x['t']}"), (PK[x["bk"]],))
                    for x in units:
                        ACT(x["wq"].bitcast(BF16)[:, 0:512], PS[x["bk"]][:, 0:512], AF.Square, (PK[x["bk"]],), (x["wqk"],))
                    for x in units:
                        MM(PS[x["bs"]][:, 0:512], ones8, x["wq"].bitcast(BF16)[:, 0:512], (x["wqk"],) + CONST,
                           (PK[x["bs"]],), start=True, stop=False)
                        MM(PS[x["bs"]][:, 0:512], sel_hi, RT[:, x["tok"]], (f"RTa{x['t']}", f"RTb{x['t']}") + CONST,
                           (PK[x["bs"]],), start=False, stop=True)
                    for x in units:
                        TS("dve", x["wr"], PS[x["bs"]][:, 0:512], float(EPSQ), ALU.add, (PK[x["bs"]],), (x["wrk"],))
                    for x in units:
                        ACT(x["wr"], x["wr"], AF.Ln, (x["wrk"],), (x["wrk"],))
                    for x in units:
                        ACT(x["wr"], x["wr"], AF.Exp, (x["wrk"],), (x["wrk"],), scale=-0.5)
                    for x in units:
                        STT("dve", KTn[:, x["tok"]], PS[x["bk"]][:, 0:512], vcol("gkn"), x["wr"], ALU.mult, ALU.mult,
                            (PK[x["bk"]], x["wrk"]) + CONST, ("KTn",))
                        TT("pool", KTr[0:64, x["tok"]], RT[0:64, x["tok"]], x["wr"][0:64, :], ALU.mult,
                           (f"RTa{x['t']}", x["wrk"]), ("KTr",))
                    for x in units:
                        t = x["t"]
                        for sub in range(4):
                            MM(PS[x["bv"]][:, sub * 128:(sub + 1) * 128],
                               ckvT[:, t * 512 + sub * 128:t * 512 + (sub + 1) * 128],
                               Wukvb[:, h * 256 + 128:h * 256 + 256], ("Wukv", f"ckvT{t}"), (PK[x["bv"]],))
                    for x in units:
                        t = x["t"]
                        CP("act", Vh[:, 4 * t:4 * t + 4, 0:128], view(PS[x["bv"]][:, 0:512], [4, 128]), (PK[x["bv"]],),
                           ("Vh",))
                if h == 3:
                    dead = tuple(f"ckvT{t_}" for t_ in range(16)) + tuple(f"RTa{t_}" for t_ in range(16)) + \
                        tuple(f"RTb{t_}" for t_ in range(16)) + ("stB", "mB", "cqnT", "Wuq", "Wqr", "Wukv")
                    DMA(Woutb, w_out.rearrange("(c p) n -> p c n", p=128), (), ("Woutb",) + dead, "wf", queue="pool")
                    for q4 in range(2):
                        DMA(W2b[:, q4 * 8:(q4 + 1) * 8, :], w2_v[:, q4 * 8:(q4 + 1) * 8, :], (), ("W2b",), "wf2",
                            queue="pool")
                SB = [0, 1, 7]
                for qT in range(4):
                    qtok = slice(qT * 512, (qT + 1) * 512)
                    ob = 2 + (h * 4 + qT) % 2

                    def s_mm(kb):
                        sb = SB[kb % 3]
                        kcols = slice(kb * 128, (kb + 1) * 128)
                        MM(PS[sb][:, 0:512], KTn[:, kcols], QTn[:, h, qtok], ("KTn", "QTn"), (PK[sb],), start=True,
                           stop=False)
                        MM(PS[sb][:, 0:512], KTr[:, kcols], QTr[:, h, qtok], ("KTr", "QTr"), (PK[sb],), start=False,
                           stop=True)
                        ACT(PT[kb % 3], PS[sb][:, 0:512], AF.Exp, (PK[sb],), (f"PT{kb % 3}",), scale=SC)

                    s_mm(0)
                    s_mm(1)
                    for kb in range(64):
                        if kb + 2 < 64:
                            s_mm(kb + 2)
                        MM(PS[ob][:, 0:512], Vh[:, kb, 0:128], PT[kb % 3], (f"PT{kb % 3}", "Vh"), (PK[ob],),
                           start=(kb == 0), stop=(kb == 63))
                        if kb == 0:
                            CP("dve", racc, PT[kb % 3], (f"PT{kb % 3}",), ("racc",))
                        else:
                            TT("dve", racc, racc, PT[kb % 3], ALU.add, (f"PT{kb % 3}", "racc"), ("racc",))
                    MM(PS[6][:, 0:512], ones_f, racc, ("racc",) + CONST, (PK[6],))
                    RECIP(rinv, PS[6][:, 0:512], (PK[6],), ("rinv",))
                    TT("dve", yT[:, 4 + h, qtok], PS[ob][:, 0:512], rinv, ALU.mult, (PK[ob], "rinv"), ("yT",))
            if dbg:
                DMA(dbg_d["yT"][:, :, :], yT, ("yT",), ("dbg_yT",), "dbg")
            P.barrier()
            if upto == "T":
                raise _Stop()

            A.reset(R2)
            Woutb = A.alloc([8, 1024], BF16)
            W2b = A.alloc([32, 1024], BF16)
            W1t = [A.alloc([8, 512], BF16) for _ in range(2)]
            x1s = [A.alloc([2, 1024]) for _ in range(2)]
            x1bs = [A.alloc([2, 1024], BF16) for _ in range(2)]
            x1gTs = [A.alloc([8, 256], BF16) for _ in range(2)]
            aT = A.alloc([32, 256], BF16)
            xrt = [A.alloc([1024]) for _ in range(2)]
            ost = [A.alloc([1024]) for _ in range(2)]
            rrs = [A.alloc([4]) for _ in range(2)]
            rtmp = [A.alloc([256], BF16) for _ in range(2)]
            junk = A.alloc([1024], BF16)
            print("phase F arena end", A.off)
            for q4 in range(2, 4):
                DMA(W2b[:, q4 * 8:(q4 + 1) * 8, :], w2_v[:, q4 * 8:(q4 + 1) * 8, :], (), ("W2b",), "wf2", queue="pool")
            o_g, _ = VC["gffn"]

            def f_outproj(ft):
                sl = ft % 2
                for s2 in range(2):
                    c = 2 * ft + s2
                    DMA(xrt[s2], xr[c], (), (f"xrt{s2}",), f"xrt{s2}")
                    for half in range(2):
                        for mc in range(8):
                            MM(PS[half][:, 0:512], yT[:, mc, c * 128:(c + 1) * 128],
                               Woutb[:, mc, half * 512:(half + 1) * 512], ("yT", "Woutb"), (PK[half],), start=(mc == 0),
                               stop=(mc == 7))
                        TT("dve", x1s[sl][:, s2, half * 512:(half + 1) * 512], PS[half][:, 0:512],
                           xrt[s2][:, half * 512:(half + 1) * 512], ALU.add, (PK[half], f"xrt{s2}"), (f"x1_{sl}_{s2}",))
                    ACT(junk, x1s[sl][:, s2, :], AF.Square, (f"x1_{sl}_{s2}",), ("junk", f"rr{sl}"),
                        accum=rrs[sl][:, s2:s2 + 1])
                    TS("dve", rrs[sl][:, s2:s2 + 1], rrs[sl][:, s2:s2 + 1], 1.0 / 1024.0, ALU.mult, (f"rr{sl}",),
                       (f"rr{sl}",), s2=EPS, op1=ALU.add)
                    RECIP(rrs[sl][:, s2:s2 + 1], rrs[sl][:, s2:s2 + 1], (f"rr{sl}",), (f"rr{sl}",))
                    CP("pool", x1bs[sl][:, s2, :], x1s[sl][:, s2, :], (f"x1_{sl}_{s2}",), (f"x1b{sl}",))

            def f_transposes(ft):
                sl = ft % 2
                for s2 in range(2):
                    for dcg in range(2):
                        for k4 in range(4):
                            dc = 4 * dcg + k4
                            MM(PS[2][:, k4 * 128:(k4 + 1) * 128], x1bs[sl][:, s2, dc * 128:(dc + 1) * 128], ident_b,
                               (f"x1b{sl}",) + CONST, (PK[2],))
                        TT("dve", x1gTs[sl][:, 4 * dcg:4 * dcg + 4, s2 * 128:(s2 + 1) * 128],
                           view(PS[2][:, 0:512], [4, 128]),
                           vcs[:, o_g + 4 * dcg:o_g + 4 * dcg + 4].unsqueeze(2).to_broadcast([128, 4, 128]), ALU.mult,
                           (PK[2],) + CONST, (f"x1gT{sl}",))

            def f_phase1(ft):
                sl_ = ft % 2
                for gI in range(8):
                    sl = gI % 2
                    DMA(W1t[sl], w1s[gI], ("w1s",), (f"W1t{sl}",), f"W1t{sl}")
                    for f4 in range(4):
                        f = 4 * gI + f4
                        bk = 2 + f % 2
                        for dc in range(8):
                            MM(PS[bk][:, 0:256], W1t[sl][:, dc, f4 * 128:(f4 + 1) * 128], x1gTs[sl_][:, dc, :],
                               (f"W1t{sl}", f"x1gT{sl_}"), (PK[bk],), start=(dc == 0), stop=(dc == 7))
                        ACT(rtmp[f % 2], PS[bk][:, 0:256], AF.Relu, (PK[bk],), (f"rtmp{f % 2}",))
                        TT("pool", aT[:, f, :], rtmp[f % 2], rtmp[f % 2], ALU.mult, (f"rtmp{f % 2}",), ("aT",))

            def f_phase2(ft):
                sl = ft % 2
                for s2 in range(2):
                    c = 2 * ft + s2
                    for half in range(2):
                        bk = 4 + s2 * 2 + half
                        for f in range(32):
                            MM(PS[bk][:, 0:512], aT[:, f, s2 * 128:(s2 + 1) * 128], W2b[:, f, half * 512:(half + 1) * 512],
                               ("aT", "W2b"), (PK[bk],), start=(f == 0), stop=(f == 31))
                        STT("dve", ost[s2][:, half * 512:(half + 1) * 512], PS[bk][:, 0:512], rrs[sl][:, s2:s2 + 1],
                            x1s[sl][:, s2, half * 512:(half + 1) * 512], ALU.mult, ALU.add,
                            (PK[bk], f"rr{sl}", f"x1_{sl}_{s2}"), (f"ost{s2}",))
                    DMA(out_d[c], ost[s2], (f"ost{s2}",), (f"out{c}",), f"ost{s2}")

            f_outproj(0)
            f_transposes(0)
            for ft in range(8):
                f_phase1(ft)
                if ft + 1 < 8:
                    f_outproj(ft + 1)
                f_phase2(ft)
                if ft + 1 < 8:
                    f_transposes(ft + 1)
        except _Stop:
            pass

        with nc.Block() as block:
            P.emit(block, sems, dma_sems)
    return nc


_NC_CACHE = {}


def kernel(**inputs):
    maps = make_in_maps(inputs)
    if "nc" not in _NC_CACHE:
        _NC_CACHE["nc"] = build()
    nc = _NC_CACHE["nc"]
    res = run_bass_kernel_spmd(nc, maps, core_ids=list(range(8)))
    out = np.zeros((2, 8192, 1024), np.float32)
    for c in range(8):
        b, j = c // 4, c % 4
        out[b, 2048 * j:2048 * (j + 1), :] = np.asarray(res.results[c]["out"], np.float32).reshape(2048, 1024)
    return out
```
